# Optimizing a Trainium2 kernel written in Bass

```python
import math
import jax
import jax.numpy as jnp
from jax import lax
import numpy as np

D_MODEL = 1024
BATCH = 4
SEQ = 4096
DEPTH = 1
DEC_BATCH = 128
DEC_SEQ = 4
PAST_LEN = 2048
PAGE_SIZE = 128

ATT_WIDTH = D_MODEL // 2
LRU_WIDTH = D_MODEL - ATT_WIDTH
ATT_HEADS = 4
V_HEAD_DIM = ATT_WIDTH // ATT_HEADS
QK_SUB_DIM = V_HEAD_DIM // 2
ROPE_DIM = QK_SUB_DIM // 4
ROPE_THETA = 500000.0
LRU_BLOCKS = 8
LRU_BLOCK_DIM = LRU_WIDTH // LRU_BLOCKS
LRU_C = 8.0
LRU_CONV = 4
D_FF = ((8 * D_MODEL // 3 + 127) // 128) * 128
FFN_CONV = 3
Q_BLOCK = 128
IN_WIDTH = 3 * ATT_WIDTH + 2 * LRU_WIDTH
EPS = 1e-6
F32 = jnp.float32

kernel_name = 'hymba_diffattn_rglru_convffn_step'


def lambda_init(layer):
    return 0.8 - 0.6 * math.exp(-0.3 * layer)


def rmsnorm(x, g):
    xf = x.astype(F32)
    y = xf * lax.rsqrt(jnp.mean(xf * xf, axis=-1, keepdims=True) + EPS)
    return (y * g.astype(F32)).astype(x.dtype)


def partial_rope(t, pos):
    half = ROPE_DIM // 2
    freqs = ROPE_THETA ** (-jnp.arange(half, dtype=F32) * 2.0 / ROPE_DIM)
    ang = pos.astype(F32)[:, None] * freqs[None, :]
    cos = jnp.cos(ang)[None, :, None, None, :]
    sin = jnp.sin(ang)[None, :, None, None, :]
    tf = t.astype(F32)
    r1 = tf[..., :half]
    r2 = tf[..., half:ROPE_DIM]
    out = jnp.concatenate([r1 * cos - r2 * sin, r1 * sin + r2 * cos, tf[..., ROPE_DIM:]], axis=-1)
    return out.astype(t.dtype)


def causal_dwconv(xpad, w, b):
    k_w = w.shape[0]
    t_len = xpad.shape[1] - k_w + 1
    out = xpad[:, 0:t_len] * w[0]
    for k in range(1, k_w):
        out = out + xpad[:, k:k + t_len] * w[k]
    return out + b


def diff_attn_block(q, k, v, mask, lam):
    s = jnp.einsum('bqhcd,bkhcd->bhcqk', q.astype(F32), k.astype(F32)) * (QK_SUB_DIM ** -0.5)
    s = jnp.where(mask, s, jnp.finfo(F32).min)
    p = jax.nn.softmax(s, axis=-1)
    p_diff = p[:, :, 0] - lam * p[:, :, 1]
    return jnp.einsum('bhqk,bkhd->bqhd', p_diff, v.astype(F32))


def diff_attn_prompt(q, k, v, lam):
    bsz, s_len = q.shape[0], q.shape[1]
    n_blk = s_len // Q_BLOCK
    kf = k.astype(F32)
    vf = v.astype(F32)
    qb = q.astype(F32).reshape(bsz, n_blk, Q_BLOCK, ATT_HEADS, 2, QK_SUB_DIM).swapaxes(0, 1)
    k_pos = jnp.arange(s_len)

    def one_block(args):
        q_i, i = args
        q_pos = i * Q_BLOCK + jnp.arange(Q_BLOCK)
        return diff_attn_block(q_i, kf, vf, k_pos[None, :] <= q_pos[:, None], lam)

    ob = lax.map(one_block, (qb, jnp.arange(n_blk)))
    return ob.swapaxes(0, 1).reshape(bsz, s_len, ATT_HEADS, V_HEAD_DIM)


def diff_attn_sample(q, k_new, v_new, k_past, v_past, lam):
    past = k_past.shape[1]
    t_len = q.shape[1]
    k_all = jnp.concatenate([k_past, k_new.astype(k_past.dtype)], axis=1)
    v_all = jnp.concatenate([v_past, v_new.astype(v_past.dtype)], axis=1)
    q_pos = past + jnp.arange(t_len)
    k_pos = jnp.arange(past + t_len)
    return diff_attn_block(q, k_all, v_all, k_pos[None, :] <= q_pos[:, None], lam)


def lru_combine(left, right):
    a1, b1 = left
    a2, b2 = right
    return a1 * a2, a2 * b1 + b2


def rg_lru(xc, h0, w_r, b_r, w_i, b_i, lru_lambda):
    bsz, t_len = xc.shape[0], xc.shape[1]
    xf = xc.astype(F32)
    xb = xf.reshape(bsz, t_len, LRU_BLOCKS, LRU_BLOCK_DIM)
    r = jax.nn.sigmoid(jnp.einsum('btni,nij->btnj', xb, w_r.astype(F32)) + b_r.astype(F32)).reshape(bsz, t_len, LRU_WIDTH)
    ig = jax.nn.sigmoid(jnp.einsum('btni,nij->btnj', xb, w_i.astype(F32)) + b_i.astype(F32)).reshape(bsz, t_len, LRU_WIDTH)
    log_a = -LRU_C * r * jax.nn.softplus(-lru_lambda.astype(F32))
    a = jnp.exp(log_a)
    gated_x = jnp.sqrt(-jnp.expm1(2.0 * log_a)) * (ig * xf)
    gated_x = gated_x.at[:, 0].add(a[:, 0] * h0.astype(F32))
    _, hs = lax.associative_scan(lru_combine, (a, gated_x), axis=1)
    return hs, hs[:, -1]


def layer_forward(x, c, pos, attn_past, lru_buf, lru_h0, ffn_buf, p, lam_init):
    dt = x.dtype
    bsz, t_len = x.shape[0], x.shape[1]
    mod = jax.nn.silu(c.astype(F32)) @ p['w_ada'].astype(F32) + p['b_ada'].astype(F32)
    mod = mod.astype(dt)[:, None, :]
    sh1, sc1, gt1, sh2, sc2, gt2 = jnp.split(mod, 6, axis=-1)

    u = rmsnorm(x, p['g_norm1']) * (1 + sc1) + sh1
    proj = u @ p['w_in']
    q, k, v, lx, lg = jnp.split(proj, [ATT_WIDTH, 2 * ATT_WIDTH, 3 * ATT_WIDTH, 3 * ATT_WIDTH + LRU_WIDTH], axis=-1)
    q = partial_rope(rmsnorm(q.reshape(bsz, t_len, ATT_HEADS, 2, QK_SUB_DIM), p['g_q']), pos)
    k = partial_rope(rmsnorm(k.reshape(bsz, t_len, ATT_HEADS, 2, QK_SUB_DIM), p['g_k']), pos)
    v = v.reshape(bsz, t_len, ATT_HEADS, V_HEAD_DIM)
    lam = (jnp.exp(jnp.sum(p['lam_q1'].astype(F32) * p['lam_k1'].astype(F32)))
           - jnp.exp(jnp.sum(p['lam_q2'].astype(F32) * p['lam_k2'].astype(F32))) + lam_init)
    if attn_past is None:
        o = diff_attn_prompt(q, k, v, lam)
    else:
        o = diff_attn_sample(q, k, v, attn_past[0], attn_past[1], lam)
    o = rmsnorm(o, p['g_subln']) * (1.0 - lam_init)
    attn_out = o.reshape(bsz, t_len, ATT_WIDTH).astype(dt)

    lx_pad = jnp.concatenate([lru_buf.astype(dt), lx], axis=1)
    xc = causal_dwconv(lx_pad, p['conv_lru_w'], p['conv_lru_b'])
    new_lru_buf = lx_pad[:, -(LRU_CONV - 1):]
    hs, h_last = rg_lru(xc, lru_h0, p['w_rgate'], p['b_rgate'], p['w_igate'], p['b_igate'], p['lru_lambda'])
    lru_out = (hs * jax.nn.gelu(lg.astype(F32), approximate=True)).astype(dt)

    m = jnp.concatenate([attn_out, lru_out], axis=-1) @ p['w_out']
    x = x + gt1 * m

    u2 = rmsnorm(x, p['g_norm2']) * (1 + sc2) + sh2
    up = u2 @ p['w_up']
    up_pad = jnp.concatenate([ffn_buf.astype(dt), up], axis=1)
    hc = causal_dwconv(up_pad, p['conv_ffn_w'], p['conv_ffn_b'])
    new_ffn_buf = up_pad[:, -(FFN_CONV - 1):]
    g, val = jnp.split(hc, 2, axis=-1)
    x = x + gt2 * ((jax.nn.silu(g) * val) @ p['w_down'])
    return x, (k, v, new_lru_buf, h_last.astype(dt), new_ffn_buf)


def setup_inputs(seed: int = 0) -> dict:
    key = jax.random.key(seed)
    ks = jax.random.split(key, 40)
    n_pages = PAST_LEN // PAGE_SIZE
    n_used = DEC_BATCH * n_pages
    n_pool = n_used + max(1, n_used // 4)
    L = DEPTH

    def nrm(k, shape, s):
        return s * jax.random.normal(k, shape, F32)

    page_table = jax.random.permutation(ks[4], n_pool)[:n_used].reshape(DEC_BATCH, n_pages).astype(jnp.int32)
    u = jax.random.uniform(ks[30], (L, LRU_WIDTH), F32, 0.9, 0.999)
    a = u ** (1.0 / LRU_C)
    lru_lambda = jnp.log(a) - jnp.log1p(-a)
    return {
        'x_prompt': nrm(ks[0], (BATCH, SEQ, D_MODEL), 1.0),
        'x_sample': nrm(ks[1], (DEC_BATCH, DEC_SEQ, D_MODEL), 1.0),
        'cache_k': nrm(ks[2], (L, n_pool, PAGE_SIZE, ATT_HEADS, 2, QK_SUB_DIM), 1.0),
        'cache_v': nrm(ks[3], (L, n_pool, PAGE_SIZE, ATT_HEADS, V_HEAD_DIM), 1.0),
        'page_table': page_table,
        'state_lru_conv': nrm(ks[5], (L, DEC_BATCH, LRU_CONV - 1, LRU_WIDTH), 1.0),
        'state_lru_h': nrm(ks[6], (L, DEC_BATCH, LRU_WIDTH), 0.5),
        'state_ffn_conv': nrm(ks[7], (L, DEC_BATCH, FFN_CONV - 1, 2 * D_FF), 1.0),
        'c_prompt': nrm(ks[8], (BATCH, D_MODEL), 1.0),
        'c_sample': nrm(ks[9], (DEC_BATCH, D_MODEL), 1.0),
        'g_norm1': 1.0 + nrm(ks[10], (L, D_MODEL), 0.02),
        'g_norm2': 1.0 + nrm(ks[11], (L, D_MODEL), 0.02),
        'w_ada': nrm(ks[12], (L, D_MODEL, 6 * D_MODEL), 0.5 * D_MODEL ** -0.5),
        'b_ada': nrm(ks[13], (L, 6 * D_MODEL), 0.02),
        'w_in': nrm(ks[14], (L, D_MODEL, IN_WIDTH), D_MODEL ** -0.5),
        'g_q': 1.0 + nrm(ks[15], (L, QK_SUB_DIM), 0.02),
        'g_k': 1.0 + nrm(ks[16], (L, QK_SUB_DIM), 0.02),
        'lam_q1': nrm(ks[17], (L, QK_SUB_DIM), 0.1),
        'lam_k1': nrm(ks[18], (L, QK_SUB_DIM), 0.1),
        'lam_q2': nrm(ks[19], (L, QK_SUB_DIM), 0.1),
        'lam_k2': nrm(ks[20], (L, QK_SUB_DIM), 0.1),
        'g_subln': 1.0 + nrm(ks[21], (L, V_HEAD_DIM), 0.02),
        'w_out': nrm(ks[22], (L, ATT_WIDTH + LRU_WIDTH, D_MODEL), (ATT_WIDTH + LRU_WIDTH) ** -0.5),
        'conv_lru_w': nrm(ks[23], (L, LRU_CONV, LRU_WIDTH), LRU_CONV ** -0.5),
        'conv_lru_b': nrm(ks[24], (L, LRU_WIDTH), 0.02),
        'w_rgate': nrm(ks[25], (L, LRU_BLOCKS, LRU_BLOCK_DIM, LRU_BLOCK_DIM), LRU_BLOCK_DIM ** -0.5),
        'b_rgate': nrm(ks[26], (L, LRU_BLOCKS, LRU_BLOCK_DIM), 0.02),
        'w_igate': nrm(ks[27], (L, LRU_BLOCKS, LRU_BLOCK_DIM, LRU_BLOCK_DIM), LRU_BLOCK_DIM ** -0.5),
        'b_igate': nrm(ks[28], (L, LRU_BLOCKS, LRU_BLOCK_DIM), 0.02),
        'lru_lambda': lru_lambda,
        'w_up': nrm(ks[31], (L, D_MODEL, 2 * D_FF), D_MODEL ** -0.5),
        'conv_ffn_w': nrm(ks[32], (L, FFN_CONV, 2 * D_FF), FFN_CONV ** -0.5),
        'conv_ffn_b': nrm(ks[33], (L, 2 * D_FF), 0.02),
        'w_down': nrm(ks[34], (L, D_FF, D_MODEL), D_FF ** -0.5),
    }


def reference(x_prompt, x_sample, cache_k, cache_v, page_table, state_lru_conv, state_lru_h, state_ffn_conv,
              c_prompt, c_sample, g_norm1, g_norm2, w_ada, b_ada, w_in, g_q, g_k, lam_q1, lam_k1, lam_q2, lam_k2,
              g_subln, w_out, conv_lru_w, conv_lru_b, w_rgate, b_rgate, w_igate, b_igate, lru_lambda,
              w_up, conv_ffn_w, conv_ffn_b, w_down):
    bsz, s_len = x_prompt.shape[0], x_prompt.shape[1]
    dbsz, t_len = x_sample.shape[0], x_sample.shape[1]
    dt = x_prompt.dtype
    past_len = page_table.shape[1] * cache_k.shape[2]
    pos_p = jnp.arange(s_len)
    pos_s = past_len + jnp.arange(t_len)
    yp, ys = x_prompt, x_sample
    st_p, st_s = [], []
    for l in range(DEPTH):
        p = {'g_norm1': g_norm1[l], 'g_norm2': g_norm2[l], 'w_ada': w_ada[l], 'b_ada': b_ada[l],
             'w_in': w_in[l], 'g_q': g_q[l], 'g_k': g_k[l], 'lam_q1': lam_q1[l], 'lam_k1': lam_k1[l],
             'lam_q2': lam_q2[l], 'lam_k2': lam_k2[l], 'g_subln': g_subln[l], 'w_out': w_out[l],
             'conv_lru_w': conv_lru_w[l], 'conv_lru_b': conv_lru_b[l], 'w_rgate': w_rgate[l],
             'b_rgate': b_rgate[l], 'w_igate': w_igate[l], 'b_igate': b_igate[l],
             'lru_lambda': lru_lambda[l], 'w_up': w_up[l], 'conv_ffn_w': conv_ffn_w[l],
             'conv_ffn_b': conv_ffn_b[l], 'w_down': w_down[l]}
        lam0 = lambda_init(l)
        yp, sp = layer_forward(yp, c_prompt, pos_p, None,
                               jnp.zeros((bsz, LRU_CONV - 1, LRU_WIDTH), dt),
                               jnp.zeros((bsz, LRU_WIDTH), F32),
                               jnp.zeros((bsz, FFN_CONV - 1, 2 * D_FF), dt), p, lam0)
        k_past = cache_k[l][page_table].reshape(dbsz, past_len, ATT_HEADS, 2, QK_SUB_DIM)
        v_past = cache_v[l][page_table].reshape(dbsz, past_len, ATT_HEADS, V_HEAD_DIM)
        ys, ss = layer_forward(ys, c_sample, pos_s, (k_past, v_past), state_lru_conv[l], state_lru_h[l],
                               state_ffn_conv[l], p, lam0)
        st_p.append(sp)
        st_s.append(ss)
    k_p = jnp.stack([s[0] for s in st_p])
    v_p = jnp.stack([s[1] for s in st_p])
    lc_p = jnp.stack([s[2] for s in st_p])
    h_p = jnp.stack([s[3] for s in st_p])
    fc_p = jnp.stack([s[4] for s in st_p])
    k_s = jnp.stack([s[0] for s in st_s])
    v_s = jnp.stack([s[1] for s in st_s])
    lc_s = jnp.stack([s[2] for s in st_s])
    h_s = jnp.stack([s[3] for s in st_s])
    fc_s = jnp.stack([s[4] for s in st_s])
    return (yp, ys, k_p, v_p, lc_p, h_p, fc_p, k_s, v_s, lc_s, h_s, fc_s)
```

```python
import math, os
import numpy as np
from contextlib import ExitStack
import concourse.bass as bass
import concourse.mybir as mybir
from concourse.bass_utils import run_bass_kernel_spmd

F32 = mybir.dt.float32
BF16 = mybir.dt.bfloat16
I32 = mybir.dt.int32
ALU = mybir.AluOpType
AF = mybir.ActivationFunctionType
AX = mybir.AxisListType


class Region:
    __slots__ = ("name", "last_w", "readers", "wsem", "rsem", "wcnt", "rcnt", "excl")

    def __init__(self, name, excl=False):
        self.name = name
        self.excl = excl
        self.last_w = None
        self.readers = []
        self.wsem = None
        self.rsem = None
        self.wcnt = 0
        self.rcnt = 0


class Op:
    __slots__ = ("eng", "fn", "deps", "idx", "needed", "dma_tok", "name")


class Builder:
    ENGS = ("pe", "act", "dve", "pool", "sp")

    def __init__(self, nc, self_sync=True):
        self.nc = nc
        self.ops = {e: [] for e in self.ENGS}
        self.self_sync = self_sync
        self.dma_sems = {}
        self.final_toks = []
        self.es = ExitStack()
        self.nreg = 0

    def sb(self, name, shape, dt):
        return self.es.enter_context(self.nc.sbuf_tensor(name, list(shape), dt))

    def ps(self, name, shape, dt):
        return self.es.enter_context(self.nc.psum_tensor(name, list(shape), dt))

    def R(self, name=None, excl=False):
        self.nreg += 1
        return Region(name or f"r{self.nreg}", excl)

    def _deps(self, eng, reads, writes):
        deps = []
        for r in reads:
            if r.last_w is not None:
                deps.append(r.last_w)
            if r.excl:
                deps.extend(t for t in r.readers if t[0] == "c" and t[1] != eng)
        for w in writes:
            if w.last_w is not None:
                deps.append(w.last_w)
            deps.extend(w.readers)
        return deps

    def op(self, eng, fn, reads=(), writes=(), name=None):
        o = Op()
        o.eng = eng
        o.fn = fn
        o.deps = self._deps(eng, reads, writes)
        o.idx = len(self.ops[eng])
        o.needed = False
        o.dma_tok = None
        o.name = name
        self.ops[eng].append(o)
        tok = ("c", eng, o.idx)
        for w in writes:
            w.last_w = tok
            w.readers = []
        for r in reads:
            if all(r is not w for w in writes):
                r.readers.append(tok)
        return tok

    def dma(self, q, fn, reads=(), writes=(), final=False, name=None):
        o = Op()
        o.eng = q
        o.fn = fn
        o.deps = self._deps(q, reads, writes)
        o.idx = len(self.ops[q])
        o.needed = False
        o.name = name
        if writes:
            reg = writes[0]
            key = ("w", id(reg))
            reg.wcnt += 16
            cnt = reg.wcnt
        else:
            reg = reads[0]
            key = ("r", id(reg))
            reg.rcnt += 16
            cnt = reg.rcnt
        if key not in self.dma_sems:
            self.dma_sems[key] = len(self.dma_sems)
        tok = ("d", key, cnt)
        o.dma_tok = tok
        self.ops[q].append(o)
        for w in writes:
            w.last_w = tok
            w.readers = []
        for r in reads:
            r.readers.append(tok)
        if final:
            self.final_toks.append(tok)
        return tok

    def finalize(self):
        nc = self.nc
        for e in self.ENGS:
            for o in self.ops[e]:
                for d in o.deps:
                    if d[0] == "c":
                        if d[1] == e and o.dma_tok is None and not (self.self_sync and e != "pe"):
                            continue
                        self.ops[d[1]][d[2]].needed = True
        semval = {}
        for e in self.ENGS:
            c = 0
            for o in self.ops[e]:
                if o.needed:
                    c += 1
                    semval[(e, o.idx)] = c
        es = self.es
        esems = {e: es.enter_context(nc.semaphore(f"s_{e}")) for e in self.ENGS}
        dsems = {}
        for key, i in self.dma_sems.items():
            dsems[key] = es.enter_context(nc.semaphore(f"d_{i}"))
        handles = {}
        block = es.enter_context(nc.Block())
        ops = self.ops
        self_sync = self.self_sync
        final_toks = self.final_toks

        def emit(e, h):
            waited = {}
            for o in ops[e]:
                need = {}
                for d in o.deps:
                    if d[0] == "c":
                        if d[1] == e and o.dma_tok is None and not (self_sync and e != "pe"):
                            continue
                        s = ("c", d[1])
                        v = semval[(d[1], d[2])]
                    else:
                        s = ("d", d[1])
                        v = d[2]
                    if waited.get(s, 0) >= v:
                        continue
                    if need.get(s, 0) < v:
                        need[s] = v
                for s, v in need.items():
                    sem = esems[s[1]] if s[0] == "c" else dsems[s[1]]
                    h.wait_ge(sem, v)
                    waited[s] = v
                inst = o.fn(h)
                if o.dma_tok is not None:
                    inst.then_inc(dsems[o.dma_tok[1]], 16)
                elif o.needed:
                    inst.then_inc(esems[e], 1)
            if e == "sp":
                fin = {}
                for t in final_toks:
                    if fin.get(t[1], 0) < t[2]:
                        fin[t[1]] = t[2]
                for k, v in fin.items():
                    if waited.get(("d", k), 0) < v:
                        h.wait_ge(dsems[k], v)

        @block.tensor
        def _(h):
            emit("pe", h)

        @block.scalar
        def _(h):
            emit("act", h)

        @block.vector
        def _(h):
            emit("dve", h)

        @block.gpsimd
        def _(h):
            emit("pool", h)

        @block.sync
        def _(h):
            emit("sp", h)

    def close(self):
        self.es.close()


LAM_INIT = 0.8 - 0.6 * math.exp(-0.3 * 0)
EPS = 1e-6
NSLOT = 3
NT32 = 7


def build_program(n_rows=327680, stage=99):
    nc = bass.Bass("TRN2", target_bir_lowering=False)
    B = Builder(nc)

    def din(name, shape, dt=F32):
        return nc.dram_tensor(name, list(shape), dt, kind="ExternalInput").ap()

    def dout(name, shape, dt=F32):
        return nc.dram_tensor(name, list(shape), dt, kind="ExternalOutput").ap()

    xp = din("xp", [2048, 1024]); xo = din("xo", [2048, 1024]); xs = din("xs", [64, 1024])
    cT = din("cT", [128, 8, 17])
    ropep = din("ropep", [128, 16, 16]); ropeo = din("ropeo", [128, 16, 16]); ropes = din("ropes", [64, 16])
    flags = din("flags", [128, 2])
    w_ada = din("w_ada", [1024, 6144]); b_adaT = din("b_adaT", [128, 48]); b_ada = din("b_ada", [1, 6144])
    g1T = din("g1T", [128, 8]); g2T = din("g2T", [128, 8])
    w_in = din("w_in", [1024, 2560]); w_out = din("w_out", [1024, 1024])
    w_up = din("w_up", [1024, 5632]); w_down = din("w_down", [2816, 1024])
    gq_rep = din("gq_rep", [128, 512]); gk_rep = din("gk_rep", [128, 512]); gsub_rep = din("gsub_rep", [128, 512])
    lamv = din("lamv", [1, 256])
    clw = din("clw", [128, 4, 4]); clb = din("clb", [128, 4])
    w_rg = din("w_rg", [8, 64, 64]); w_ig = din("w_ig", [8, 64, 64])
    b_rg = din("b_rg", [128, 4]); b_ig = din("b_ig", [128, 4]); lru_lam = din("lru_lam", [128, 4])
    cfw = din("cfw", [128, 44, 3]); cfb = din("cfb", [128, 44])
    slc = din("slc", [128, 4, 16, 3]); slh = din("slh", [128, 4, 16]); sfc = din("sfc", [128, 44, 16, 2])
    ptab = din("ptab", [1, 256], I32)
    smask = din("smask", [64, 16, 8]); cmb = din("cmb", [8, 124]); gsel = din("gsel", [128, 5])
    cache_k = din("cache_k", [n_rows, 512]); cache_v = din("cache_v", [n_rows, 512])

    y_o = dout("y_o", [2048, 1024]); y_s = dout("y_s", [64, 1024])
    k_o = dout("k_o", [2048, 512]); v_o = dout("v_o", [2048, 512])
    k_s = dout("k_s", [64, 512]); v_s = dout("v_s", [64, 512])
    lc_p = dout("lc_p", [128, 4, 3]); lh_p = dout("lh_p", [128, 4]); fc_p = dout("fc_p", [128, 44, 2])
    lc_s = dout("lc_s", [128, 4, 16, 3]); lh_s = dout("lh_s", [128, 4, 16]); fc_s = dout("fc_s", [128, 44, 16, 2])

    def mm(out, lhsT, rhs, start, stop, reads, writes):
        return B.op("pe", lambda h: h.matmul(out, lhsT=lhsT, rhs=rhs, start=start, stop=stop), reads, writes)

    def tr(out, in_, ident, reads, writes):
        return B.op("pe", lambda h: h.transpose(out=out, in_=in_, identity=ident), reads, writes)

    def act(out, in_, func, reads, writes, **kw):
        return B.op("act", lambda h: h.activation(out=out, in_=in_, func=func, **kw), reads, writes)

    def tt(out, in0, in1, op, reads, writes, eng="dve"):
        return B.op(eng, lambda h: h.tensor_tensor(out=out, in0=in0, in1=in1, op=op), reads, writes)

    def ts(out, in0, s1, s2, op0, op1, reads, writes, eng="dve"):
        if s2 is None:
            return B.op(eng, lambda h: h.tensor_scalar(out=out, in0=in0, scalar1=s1, scalar2=None, op0=op0), reads, writes)
        return B.op(eng, lambda h: h.tensor_scalar(out=out, in0=in0, scalar1=s1, scalar2=s2, op0=op0, op1=op1), reads, writes)

    def stt(out, in0, scalar, in1, op0, op1, reads, writes, eng="dve"):
        return B.op(eng, lambda h: h.scalar_tensor_tensor(out=out, in0=in0, scalar=scalar, in1=in1, op0=op0, op1=op1), reads, writes)

    def cp(out, in_, reads, writes, eng="dve"):
        return B.op(eng, lambda h: h.tensor_copy(out, in_), reads, writes)

    def memset(ap, val, writes, eng="dve"):
        return B.op(eng, lambda h: h.memset(ap, val), (), writes)

    def ld(out, in_, writes, reads=(), q="sp"):
        return B.dma(q, lambda h: h.dma_start(out=out, in_=in_, allow_slow_non_contiguous=True), reads, writes)

    def st(out, in_, reads, q=None):
        q = q or os.environ.get("STQ", "sp")
        return B.dma(q, lambda h: h.dma_start(out=out, in_=in_, allow_slow_non_contiguous=True), reads, (), final=True)

    class Rot:
        def __init__(self, name, shape, dt, n):
            self.bufs = [(B.sb(f"{name}{i}", shape, dt), B.R(f"{name}{i}")) for i in range(n)]
            self.i = 0

        def next(self):
            b = self.bufs[self.i % len(self.bufs)]
            self.i += 1
            return b

    KT = B.sb("KT", [128, 4, 4096], BF16)
    rKT = [B.R(f"KT{i}") for i in range(32)]
    VA = B.sb("VA", [128, 32, 4, 130], BF16)
    rVA = [B.R(f"VA{i}") for i in range(32)]
    wslot = Rot("wslot", [128, 4096], BF16, NSLOT)
    gtp = B.sb("gtp", [128, 2, 1024], F32); rgtp = [B.R(), B.R()]
    ABc = B.sb("ABc", [128, 4, 8, 65], F32); rAB = B.R()
    xblk = B.sb("xblk", [128, 4, 1024], F32); rx = [B.R(f"x{i}") for i in range(4)]
    xnbf = B.sb("xnbf", [128, 1024], BF16); rxn = B.R()
    uT = B.sb("uT", [128, 8, 512], BF16); ruT = B.R()
    chunks = [(B.sb(f"ch{i}", [128, 512], BF16), B.R(f"ch{i}")) for i in range(22)]
    QT = B.sb("QT", [128, 4, 2, 512], BF16); rQT = B.R()
    QsTt = B.sb("QsTt", [128, 4, 64], BF16)
    qkbf = B.sb("qkbf", [128, 512], BF16); rqkbf = B.R()
    t32 = Rot("t32", [128, 512], F32, NT32)
    Et = Rot("Et", [128, 512], BF16, 3)
    attnbf = B.sb("attnbf", [128, 512], BF16); rattnbf = B.R()
    ropet = B.sb("ropet", [128, 33, 16], F32); rrope = B.R()
    small = B.sb("small", [128, 64], F32); rsmall = B.R()
    stat = Rot("stat", [128, 16], F32, 6)
    ident = B.sb("ident", [128, 128], BF16); rident = B.R()
    id32_t, rid32 = t32.next()
    id32 = id32_t[:, 0:128]
    tri = B.sb("tri", [128, 128], BF16); rtri = B.R()
    cols = B.sb("cols", [128, 40], F32); rcols = B.R()
    gq64 = B.sb("gq64", [128, 2, 64], F32); gsub = B.sb("gsub", [128, 128], F32); rgqk = B.R()
    lruc = B.sb("lruc", [128, 4, 12], F32); rlruc = B.R()
    Wg = B.sb("Wg", [128, 8, 128], BF16); rWg = B.R()
    hist = B.sb("hist", [128, 4, 16, 3], F32); rhist = B.R()
    carry = B.sb("carry", [128, 4, 16], F32); rcarry = B.R()
    fcw = B.sb("fcw", [128, 44, 4], F32); rfcw = B.R()
    fhist = B.sb("fhist", [128, 44, 16, 2], F32); rfhist = B.R()
    scx = B.sb("scx", [128, 8, 65], BF16); rscx = B.R()
    btile = B.sb("btile", [128, 44, 2], F32); rbt = B.R()
    def screp(kc):
        return chunks[kc // 4][0][:, (kc % 4) * 128:(kc % 4 + 1) * 128], chunks[kc // 4][1]
    ropetile_zero = None

    psB = B.ps("psB", [128, 5, 512], F32); rB = [B.R(f"pB{i}", excl=True) for i in range(5)]
    psA = [(B.ps(f"psA{i}", [128, 512], F32), B.R(f"pA{i}", excl=True)) for i in range(2)]
    psT = B.ps("psT", [128, 1024], BF16); rT = B.R("pT", excl=True)
    pa_i = [0]

    def PA():
        b = psA[pa_i[0] % 2]
        pa_i[0] += 1
        return b

    pb_i = [0]

    def PB():
        k = pb_i[0] % 5
        pb_i[0] += 1
        return psB[:, k, :], rB[k]

    C_EPS, C_PBIAS, C_PMUL, C_ZERO, C_NLAM, C_ONE = 0, 1, 2, 3, 4, 5

    memset(id32[:], 1.0, [rid32], eng="pool")
    B.op("pool", lambda h: h.affine_select(out=id32[:], in_=id32[:], pattern=[[-1, 128]], compare_op=ALU.is_equal,
                                           fill=0.0, base=0, channel_multiplier=1), [rid32], [rid32])
    cp(ident[:], id32[:], [rid32], [rident])
    memset(id32[:], 1.0, [rid32], eng="pool")
    B.op("pool", lambda h: h.affine_select(out=id32[:], in_=id32[:], pattern=[[1, 128]], compare_op=ALU.is_ge,
                                           fill=0.0, base=0, channel_multiplier=-1), [rid32], [rid32])
    cp(tri[:], id32[:], [rid32], [rtri])
    memset(cols[:], 0.0, [rcols])
    memset(cols[:, C_EPS:C_EPS + 1], EPS, [rcols])
    memset(cols[:, C_ONE:C_ONE + 1], 1.0, [rcols])
    ld(cols[:, C_PBIAS:C_PBIAS + 2], flags, [rcols])
    ld(gq64[:, 0, :], gq_rep[:, 0:64], [rgqk]); ld(gq64[:, 1, :], gk_rep[:, 0:64], [rgqk]); ld(gsub[:, :], gsub_rep[:, 0:128], [rgqk])
    ts(gsub[:, :], gsub[:, :], 1.0 - LAM_INIT, None, ALU.mult, None, [rgqk], [rgqk])
    ld(ropet[:, 0:16, :], ropep, [rrope]); ld(ropet[:, 16:32, :], ropeo, [rrope]); ld(ropet[0:64, 32, :], ropes, [rrope])
    memset(VA[:, :, :, 128:130], 1.0, rVA)
    memset(QT[:, :, :, :], 0.0, [rQT])
    epsc = cols[:, C_EPS:C_EPS + 1]
    zeroc = cols[:, C_ZERO:C_ZERO + 1]
    pbiasc = cols[:, C_PBIAS:C_PBIAS + 1]
    pmulc = cols[:, C_PMUL:C_PMUL + 1]

    lt, rlt = t32.next()
    ld(lt[:, 0:256], lamv.partition_broadcast(128), [rlt])
    tt(lt[:, 256:320], lt[:, 0:64], lt[:, 64:128], ALU.mult, [rlt], [rlt])
    tt(lt[:, 320:384], lt[:, 128:192], lt[:, 192:256], ALU.mult, [rlt], [rlt])
    B.op("dve", lambda h: h.tensor_reduce(out=small[:, 0:2], in_=lt[:, 256:384].rearrange("p (a d) -> p a d", d=64),
                                          axis=AX.X, op=ALU.add), [rlt], [rsmall])
    act(small[:, 2:4], small[:, 0:2], AF.Exp, [rsmall], [rsmall])
    tt(small[:, 4:5], small[:, 3:4], small[:, 2:3], ALU.subtract, [rsmall], [rsmall])
    ts(cols[:, C_NLAM:C_NLAM + 1], small[:, 4:5], -LAM_INIT, None, ALU.add, None, [rsmall], [rcols])

    ld(lruc[:, :, 0:4], clw, [rlruc]); ld(lruc[:, :, 4], clb, [rlruc]); ld(lruc[:, :, 5], b_rg, [rlruc])
    ld(lruc[:, :, 6], b_ig, [rlruc]); ld(lruc[:, :, 8], lru_lam, [rlruc])
    act(lruc[:, :, 9], lruc[:, :, 8], AF.Exp, [rlruc], [rlruc], scale=-1.0)
    act(lruc[:, :, 10], lruc[:, :, 9], AF.Ln, [rlruc], [rlruc], bias=cols[:, C_ONE:C_ONE + 1])
    ts(lruc[:, :, 7], lruc[:, :, 10], -8.0, None, ALU.mult, None, [rlruc], [rlruc])
    ld(fcw[:, :, 0:3], cfw, [rfcw]); ld(fcw[:, :, 3], cfb, [rfcw])
    memset(Wg[:], 0.0, [rWg])
    for n in range(8):
        ch, o = n // 2, (n % 2) * 64
        ld(Wg[o:o + 64, ch, o:o + 64], w_rg[n], [rWg], q="pool")
        ld(Wg[o:o + 64, 4 + ch, o:o + 64], w_ig[n], [rWg], q="pool")
    memset(hist[:], 0.0, [rhist]); memset(carry[:], 0.0, [rcarry]); memset(fhist[:], 0.0, [rfhist])

    def load_slot(src2d, nk, ncol):
        s, r = wslot.next()
        v = s[:, 0:nk * ncol].rearrange("p (a b) -> p a b", a=nk)
        B.dma("pool", lambda h: h.dma_start(out=v, in_=src2d.rearrange("(a p) n -> p a n", p=128)), (), [r])
        return v, r

    ct, rct = t32.next()
    ctv = ct[:, 0:136].rearrange("p (a b) -> p a b", a=8)
    ld(ctv, cT, [rct])
    ct2, rct2 = t32.next()
    ct2v = ct2[:, 0:136].rearrange("p (a b) -> p a b", a=8)
    act(ct2v, ctv, AF.Silu, [rct], [rct2])
    cp(scx[:, :, 0:1], ct2v[:, :, 0:1], [rct2], [rscx])
    for kc in range(8):
        cp(scx[:, kc, 1:65].rearrange("p (s t) -> p s t", t=4), ct2v[:, kc, 1:17].unsqueeze(2).to_broadcast([128, 16, 4]),
           [rct2], [rscx])
        cp(screp(kc)[0], ct2v[:, kc, 0:1].to_broadcast([128, 128]), [rct2], [screp(kc)[1]])
    adac = B.sb("adac", [128, 64], F32)
    badT, rbad = adac[:, 0:48], B.R()
    ld(badT[:, 0:48], b_adaT, [rbad])
    gT, rgT = adac[:, 48:64], B.R()
    ld(gT[:, 0:8], g1T, [rgT]); ld(gT[:, 8:16], g2T, [rgT])

    def ada_feat(slot_v, rslot, g, dst_idx, is_scale):
        for j in range(4):
            fch = g * 4 + j
            p, rp = PA()
            for kc in range(8):
                mm(p[:, 0:65], slot_v[:, kc, j * 128:(j + 1) * 128], scx[:, kc, :], kc == 0, kc == 7, [rslot, rscx], [rp])
            kk = fch % 8
            if is_scale:
                ts(ABc[:, dst_idx, kk, :], p[:, 0:65], badT[:, fch:fch + 1], 1.0, ALU.add, ALU.add, [rp, rbad, rAB], [rAB])
                gi = kk if dst_idx == 0 else 8 + kk
                ts(ABc[:, dst_idx, kk, :], ABc[:, dst_idx, kk, :], gT[:, gi:gi + 1], None, ALU.mult, None, [rAB, rgT], [rAB])
            else:
                ts(ABc[:, dst_idx, kk, :], p[:, 0:65], badT[:, fch:fch + 1], None, ALU.add, None, [rp, rbad, rAB], [rAB])

    def ada_tok(slot_v, rslot, g, which, half, sample):
        c0 = g * 512
        P = 64 if sample else 128
        dst = gtp[0:P, which, half * 512:(half + 1) * 512]
        ld(dst, b_ada[0:1, c0:c0 + 512].partition_broadcast(P), [rgtp[which]])
        p, rp = PA()
        for kc in range(8):
            if sample:
                mm(p[0:64, :], scx[:, kc, 1:65], slot_v[:, kc, :], kc == 0, kc == 7, [rslot, rscx], [rp])
            else:
                mm(p[:, :], screp(kc)[0], slot_v[:, kc, :], kc == 0, kc == 7, [rslot, screp(kc)[1]], [rp])
        tt(dst, dst, p[0:P, :], ALU.add, [rp, rgtp[which]], [rgtp[which]])

    def ada_group(g):
        sv, rs = load_slot(w_ada[:, g * 512:(g + 1) * 512], 8, 512)
        sec = g // 2
        if sec == 0:
            ada_feat(sv, rs, g, 1, False)
        elif sec == 1:
            ada_feat(sv, rs, g, 0, True)
        elif sec == 2:
            ada_tok(sv, rs, g, 0, g % 2, False)
        elif sec == 3:
            ada_feat(sv, rs, g, 3, False)
        elif sec == 4:
            ada_feat(sv, rs, g, 2, True)
        else:
            ada_tok(sv, rs, g, 1, g % 2, False)

    for g in range(4):
        ada_group(g)
    ADA_LATER = {0: (4, 5), 1: (6, 7, 8, 9), 2: (10, 11)}

    marks = []

    def mark(label):
        marks.append((label, len(B.ops["pe"]), len(B.ops["act"]), len(B.ops["dve"])))

    def finish():
        mark("end")
        if os.environ.get("KMARKS"):
            import json
            json.dump(marks, open(os.environ["KMARKS"], "w"))
        B.finalize()
        B.close()
        import sys
        print("ops:", {e: len(B.ops[e]) for e in B.ENGS}, "needed:", {e: sum(o.needed for o in B.ops[e]) for e in B.ENGS},
              "dma sems:", len(B.dma_sems), file=sys.stderr)
        return nc

    if stage == 0:
        for g in range(4, 12):
            ada_group(g)
        st(y_s[:, :], gtp[0:64, 0, :], [rgtp[0]])
        st(y_o[0:128, :], gtp[:, 1, :], [rgtp[1]])
        st(k_s[:, 0:260].rearrange("p (a b) -> p a b", a=4), ABc[0:64, :, 0, :], [rAB])
        return finish()

    def rstd_from_ss(out_ap, ss_ap, n, reads, writes):
        act(out_ap, ss_ap, AF.Ln, reads, writes, scale=1.0 / n, bias=epsc[0:ss_ap.shape[0], :])
        act(out_ap, out_ap, AF.Exp, writes, writes, scale=-0.5)

    def norm_transpose(P, xt, rxt, t, abi, col0, ncol):
        sb_, rs_ = stat.next()
        act(xnbf[0:P, :], xt, AF.Square, [rxt], [rxn, rs_], accum_out=sb_[0:P, 0:1])
        rstd_from_ss(sb_[0:P, 1:2], sb_[0:P, 0:1], 1024, [rs_, rcols], [rs_])
        ts(xnbf[0:P, :], xt, sb_[0:P, 1:2], None, ALU.mult, None, [rxt, rs_], [rxn])
        for kc in range(8):
            tr(psT[:, kc * 128:kc * 128 + P], xnbf[0:P, kc * 128:(kc + 1) * 128], ident[0:P, 0:P], [rxn, rident], [rT])
        pv = psT[:, :].rearrange("p (a b) -> p a b", a=8)[:, :, 0:P]
        if ncol == 1:
            A = ABc[:, abi, :, col0:col0 + 1].to_broadcast([128, 8, P])
            Bb = ABc[:, abi + 1, :, col0:col0 + 1].to_broadcast([128, 8, P])
        else:
            A = ABc[:, abi, :, col0:col0 + P]
            Bb = ABc[:, abi + 1, :, col0:col0 + P]
        for hf in range(2):
            tmp, rtmp = t32.next()
            tv = tmp[:, 0:4 * P].rearrange("p (a b) -> p a b", a=4)
            tt(tv, pv[:, hf * 4:(hf + 1) * 4, :], A[:, hf * 4:(hf + 1) * 4, :], ALU.mult, [rT, rAB], [rtmp])
            tt(uT[:, hf * 4:(hf + 1) * 4, t * 128:t * 128 + P], tv, Bb[:, hf * 4:(hf + 1) * 4, :], ALU.add, [rtmp, rAB], [ruT])

    def qk_post_gen(P, ps_ap, rps, gi, rope_t, k_dst, q_dst, ktile, t, dst_reg=None):
        sq, rsq = t32.next()
        qn, rqn = t32.next()
        sb_, rs_ = stat.next()
        act(sq[0:P, :], ps_ap, AF.Square, [rps], [rsq])
        yield
        B.op("dve", lambda h: h.tensor_reduce(out=sb_[0:P, 0:8], in_=sq[0:P, :].rearrange("p (g d) -> p g d", d=64),
                                              axis=AX.X, op=ALU.add), [rsq], [rs_])
        yield
        act(sb_[0:P, 8:16], sb_[0:P, 0:8], AF.Ln, [rs_, rcols], [rs_], scale=1.0 / 64, bias=epsc[0:P, :])
        yield
        act(sb_[0:P, 8:16], sb_[0:P, 8:16], AF.Exp, [rs_], [rs_], scale=-0.5)
        yield
        qv = qn[0:P, :].rearrange("p (g d) -> p g d", d=64)
        tt(qv, ps_ap.rearrange("p (g d) -> p g d", d=64), sb_[0:P, 8:16].unsqueeze(2).to_broadcast([P, 8, 64]), ALU.mult,
           [rps, rs_], [rqn])
        yield
        tt(qv, qv, gq64[0:P, gi, :].unsqueeze(1).to_broadcast([P, 8, 64]), ALU.mult, [rqn, rgqk], [rqn])
        yield
        cosb = ropet[0:P, rope_t, 0:8].unsqueeze(1).to_broadcast([P, 8, 8])
        sinb = ropet[0:P, rope_t, 8:16].unsqueeze(1).to_broadcast([P, 8, 8])
        tv = sq[0:P, 0:256].rearrange("p (a g d) -> p a g d", a=4, d=8)
        r1, r2 = qv[:, :, 0:8], qv[:, :, 8:16]
        tt(tv[:, 0], r1, cosb, ALU.mult, [rqn, rrope], [rsq])
        yield
        tt(tv[:, 1], r2, sinb, ALU.mult, [rqn, rrope], [rsq])
        yield
        tt(tv[:, 2], r1, sinb, ALU.mult, [rqn, rrope], [rsq])
        yield
        tt(tv[:, 3], r2, cosb, ALU.mult, [rqn, rrope], [rsq])
        yield
        tt(r1, tv[:, 0], tv[:, 1], ALU.subtract, [rsq], [rqn])
        yield
        tt(r2, tv[:, 2], tv[:, 3], ALU.add, [rsq], [rqn])
        yield
        if k_dst is not None:
            st(k_dst, qn[0:P, :], [rqn])
        cp(qkbf[0:P, :], qn[0:P, :], [rqn], [rqkbf])
        for hh in range(4):
            tr(psT[:, hh * 128:hh * 128 + P], qkbf[0:P, hh * 128:(hh + 1) * 128], ident[0:P, 0:P], [rqkbf, rident], [rT])
        pv = psT[:, 0:512].rearrange("p (a b) -> p a b", a=4)[:, :, 0:P]
        if q_dst == "blk":
            for c in range(2):
                cp(QT[c * 64:(c + 1) * 64, :, c, t * 128:t * 128 + P], pv[c * 64:(c + 1) * 64, :, :], [rT], [rQT])
        elif q_dst is not None:
            cp(q_dst, pv, [rT], [dst_reg if dst_reg is not None else rQT])
        else:
            cp(KT[:, :, ktile * 128:ktile * 128 + P], pv, [rT], [rKT[ktile]])

    def lockstep(gens):
        gens = list(gens)
        while gens:
            nxt = []
            for g in gens:
                try:
                    next(g)
                    nxt.append(g)
                except StopIteration:
                    pass
            gens = nxt

    def qk_post(*a, **k):
        lockstep([qk_post_gen(*a, **k)])

    def lru_chunk_gen(ch, T, slots, rslots, do_out, sample, moT_base):
        NS, TT = (16, 4) if sample else (1, T)
        lxv, rlx = slots[0], rslots[0]
        p, rp = PB()
        for kc in range(8):
            mm(p[:, 0:T], lxv[:, kc, ch * 128:(ch + 1) * 128], uT[:, kc, 0:T], kc == 0, kc == 7, [rlx, ruT], [rp])
        p3 = p[:, 0:T].rearrange("p (s t) -> p s t", s=NS)
        xc, rxc = t32.next()
        rg, rrg = t32.next()
        ig, rig = t32.next()
        x3 = xc[:, 0:T].rearrange("p (s t) -> p s t", s=NS)
        yield
        act(xc[:, 0:T], p[:, 0:T], AF.Identity, [rp, rlruc], [rxc], scale=lruc[:, ch, 3:4], bias=lruc[:, ch, 4:5])
        yield
        for k, sh in ((2, 1), (1, 2), (0, 3)):
            stt(x3[:, :, sh:TT], p3[:, :, 0:TT - sh], lruc[:, ch, k:k + 1], x3[:, :, sh:TT], ALU.mult, ALU.add,
                [rp, rlruc, rxc], [rxc])
            yield
        hv = hist[:, ch, 0:NS, :]
        stt(x3[:, :, 0:3], hv[:, :, 0:3], lruc[:, ch, 0:1], x3[:, :, 0:3], ALU.mult, ALU.add, [rhist, rlruc, rxc], [rxc])
        stt(x3[:, :, 0:2], hv[:, :, 1:3], lruc[:, ch, 1:2], x3[:, :, 0:2], ALU.mult, ALU.add, [rhist, rlruc, rxc], [rxc])
        stt(x3[:, :, 0:1], hv[:, :, 2:3], lruc[:, ch, 2:3], x3[:, :, 0:1], ALU.mult, ALU.add, [rhist, rlruc, rxc], [rxc])
        if sample:
            act(hv, p3[:, :, 1:4], AF.Identity, [rp], [rhist])
        else:
            act(hv, p3[:, :, TT - 3:TT], AF.Identity, [rp], [rhist])
        yield
        xb, rxb = Et.next()
        cp(xb[:, 0:T], xc[:, 0:T], [rxc], [rxb])
        yield
        pr, rpr = PB()
        mm(pr[:, 0:T], Wg[:, ch, :], xb[:, 0:T], True, True, [rWg, rxb], [rpr])
        act(rg[:, 0:T], pr[:, 0:T], AF.Sigmoid, [rpr, rlruc], [rrg], bias=lruc[:, ch, 5:6])
        yield
        pi, rpi = PB()
        mm(pi[:, 0:T], Wg[:, 4 + ch, :], xb[:, 0:T], True, True, [rWg, rxb], [rpi])
        act(ig[:, 0:T], pi[:, 0:T], AF.Sigmoid, [rpi, rlruc], [rig], bias=lruc[:, ch, 6:7])
        yield
        tt(ig[:, 0:T], ig[:, 0:T], xc[:, 0:T], ALU.mult, [rig, rxc], [rig])
        yield
        act(rg[:, 0:T], rg[:, 0:T], AF.Exp, [rrg, rlruc], [rrg], scale=lruc[:, ch, 7:8])
        yield
        tt(xc[:, 0:T], rg[:, 0:T], rg[:, 0:T], ALU.mult, [rrg], [rxc])
        yield
        act(xc[:, 0:T], xc[:, 0:T], AF.Ln, [rxc, rcols], [rxc], scale=-1.0, bias=cols[:, C_ONE:C_ONE + 1])
        yield
        act(xc[:, 0:T], xc[:, 0:T], AF.Exp, [rxc], [rxc], scale=0.5)
        yield
        tt(ig[:, 0:T], ig[:, 0:T], xc[:, 0:T], ALU.mult, [rig, rxc], [rig])
        yield
        a3 = rg[:, 0:T].rearrange("p (s t) -> p s t", s=NS)
        b3 = ig[:, 0:T].rearrange("p (s t) -> p s t", s=NS)
        if sample:
            h0 = carry[:, ch, :].unsqueeze(2)
            tmpc, rtmpc = stat.next()
            tt(tmpc[:, 0:16].unsqueeze(2), a3[:, :, 0:1], h0, ALU.mult, [rrg, rcarry], [rtmpc])
            tt(b3[:, :, 0:1], b3[:, :, 0:1], tmpc[:, 0:16].unsqueeze(2), ALU.add, [rig, rtmpc], [rig])
            memset(a3[:, :, 0:1], 0.0, [rrg])
            B.op("dve", lambda h, o=xc, a=rg, b=ig: h.tensor_tensor_scan(out=o[:, 0:T], data0=a[:, 0:T], data1=b[:, 0:T],
                                                                          initial=0.0, op0=ALU.mult, op1=ALU.add),
                 [rrg, rig], [rxc])
            cp(carry[:, ch, :].unsqueeze(2), x3[:, :, 3:4], [rxc], [rcarry])
        else:
            B.op("dve", lambda h, o=xc, a=rg, b=ig, c=ch: h.tensor_tensor_scan(out=o[:, 0:T], data0=a[:, 0:T], data1=b[:, 0:T],
                                                                                initial=carry[:, c, 0:1], op0=ALU.mult, op1=ALU.add),
                 [rrg, rig, rcarry], [rxc])
            cp(carry[:, ch, 0:1], xc[:, T - 1:T], [rxc], [rcarry])
        yield
        if do_out:
            lgv, rlg = slots[1], rslots[1]
            p2, rp2 = PB()
            for kc in range(8):
                mm(p2[:, 0:T], lgv[:, kc, ch * 128:(ch + 1) * 128], uT[:, kc, 0:T], kc == 0, kc == 7, [rlg, ruT], [rp2])
            gl, rgl = rg, rrg
            act(gl[:, 0:T], p2[:, 0:T], AF.Gelu_apprx_tanh, [rp2], [rgl])
            yield
            mo, rmo = chunks[moT_base + ch]
            tt(mo[:, 0:T], xc[:, 0:T], gl[:, 0:T], ALU.mult, [rxc, rgl], [rmo])

    def lru_chunks(T, nseq, slots, rslots, do_out, sample, moT_base):
        for pair in ((0, 1), (2, 3)):
            lockstep([lru_chunk_gen(ch, T, slots, rslots, do_out, sample, moT_base) for ch in pair])

    def sub_ln_to_moT(P, at, rat, nq, head, defer=False):
        sq, rsq = t32.next()
        act(sq[0:P, 0:nq * 128], at[0:P, 0:nq * 128], AF.Square, [rat], [rsq])
        sb_, rs_ = stat.next()
        B.op("dve", lambda h: h.tensor_reduce(out=sb_[0:P, 0:nq], in_=sq[0:P, 0:nq * 128].rearrange("p (g d) -> p g d", d=128),
                                              axis=AX.X, op=ALU.add), [rsq], [rs_])
        rstd_from_ss(sb_[0:P, 8:8 + nq], sb_[0:P, 0:nq], 128, [rs_, rcols], [rs_])
        a3 = at[0:P, 0:nq * 128].rearrange("p (g d) -> p g d", d=128)
        tt(a3, a3, sb_[0:P, 8:8 + nq].unsqueeze(2).to_broadcast([P, nq, 128]), ALU.mult, [rat, rs_], [rat])
        tt(attnbf[0:P, 0:nq * 128].rearrange("p (g d) -> p g d", d=128), a3,
           gsub[0:P, :].unsqueeze(1).to_broadcast([P, nq, 128]), ALU.mult, [rat, rgqk], [rattnbf])
        if defer:
            return
        sub_ln_tail(P, nq, head)

    def sub_ln_tail(P, nq, head):
        for qs in range(nq):
            tr(psT[:, qs * 128:qs * 128 + P], attnbf[0:P, qs * 128:(qs + 1) * 128], ident[0:P, 0:P], [rattnbf, rident], [rT])
        mo, rmo = chunks[head]
        if nq == 4:
            cp(mo[:, 0:512], psT[:, 0:512], [rT], [rmo])
        else:
            cp(mo[:, 0:P], psT[:, 0:P], [rT], [rmo])

    def attn_prompt(full_tiles, diag_tiles):
        Ov = [psB[:, 2 * c:2 * c + 2, :].rearrange("p a (s w) -> p (a s) w", w=256) for c in range(2)]
        rO = [[rB[0], rB[1]], [rB[2], rB[3]]]
        sbanks = [psA[0], psA[1], (psB[:, 4, :], rB[4])]
        LOOK = 2
        for hh in range(4):
            seq = [(kt, bc, None) for kt, bc in full_tiles] + [(kt, zeroc, j) for j, kt in enumerate(diag_tiles)]
            steps = [(n, kt, bc, dj, c) for n, (kt, bc, dj) in enumerate(seq) for c in range(2)]
            inflight = {}

            def s_issue(i):
                n, kt, bc, dj, c = steps[i]
                q0 = 0 if dj is None else dj * 128
                p, rp = sbanks[i % 3]
                mm(p[:, q0:512], KT[:, hh, kt * 128:(kt + 1) * 128], QT[:, hh, c, q0:512],
                   True, True, [rKT[kt], rQT], [rp])
                e, re = Et.next()
                act(e[:, q0:512], p[:, q0:512], AF.Exp, [rp, rcols], [re], scale=0.125, bias=bc)
                if dj is not None:
                    tt(e[:, q0:q0 + 128], e[:, q0:q0 + 128], tri[:, :], ALU.mult, [re, rtri], [re])
                inflight[i] = (e, re)

            def pv_issue(i):
                n, kt, bc, dj, c = steps[i]
                q0 = 0 if dj is None else dj * 128
                e, re = inflight.pop(i)
                for qs in range(q0 // 128, 4):
                    last = (dj is not None and qs == dj and qs % 2 == 1)
                    mm(Ov[c][:, qs, 0:129], e[:, qs * 128:(qs + 1) * 128], VA[:, kt, hh, 0:129], (n == 0 and qs % 2 == 0), last,
                       [re, rVA[kt]], [rO[c][qs // 2]])

            for i in range(len(steps) + LOOK):
                if i < len(steps):
                    s_issue(i)
                if i - LOOK >= 0:
                    pv_issue(i - LOOK)
            if hh > 0:
                sub_ln_tail(128, 4, hh - 1)
            sb_, rs_ = stat.next()
            B.op("dve", lambda h, s=sb_: h.reciprocal(out=s[:, 0:4].unsqueeze(2), in_=Ov[0][:, :, 128:129]), rO[0], [rs_])
            B.op("dve", lambda h, s=sb_: h.reciprocal(out=s[:, 4:8].unsqueeze(2), in_=Ov[1][:, :, 128:129]), rO[1], [rs_])
            ts(sb_[:, 4:8], sb_[:, 4:8], cols[:, C_NLAM:C_NLAM + 1], None, ALU.mult, None, [rs_, rcols], [rs_])
            at, rat = t32.next()
            a3 = at[:, :].rearrange("p (g d) -> p g d", d=128)
            tt(a3, Ov[0][:, :, 0:128], sb_[:, 0:4].unsqueeze(2).to_broadcast([128, 4, 128]), ALU.mult, rO[0] + [rs_], [rat])
            t2, rt2 = t32.next()
            b3 = t2[:, :].rearrange("p (g d) -> p g d", d=128)
            tt(b3, Ov[1][:, :, 0:128], sb_[:, 4:8].unsqueeze(2).to_broadcast([128, 4, 128]), ALU.mult, rO[1] + [rs_], [rt2])
            tt(at[:, :], at[:, :], t2[:, :], ALU.add, [rat, rt2], [rat])
            sub_ln_to_moT(128, at, rat, 4, hh, defer=True)
        sub_ln_tail(128, 4, 3)

    def out_proj_residual(P, ntile, gt_ap_fn, rgt):
        for nh in range(2):
            sv, rs = load_slot(w_out[:, nh * 512:(nh + 1) * 512], 8, 512)
            for t in range(ntile):
                p, rp = PB()
                for fc in range(8):
                    mo, rmo = chunks[fc]
                    mm(p[0:P, :], mo[:, t * 128:t * 128 + P], sv[:, fc, :], fc == 0, fc == 7, [rmo, rs], [rp])
                tmp, rtmp = t32.next()
                tt(tmp[0:P, :], p[0:P, :], gt_ap_fn(0, nh), ALU.mult, [rp, rgt[0]], [rtmp])
                xs_ = xblk[0:P, t, nh * 512:(nh + 1) * 512]
                tt(xs_, xs_, tmp[0:P, :], ALU.add, [rx[t], rtmp], [rx[t]])

    def ffn_block(P, ntile, T, abcol0, abncol, gt_ap_fn, rgt, sample, full, y_dst_fn, only_last_tile=False):
        NS, TT = (16, 4) if sample else (1, T)
        if only_last_tile:
            norm_transpose(128, xblk[:, ntile - 1, :], rx[ntile - 1], 0, 2, abcol0, abncol)
            T = 128; TT = 128
        else:
            for t in range(ntile):
                norm_transpose(P, xblk[0:P, t, :], rx[t], t, 2, abcol0, abncol)
        segs = [(0, 4), (4, 4), (8, 4), (12, 4), (16, 4), (20, 2)]
        if not sample:
            tt(btile[:, :, 0], fcw[:, :, 0], fhist[:, :, 0, 0], ALU.mult, [rfcw, rfhist], [rbt])
            tt(small[:, 0:44], fcw[:, :, 1], fhist[:, :, 0, 1], ALU.mult, [rfcw, rfhist], [rsmall])
            tt(btile[:, :, 0], btile[:, :, 0], small[:, 0:44], ALU.add, [rbt, rsmall], [rbt])
            tt(btile[:, :, 1], fcw[:, :, 0], fhist[:, :, 0, 1], ALU.mult, [rfcw, rfhist], [rbt])
        for s0, n in segs:
            sg, rsg = load_slot(w_up[:, s0 * 128:(s0 + n) * 128], 8, n * 128)
            sv_, rsv = load_slot(w_up[:, 2816 + s0 * 128:2816 + (s0 + n) * 128], 8, n * 128)
            for j in range(n):
                c = s0 + j
                hcs = []
                for which, (sl, rsl, fidx) in enumerate(((sg, rsg, c), (sv_, rsv, 22 + c))):
                    p, rp = PB()
                    for kc in range(8):
                        mm(p[:, 0:T], sl[:, kc, j * 128:(j + 1) * 128], uT[:, kc, 0:T], kc == 0, kc == 7, [rsl, ruT], [rp])
                    p3 = p[:, 0:T].rearrange("p (s t) -> p s t", s=NS)
                    hc, rhc = t32.next()
                    h3 = hc[:, 0:T].rearrange("p (s t) -> p s t", s=NS)
                    act(hc[:, 0:T], p[:, 0:T], AF.Identity, [rp, rfcw], [rhc], scale=fcw[:, fidx, 2:3], bias=fcw[:, fidx, 3:4])
                    stt(h3[:, :, 1:TT], p3[:, :, 0:TT - 1], fcw[:, fidx, 1:2], h3[:, :, 1:TT], ALU.mult, ALU.add, [rp, rfcw, rhc], [rhc])
                    stt(h3[:, :, 2:TT], p3[:, :, 0:TT - 2], fcw[:, fidx, 0:1], h3[:, :, 2:TT], ALU.mult, ALU.add, [rp, rfcw, rhc], [rhc])
                    fh = fhist[:, fidx, 0:NS, :]
                    if sample:
                        stt(h3[:, :, 0:2], fh[:, :, 0:2], fcw[:, fidx, 0:1], h3[:, :, 0:2], ALU.mult, ALU.add, [rfhist, rfcw, rhc], [rhc])
                        stt(h3[:, :, 0:1], fh[:, :, 1:2], fcw[:, fidx, 1:2], h3[:, :, 0:1], ALU.mult, ALU.add, [rfhist, rfcw, rhc], [rhc])
                    else:
                        tt(hc[:, 0:2], hc[:, 0:2], btile[:, fidx, :], ALU.add, [rhc, rbt], [rhc])
                    act(fh, p3[:, :, TT - 2:TT], AF.Identity, [rp], [rfhist])
                    hcs.append((hc, rhc))
                if full:
                    (hg, rhg), (hv, rhv) = hcs
                    act(hg[:, 0:T], hg[:, 0:T], AF.Silu, [rhg], [rhg])
                    a_, ra_ = chunks[c]
                    tt(a_[:, 0:T], hg[:, 0:T], hv[:, 0:T], ALU.mult, [rhg, rhv], [ra_])
        if not full:
            return
        for nh in range(2):
            accs = [PB() for _ in range(ntile)]
            for s0, n in ((0, 8), (8, 8), (16, 6)):
                sd, rsd = load_slot(w_down[s0 * 128:(s0 + n) * 128, nh * 512:(nh + 1) * 512], n, 512)
                for t in range(ntile):
                    p, rp = accs[t]
                    for j in range(n):
                        c = s0 + j
                        a_, ra_ = chunks[c]
                        mm(p[0:P, :], a_[:, t * 128:t * 128 + P], sd[:, j, :], c == 0, c == 21, [ra_, rsd], [rp])
            for t in range(ntile):
                p, rp = accs[t]
                tmp, rtmp = t32.next()
                tt(tmp[0:P, :], p[0:P, :], gt_ap_fn(1, nh), ALU.mult, [rp, rgt[1]], [rtmp])
                xs_ = xblk[0:P, t, nh * 512:(nh + 1) * 512]
                tt(xs_, xs_, tmp[0:P, :], ALU.add, [rx[t], rtmp], [rx[t]])
        for t in range(ntile):
            st(y_dst_fn(t), xblk[0:P, t, :], [rx[t]])

    def gtp_fn(which, nh):
        return gtp[:, which, nh * 512:(nh + 1) * 512]

    def gts_fn(which, nh):
        return gtp[0:64, which, nh * 512:(nh + 1) * 512]

    def prompt_block(kind, bi):
        mark(f"{kind}{bi}:start")
        src = xo if kind == "own" else xp
        r0 = bi * 512
        do_q = kind in ("own", "halo")
        tile0 = (16 + bi * 4) if kind == "own" else bi * 4
        for t in range(4):
            ld(xblk[:, t, :], src[r0 + t * 128:r0 + (t + 1) * 128, :], [rx[t]])
        for t in range(4):
            norm_transpose(128, xblk[:, t, :], rx[t], t, 0, 0, 1)
        groups = ([0] if do_q else []) + [1, 2]
        for cg in groups:
            sv, rs = load_slot(w_in[:, cg * 512:(cg + 1) * 512], 8, 512)
            if cg in (0, 1):
                for pair in ((0, 1, 2), (3,)):
                    gens = []
                    for t in pair:
                        p, rp = PB()
                        for kc in range(8):
                            mm(p[:, :], uT[:, kc, t * 128:(t + 1) * 128], sv[:, kc, :], kc == 0, kc == 7, [ruT, rs], [rp])
                        kt = tile0 + t
                        if cg == 0:
                            gens.append(qk_post_gen(128, p[:, :], rp, 0, kt, None, "blk", None, t))
                        else:
                            kd = k_o[r0 + t * 128:r0 + (t + 1) * 128, :] if kind == "own" else None
                            gens.append(qk_post_gen(128, p[:, :], rp, 1, kt, kd, None, kt, t))
                    lockstep(gens)
                continue
            for t in range(4):
                p, rp = PB()
                for kc in range(8):
                    mm(p[:, :], uT[:, kc, t * 128:(t + 1) * 128], sv[:, kc, :], kc == 0, kc == 7, [ruT, rs], [rp])
                kt = tile0 + t
                if kind == "own":
                    v32, rv32 = t32.next()
                    cp(v32[:, :], p[:, :], [rp], [rv32])
                    st(v_o[r0 + t * 128:r0 + (t + 1) * 128, :], v32[:, :], [rv32])
                cp(VA[:, kt, :, 0:128], p[:, :].rearrange("p (a b) -> p a b", a=4), [rp], [rVA[kt]])
        mark(f"{kind}{bi}:lru")
        lxs, rlxs = load_slot(w_in[:, 1536:2048], 8, 512)
        if do_q:
            lgs, rlgs = load_slot(w_in[:, 2048:2560], 8, 512)
            lru_chunks(512, 1, (lxs, lgs), (rlxs, rlgs), True, False, 4)
        else:
            lru_chunks(512, 1, (lxs, None), (rlxs, None), False, False, 4)
        if not do_q:
            return
        if kind == "halo":
            full = [(kt, pbiasc) for kt in range(0, 12)]
            diag = [12, 13, 14, 15]
        else:
            full = [(kt, pbiasc) for kt in range(0, 16)] + [(kt, zeroc) for kt in range(16, 16 + bi * 4)]
            diag = [16 + bi * 4 + j for j in range(4)]
        mark(f"{kind}{bi}:attn")
        attn_prompt(full, diag)
        mark(f"{kind}{bi}:outproj")
        out_proj_residual(128, 4, gtp_fn, rgtp)
        mark(f"{kind}{bi}:ffn")
        if kind == "halo":
            ffn_block(128, 4, 512, 0, 1, gtp_fn, rgtp, False, False, None, only_last_tile=True)
            ts(carry[:, :, 0:1], carry[:, :, 0:1], pmulc, None, ALU.mult, None, [rcarry, rcols], [rcarry])
            ts(hist[:, :, 0, :], hist[:, :, 0, :], pmulc, None, ALU.mult, None, [rhist, rcols], [rhist])
            ts(fhist[:, :, 0, :], fhist[:, :, 0, :], pmulc, None, ALU.mult, None, [rfhist, rcols], [rfhist])
        elif stage == 31:
            pass
        elif stage == 32:
            ffn_block(128, 4, 512, 0, 1, gtp_fn, rgtp, False, False, None)
        else:
            ffn_block(128, 4, 512, 0, 1, gtp_fn, rgtp, False, True, lambda t: y_o[r0 + t * 128:r0 + (t + 1) * 128, :])

    for bi in range(3):
        prompt_block("pre", bi)
        for g in ADA_LATER[bi]:
            ada_group(g)
        if stage == 1:
            return finish()
    prompt_block("halo", 3)
    if stage == 2:
        return finish()
    for bi in range(4):
        prompt_block("own", bi)
        if stage in (3, 31, 32):
            return finish()
    if stage == 4:
        return finish()
    st(lc_p, hist[:, :, 0, :], [rhist]); st(lh_p, carry[:, :, 0], [rcarry]); st(fc_p, fhist[:, :, 0, :], [rfhist])

    mark("sample:start")
    for g in (4, 5, 10, 11):
        sv, rs = load_slot(w_ada[:, g * 512:(g + 1) * 512], 8, 512)
        ada_tok(sv, rs, g, 0 if g < 6 else 1, g % 2, True)
    ld(hist[:], slc, [rhist]); ld(carry[:], slh, [rcarry]); ld(fhist[:], sfc, [rfhist])
    ld(xblk[0:64, 0, :], xs, [rx[0]])
    norm_transpose(64, xblk[0:64, 0, :], rx[0], 0, 0, 1, 64)
    allK = rKT + rVA
    rKpTs = [B.R("KpT0"), B.R("KpT1")]; rvss = [[B.R(f"vs{i}_{g}") for g in range(4)] for i in range(2)]
    rkg = [B.R(f"kg{i}") for i in range(2)]; rQsb = B.R("Qsb"); rKsT = B.R("KsT"); rVs = B.R("VAs")
    for rr in [rKpTs[0], rKpTs[1]] + rvss[0] + rvss[1]:
        for o_ in allK:
            if o_.last_w is not None:
                rr.readers.append(o_.last_w)
            rr.readers.extend(o_.readers)
    KpTs = [KT[:, :, 0:2048], KT[:, :, 2048:4096]]
    VAf = VA[:, :, :, :].rearrange("p a b c -> p (a b c)")
    vss = [VAf[:, i * 8192:(i + 1) * 8192].rearrange("p (g f) -> p g f", f=512) for i in range(2)]
    smpbuf = B.sb("smpbuf", [128, 5376], BF16)
    kg = [smpbuf[:, i * 2048:(i + 1) * 2048].rearrange("p (g f) -> p g f", f=512) for i in range(2)]
    Qsb = smpbuf[:, 4096:4096 + 512].rearrange("p (a s c) -> p a s c", a=4, s=16)
    KsT = smpbuf[:, 4608:4608 + 256].rearrange("p (a t) -> p a t", a=4)
    VAs = smpbuf[:, 4864:4864 + 512].rearrange("p (a d) -> p a d", a=4)
    QsT = QsTt[:, :, :]
    for cg in range(3):
        sv, rs = load_slot(w_in[:, cg * 512:(cg + 1) * 512], 8, 512)
        p, rp = PB()
        for kc in range(8):
            mm(p[0:64, :], uT[:, kc, 0:64], sv[:, kc, :], kc == 0, kc == 7, [ruT, rs], [rp])
        if cg == 0:
            qk_post(64, p[0:64, :], rp, 0, 32, None, QsTt[:, :, :], None, 0)
        elif cg == 1:
            sq_k = k_s
            qk_post(64, p[0:64, :], rp, 1, 32, sq_k, KsT, None, 0, dst_reg=rKsT)
        else:
            v32, rv32 = t32.next()
            cp(v32[0:64, :], p[0:64, :], [rp], [rv32])
            st(v_s, v32[0:64, :], [rv32])
            cp(VAs[0:64, :, :], p[0:64, :].rearrange("p (a b) -> p a b", a=4), [rp], [rVs])
    lxs, rlxs = load_slot(w_in[:, 1536:2048], 8, 512)
    lgs, rlgs = load_slot(w_in[:, 2048:2560], 8, 512)
    lru_chunks(64, 16, (lxs, lgs), (rlxs, rlgs), True, True, 4)
    st(lc_s, hist[:], [rhist]); st(lh_s, carry[:], [rcarry])
    memset(Qsb, 0.0, [rQsb])
    for c in range(2):
        cp(Qsb[c * 64:(c + 1) * 64, :, :, c * 4:(c + 1) * 4], QsT[c * 64:(c + 1) * 64, :, :].rearrange("p a (s t) -> p a s t", t=4),
           [rQT], [rQsb])
    ptt_, rpti = t32.next()
    pti = ptt_[:, 0:256].bitcast(I32)
    ptf_, rptf = t32.next()
    ptf = ptf_[:, 0:256]
    idx = B.sb("idx", [128, 64], I32); ridx = B.R()
    gselt = B.sb("gselt", [128, 5], F32); rgsel = B.R()
    ld(pti[:], ptab.partition_broadcast(128), [rpti])
    ld(gselt[:], gsel, [rgsel])
    cp(ptf[:], pti[:], [rpti], [rptf])
    ptf3 = ptf.rearrange("p (a j) -> p a j", j=4)
    i2f = ptf_[:, 256:320]
    ts(i2f, ptf3[:, :, 0], gselt[:, 0:1], None, ALU.mult, None, [rptf, rgsel], [rptf])
    for j in range(1, 4):
        stt(i2f, ptf3[:, :, j], gselt[:, j:j + 1], i2f, ALU.mult, ALU.add, [rptf, rgsel], [rptf])
    ts(i2f, i2f, 32.0, gselt[:, 4:5], ALU.mult, ALU.add, [rptf, rgsel], [rptf])
    cp(idx[:], i2f, [rptf], [ridx])
    cache_k4 = cache_k.rearrange("(r f) d -> r (f d)", f=4)
    cache_v4 = cache_v.rearrange("(r f) d -> r (f d)", f=4)
    msk = B.sb("msk", [64, 16, 8], F32); rmsk = B.R()
    cmbt = B.sb("cmbt", [8, 124], F32); rcmb = B.R()
    ld(msk[:], smask, [rmsk]); ld(cmbt[:], cmb, [rcmb])
    ones_bf = B.sb("ones_bf", [128, 1], BF16); rones = B.R()
    memset(ones_bf[:], 1.0, [rones])
    lamc8 = B.sb("lamc8", [8, 1], F32); rlamc8 = B.R()
    memset(lamc8[:], 1.0, [rlamc8])
    cp(lamc8[0:8, :], cols[0:8, C_NLAM:C_NLAM + 1], [rcols], [rlamc8])
    memset(lamc8[0:4, :], 1.0, [rlamc8])
    att_all, ratt = psB[:, 4, :], rB[4]
    pb4 = [0]

    def PB4():
        k = pb4[0] % 3
        pb4[0] += 1
        return psB[:, k, :], rB[k]
    psT2 = psB[:, 3, :].bitcast(BF16)
    tbufs = [(psT, rT), (psT2, rB[3])]
    mark("sample:seqs")
    for s in range(16):
        KpT, rKpT = KpTs[s % 2], rKpTs[s % 2]
        vsb, rvs = vss[s % 2], rvss[s % 2]
        for g in range(4):
            kb, rkb = kg[g % 2], rkg[g % 2]
            B.dma("pool", lambda h, kb=kb, col=s * 4 + g: h.indirect_dma_start(
                out=kb.rearrange("p a f -> p (a f)"), out_offset=None, in_=cache_k4[:, :],
                in_offset=bass.IndirectOffsetOnAxis(ap=idx[:, col:col + 1], axis=0)), [ridx], [rkb])
            for j in range(4):
                pg = g * 4 + j
                tb, rtb = tbufs[pg % 2]
                for hh in range(4):
                    tr(tb[:, hh * 128:(hh + 1) * 128], kb[:, j, hh * 128:(hh + 1) * 128], ident[:, :], [rkb, rident], [rtb])
                cp(KpT[:, :, pg * 128:(pg + 1) * 128], tb[:, 0:512].rearrange("p (a b) -> p a b", a=4), [rtb], [rKpT])
        for g in range(4):
            B.dma("pool", lambda h, g=g, col=s * 4 + g, vsb=vsb: h.indirect_dma_start(
                out=vsb[:, g * 4:(g + 1) * 4, :].rearrange("p a f -> p (a f)"), out_offset=None, in_=cache_v4[:, :],
                in_offset=bass.IndirectOffsetOnAxis(ap=idx[:, col:col + 1], axis=0)), [ridx], [rvs[g]])
        pS, rpS = PA()
        pS4 = pS[:, :].rearrange("p (k a c) -> p k a c", k=16, a=4)
        for pg in range(16):
            for hh in range(4):
                mm(pS4[:, pg, hh, :], KpT[:, hh, pg * 128:(pg + 1) * 128], Qsb[:, hh, s, :], True, True, [rKpT, rQsb], [rpS])
        eS, reS = Et.next()
        eS4 = eS[:, :].rearrange("p (k a c) -> p k a c", k=16, a=4)
        act(eS[:, :], pS[:, :], AF.Exp, [rpS], [reS], scale=0.125)
        pN, rpN = PA()
        for hh in range(4):
            mm(pN[0:64, hh * 8:(hh + 1) * 8], KsT[:, hh, :], Qsb[:, hh, s, :], True, True, [rKsT, rQsb], [rpN])
        eN, reN = Et.next()
        eN32, reN32 = t32.next()
        act(eN32[0:64, 0:32], pN[0:64, 0:32], AF.Exp, [rpN], [reN32], scale=0.125)
        tt(eN[0:64, 0:32].rearrange("p (a c) -> p a c", a=4), eN32[0:64, 0:32].rearrange("p (a c) -> p a c", a=4),
           msk[:, s, :].unsqueeze(1).to_broadcast([64, 4, 8]), ALU.mult, [reN32, rmsk], [reN])
        pO, rpO = PB4()
        pSm, rpSm = PA()
        for hh in range(4):
            for pg in range(16):
                mm(pO[0:8, hh * 128:(hh + 1) * 128], eS4[:, pg, hh, :], vsb[:, pg, hh * 128:(hh + 1) * 128], pg == 0, False,
                   [reS, rvs[pg // 4]], [rpO])
                mm(pSm[0:8, hh:hh + 1], eS4[:, pg, hh, :], ones_bf[:, :], pg == 0, False, [reS, rones], [rpSm])
            mm(pO[0:8, hh * 128:(hh + 1) * 128], eN[0:64, hh * 8:(hh + 1) * 8], VAs[0:64, hh, :], False, True, [reN, rVs], [rpO])
            mm(pSm[0:8, hh:hh + 1], eN[0:64, hh * 8:(hh + 1) * 8], ones_bf[0:64, :], False, True, [reN, rones], [rpSm])
        sb_, rs_ = stat.next()
        B.op("dve", lambda h, s_=sb_, p_=pSm: h.reciprocal(out=s_[0:8, 0:4], in_=p_[0:8, 0:4]), [rpSm], [rs_])
        ts(sb_[0:8, 0:4], sb_[0:8, 0:4], lamc8[:, 0:1], None, ALU.mult, None, [rs_, rlamc8], [rs_])
        osc, rosc = t32.next()
        tt(osc[0:8, :].rearrange("p (a d) -> p a d", a=4), pO[0:8, :].rearrange("p (a d) -> p a d", a=4),
           sb_[0:8, 0:4].unsqueeze(2).to_broadcast([8, 4, 128]), ALU.mult, [rpO, rs_], [rosc])
        mm(att_all[0:64, :], cmbt[:, 60 - 4 * s:124 - 4 * s], osc[0:8, :], s == 0, s == 15, [rcmb, rosc], [ratt])
    mark("sample:tail")
    at, rat = t32.next()
    cp(at[0:64, :], att_all[0:64, :], [ratt], [rat])
    for hh in range(4):
        ah, rah = t32.next()
        cp(ah[0:64, 0:128], at[0:64, hh * 128:(hh + 1) * 128], [rat], [rah])
        sub_ln_to_moT(64, ah, rah, 1, hh)
    out_proj_residual(64, 1, gts_fn, rgtp)
    ffn_block(64, 1, 64, 1, 64, gts_fn, rgtp, True, True, lambda t: y_s[:, :])
    st(fc_s, fhist[:], [rfhist])

    return finish()


_NC_CACHE = {}


def _rope_table(pos):
    half = 8
    freqs = (np.float32(500000.0) ** (-np.arange(half, dtype=np.float32) * np.float32(2.0) / np.float32(16))).astype(np.float32)
    ang = pos.astype(np.float32)[:, None] * freqs[None, :]
    return np.concatenate([np.cos(ang), np.sin(ang)], axis=1).astype(np.float32)


def kernel(x_prompt, x_sample, cache_k, cache_v, page_table, state_lru_conv, state_lru_h, state_ffn_conv,
           c_prompt, c_sample, g_norm1, g_norm2, w_ada, b_ada, w_in, g_q, g_k, lam_q1, lam_k1, lam_q2, lam_k2,
           g_subln, w_out, conv_lru_w, conv_lru_b, w_rgate, b_rgate, w_igate, b_igate, lru_lambda,
           w_up, conv_ffn_w, conv_ffn_b, w_down):
    f32 = np.float32
    A = lambda a: np.ascontiguousarray(np.asarray(a))
    x_prompt = A(x_prompt); x_sample = A(x_sample)
    n_rows = int(np.prod(np.shape(cache_k)[:3]))
    if "nc" not in _NC_CACHE:
        _NC_CACHE["nc"] = build_program(n_rows, _NC_CACHE.get("stage", 99))
    nc = _NC_CACHE["nc"]

    ck = A(cache_k).reshape(-1, 512)
    cv = A(cache_v).reshape(-1, 512)
    featT = lambda v, n: A(np.asarray(v, f32).reshape(n, 128).T)
    shared = {
        "w_ada": A(w_ada[0]), "b_adaT": featT(b_ada[0], 48), "b_ada": A(b_ada[0].reshape(1, 6144)),
        "g1T": featT(g_norm1[0], 8), "g2T": featT(g_norm2[0], 8),
        "w_in": A(w_in[0]), "w_out": A(w_out[0]), "w_up": A(w_up[0]), "w_down": A(w_down[0]),
        "gq_rep": A(np.broadcast_to(np.tile(np.asarray(g_q[0]), 8)[None, :], (128, 512))),
        "gk_rep": A(np.broadcast_to(np.tile(np.asarray(g_k[0]), 8)[None, :], (128, 512))),
        "gsub_rep": A(np.broadcast_to(np.tile(np.asarray(g_subln[0]), 4)[None, :], (128, 512))),
        "lamv": A(np.concatenate([lam_q1[0], lam_k1[0], lam_q2[0], lam_k2[0]]).reshape(1, 256)),
        "clw": A(np.asarray(conv_lru_w[0]).reshape(4, 4, 128).transpose(2, 1, 0)),
        "clb": featT(conv_lru_b[0], 4),
        "w_rg": A(w_rgate[0]), "w_ig": A(w_igate[0]),
        "b_rg": featT(np.asarray(b_rgate[0]).reshape(-1), 4), "b_ig": featT(np.asarray(b_igate[0]).reshape(-1), 4),
        "lru_lam": featT(lru_lambda[0], 4),
        "cfw": A(np.asarray(conv_ffn_w[0]).reshape(3, 44, 128).transpose(2, 1, 0)),
        "cfb": featT(conv_ffn_b[0], 44),
        "cache_k": ck, "cache_v": cv,
    }
    smask = np.zeros((64, 16, 8), f32)
    cmb = np.zeros((8, 124), f32)
    for c in range(2):
        for q in range(4):
            cmb[c * 4 + q, 60 + q] = 1.0
    for s in range(16):
        for j in range(4):
            for c in range(2):
                for q in range(4):
                    if j <= q:
                        smask[s * 4 + j, s, c * 4 + q] = 1.0
    gsel = np.zeros((128, 5), f32)
    for p in range(128):
        gsel[p, p // 32] = 1.0
        gsel[p, 4] = float(p % 32)
    shared["gsel"] = gsel
    shared["smask"] = smask
    shared["cmb"] = cmb

    past_len = page_table.shape[1] * 128
    in_maps = []
    for core in range(8):
        b, h = core // 2, core % 2
        m = dict(shared)
        m["xp"] = A(x_prompt[b, 0:2048])
        m["xo"] = A(x_prompt[b, h * 2048:(h + 1) * 2048])
        m["xs"] = A(x_sample[core * 16:(core + 1) * 16].reshape(64, 1024))
        cvec = np.concatenate([np.asarray(c_prompt[b])[None, :], np.asarray(c_sample[core * 16:(core + 1) * 16])], axis=0)
        m["cT"] = A(cvec.reshape(17, 8, 128).transpose(2, 1, 0))
        rp = _rope_table(np.arange(0, 2048))
        ro = _rope_table(np.arange(h * 2048, (h + 1) * 2048))
        m["ropep"] = A(rp.reshape(16, 128, 16).transpose(1, 0, 2))
        m["ropeo"] = A(ro.reshape(16, 128, 16).transpose(1, 0, 2))
        m["ropes"] = A(np.tile(_rope_table(past_len + np.arange(4)), (16, 1)))
        fl = np.zeros((128, 2), f32)
        fl[:, 0] = 0.0 if h == 1 else -10000.0
        fl[:, 1] = 1.0 if h == 1 else 0.0
        m["flags"] = fl
        sl = slice(core * 16, (core + 1) * 16)
        m["slc"] = A(np.asarray(state_lru_conv[0][sl]).reshape(16, 3, 4, 128).transpose(3, 2, 0, 1))
        m["slh"] = A(np.asarray(state_lru_h[0][sl]).reshape(16, 4, 128).transpose(2, 1, 0))
        m["sfc"] = A(np.asarray(state_ffn_conv[0][sl]).reshape(16, 2, 44, 128).transpose(3, 2, 0, 1))
        m["ptab"] = A(np.asarray(page_table[sl], np.int32).reshape(1, 256))
        in_maps.append(m)

    res = run_bass_kernel_spmd(nc, in_maps, core_ids=list(range(8)))
    R = res.results

    y_p = np.zeros((4, 4096, 1024), f32); k_p = np.zeros((1, 4, 4096, 512), f32); v_p = np.zeros((1, 4, 4096, 512), f32)
    y_s = np.zeros((128, 4, 1024), f32); k_s = np.zeros((1, 128, 4, 512), f32); v_s = np.zeros((1, 128, 4, 512), f32)
    lc_p = np.zeros((1, 4, 3, 512), f32); lh_p = np.zeros((1, 4, 512), f32); fc_p = np.zeros((1, 4, 2, 5632), f32)
    lc_s = np.zeros((1, 128, 3, 512), f32); lh_s = np.zeros((1, 128, 512), f32); fc_s = np.zeros((1, 128, 2, 5632), f32)
    for core in range(8):
        b, h = core // 2, core % 2
        r = R[core]
        y_p[b, h * 2048:(h + 1) * 2048] = r["y_o"]
        k_p[0, b, h * 2048:(h + 1) * 2048] = r["k_o"]
        v_p[0, b, h * 2048:(h + 1) * 2048] = r["v_o"]
        sl = slice(core * 16, (core + 1) * 16)
        y_s[sl] = r["y_s"].reshape(16, 4, 1024)
        k_s[0, sl] = r["k_s"].reshape(16, 4, 512)
        v_s[0, sl] = r["v_s"].reshape(16, 4, 512)
        if h == 1:
            lc_p[0, b] = r["lc_p"].transpose(2, 1, 0).reshape(3, 512)
            lh_p[0, b] = r["lh_p"].T.reshape(512)
            fc_p[0, b] = r["fc_p"].transpose(2, 1, 0).reshape(2, 5632)
        lc_s[0, sl] = r["lc_s"].transpose(2, 3, 1, 0).reshape(16, 3, 512)
        lh_s[0, sl] = r["lh_s"].transpose(2, 1, 0).reshape(16, 512)
        fc_s[0, sl] = r["fc_s"].transpose(2, 3, 1, 0).reshape(16, 2, 5632)
    return (y_p, y_s.reshape(128, 4, 1024), k_p.reshape(1, 4, 4096, 4, 2, 64), v_p.reshape(1, 4, 4096, 4, 128),
            lc_p, lh_p, fc_p, k_s.reshape(1, 128, 4, 4, 2, 64), v_s.reshape(1, 128, 4, 4, 128), lc_s, lh_s, fc_s)
```

```python
import math, os
import numpy as np
from contextlib import ExitStack
import concourse.bass as bass
import concourse.mybir as mybir
from concourse.bass_utils import run_bass_kernel_spmd

F32 = mybir.dt.float32
BF16 = mybir.dt.bfloat16
I32 = mybir.dt.int32
ALU = mybir.AluOpType
AF = mybir.ActivationFunctionType
AX = mybir.AxisListType


class Region:
    __slots__ = ("name", "last_w", "readers", "wsem", "rsem", "wcnt", "rcnt", "excl")

    def __init__(self, name, excl=False):
        self.name = name
        self.excl = excl
        self.last_w = None
        self.readers = []
        self.wsem = None
        self.rsem = None
        self.wcnt = 0
        self.rcnt = 0


class Op:
    __slots__ = ("eng", "fn", "deps", "idx", "needed", "dma_tok", "name")


class Builder:
    ENGS = ("pe", "act", "dve", "pool", "sp")

    def __init__(self, nc, self_sync=True):
        self.nc = nc
        self.ops = {e: [] for e in self.ENGS}
        self.self_sync = self_sync
        self.dma_sems = {}
        self.final_toks = []
        self.es = ExitStack()
        self.nreg = 0

    def sb(self, name, shape, dt):
        return self.es.enter_context(self.nc.sbuf_tensor(name, list(shape), dt))

    def ps(self, name, shape, dt):
        return self.es.enter_context(self.nc.psum_tensor(name, list(shape), dt))

    def R(self, name=None, excl=False):
        self.nreg += 1
        return Region(name or f"r{self.nreg}", excl)

    def _deps(self, eng, reads, writes):
        deps = []
        for r in reads:
            if r.last_w is not None:
                deps.append(r.last_w)
            if r.excl:
                deps.extend(t for t in r.readers if t[0] == "c" and t[1] != eng)
        for w in writes:
            if w.last_w is not None:
                deps.append(w.last_w)
            deps.extend(w.readers)
        return deps

    def op(self, eng, fn, reads=(), writes=(), name=None):
        o = Op()
        o.eng = eng
        o.fn = fn
        o.deps = self._deps(eng, reads, writes)
        o.idx = len(self.ops[eng])
        o.needed = False
        o.dma_tok = None
        o.name = name
        self.ops[eng].append(o)
        tok = ("c", eng, o.idx)
        for w in writes:
            w.last_w = tok
            w.readers = []
        for r in reads:
            if all(r is not w for w in writes):
                r.readers.append(tok)
        return tok

    def dma(self, q, fn, reads=(), writes=(), final=False, name=None):
        o = Op()
        o.eng = q
        o.fn = fn
        o.deps = self._deps(q, reads, writes)
        o.idx = len(self.ops[q])
        o.needed = False
        o.name = name
        if writes:
            reg = writes[0]
            key = ("w", id(reg))
            reg.wcnt += 16
            cnt = reg.wcnt
        else:
            reg = reads[0]
            key = ("r", id(reg))
            reg.rcnt += 16
            cnt = reg.rcnt
        if key not in self.dma_sems:
            self.dma_sems[key] = len(self.dma_sems)
        tok = ("d", key, cnt)
        o.dma_tok = tok
        self.ops[q].append(o)
        for w in writes:
            w.last_w = tok
            w.readers = []
        for r in reads:
            r.readers.append(tok)
        if final:
            self.final_toks.append(tok)
        return tok

    def finalize(self):
        nc = self.nc
        for e in self.ENGS:
            for o in self.ops[e]:
                for d in o.deps:
                    if d[0] == "c":
                        if d[1] == e and o.dma_tok is None and not (self.self_sync and e != "pe"):
                            continue
                        self.ops[d[1]][d[2]].needed = True
        semval = {}
        for e in self.ENGS:
            c = 0
            for o in self.ops[e]:
                if o.needed:
                    c += 1
                    semval[(e, o.idx)] = c
        es = self.es
        esems = {e: es.enter_context(nc.semaphore(f"s_{e}")) for e in self.ENGS}
        dsems = {}
        for key, i in self.dma_sems.items():
            dsems[key] = es.enter_context(nc.semaphore(f"d_{i}"))
        handles = {}
        block = es.enter_context(nc.Block())
        ops = self.ops
        self_sync = self.self_sync
        final_toks = self.final_toks

        def emit(e, h):
            waited = {}
            for o in ops[e]:
                need = {}
                for d in o.deps:
                    if d[0] == "c":
                        if d[1] == e and o.dma_tok is None and not (self_sync and e != "pe"):
                            continue
                        s = ("c", d[1])
                        v = semval[(d[1], d[2])]
                    else:
                        s = ("d", d[1])
                        v = d[2]
                    if waited.get(s, 0) >= v:
                        continue
                    if need.get(s, 0) < v:
                        need[s] = v
                for s, v in need.items():
                    sem = esems[s[1]] if s[0] == "c" else dsems[s[1]]
                    h.wait_ge(sem, v)
                    waited[s] = v
                inst = o.fn(h)
                if o.dma_tok is not None:
                    inst.then_inc(dsems[o.dma_tok[1]], 16)
                elif o.needed:
                    inst.then_inc(esems[e], 1)
            if e == "sp":
                fin = {}
                for t in final_toks:
                    if fin.get(t[1], 0) < t[2]:
                        fin[t[1]] = t[2]
                for k, v in fin.items():
                    if waited.get(("d", k), 0) < v:
                        h.wait_ge(dsems[k], v)

        @block.tensor
        def _(h):
            emit("pe", h)

        @block.scalar
        def _(h):
            emit("act", h)

        @block.vector
        def _(h):
            emit("dve", h)

        @block.gpsimd
        def _(h):
            emit("pool", h)

        @block.sync
        def _(h):
            emit("sp", h)

    def close(self):
        self.es.close()


LAM_INIT = 0.8 - 0.6 * math.exp(-0.3 * 0)
EPS = 1e-6
NSLOT = 3
NT32 = 7


def build_program(n_rows=327680, stage=99):
    nc = bass.Bass("TRN2", target_bir_lowering=False)
    B = Builder(nc)

    def din(name, shape, dt=F32):
        return nc.dram_tensor(name, list(shape), dt, kind="ExternalInput").ap()

    def dout(name, shape, dt=F32):
        return nc.dram_tensor(name, list(shape), dt, kind="ExternalOutput").ap()

    xp = din("xp", [2048, 1024]); xo = din("xo", [2048, 1024]); xs = din("xs", [64, 1024])
    cT = din("cT", [128, 8, 17])
    ropep = din("ropep", [128, 16, 16]); ropeo = din("ropeo", [128, 16, 16]); ropes = din("ropes", [64, 16])
    flags = din("flags", [128, 2])
    w_ada = din("w_ada", [1024, 6144]); b_adaT = din("b_adaT", [128, 48]); b_ada = din("b_ada", [1, 6144])
    g1T = din("g1T", [128, 8]); g2T = din("g2T", [128, 8])
    w_in = din("w_in", [1024, 2560]); w_out = din("w_out", [1024, 1024])
    w_up = din("w_up", [1024, 5632]); w_down = din("w_down", [2816, 1024])
    gq_rep = din("gq_rep", [128, 512]); gk_rep = din("gk_rep", [128, 512]); gsub_rep = din("gsub_rep", [128, 512])
    lamv = din("lamv", [1, 256])
    clw = din("clw", [128, 4, 4]); clb = din("clb", [128, 4])
    w_rg = din("w_rg", [8, 64, 64]); w_ig = din("w_ig", [8, 64, 64])
    b_rg = din("b_rg", [128, 4]); b_ig = din("b_ig", [128, 4]); lru_lam = din("lru_lam", [128, 4])
    cfw = din("cfw", [128, 44, 3]); cfb = din("cfb", [128, 44])
    slc = din("slc", [128, 4, 16, 3]); slh = din("slh", [128, 4, 16]); sfc = din("sfc", [128, 44, 16, 2])
    ptab = din("ptab", [1, 256], I32)
    smask = din("smask", [64, 16, 8]); cmb = din("cmb", [8, 124]); gsel = din("gsel", [128, 5])
    cache_k = din("cache_k", [n_rows, 512]); cache_v = din("cache_v", [n_rows, 512])

    y_o = dout("y_o", [2048, 1024]); y_s = dout("y_s", [64, 1024])
    k_o = dout("k_o", [2048, 512]); v_o = dout("v_o", [2048, 512])
    k_s = dout("k_s", [64, 512]); v_s = dout("v_s", [64, 512])
    lc_p = dout("lc_p", [128, 4, 3]); lh_p = dout("lh_p", [128, 4]); fc_p = dout("fc_p", [128, 44, 2])
    lc_s = dout("lc_s", [128, 4, 16, 3]); lh_s = dout("lh_s", [128, 4, 16]); fc_s = dout("fc_s", [128, 44, 16, 2])

    def mm(out, lhsT, rhs, start, stop, reads, writes):
        return B.op("pe", lambda h: h.matmul(out, lhsT=lhsT, rhs=rhs, start=start, stop=stop), reads, writes)

    def tr(out, in_, ident, reads, writes):
        return B.op("pe", lambda h: h.transpose(out=out, in_=in_, identity=ident), reads, writes)

    def act(out, in_, func, reads, writes, **kw):
        return B.op("act", lambda h: h.activation(out=out, in_=in_, func=func, **kw), reads, writes)

    def tt(out, in0, in1, op, reads, writes, eng="dve"):
        return B.op(eng, lambda h: h.tensor_tensor(out=out, in0=in0, in1=in1, op=op), reads, writes)

    def ts(out, in0, s1, s2, op0, op1, reads, writes, eng="dve"):
        if s2 is None:
            return B.op(eng, lambda h: h.tensor_scalar(out=out, in0=in0, scalar1=s1, scalar2=None, op0=op0), reads, writes)
        return B.op(eng, lambda h: h.tensor_scalar(out=out, in0=in0, scalar1=s1, scalar2=s2, op0=op0, op1=op1), reads, writes)

    def stt(out, in0, scalar, in1, op0, op1, reads, writes, eng="dve"):
        return B.op(eng, lambda h: h.scalar_tensor_tensor(out=out, in0=in0, scalar=scalar, in1=in1, op0=op0, op1=op1), reads, writes)

    def cp(out, in_, reads, writes, eng="dve"):
        return B.op(eng, lambda h: h.tensor_copy(out, in_), reads, writes)

    def memset(ap, val, writes, eng="dve"):
        return B.op(eng, lambda h: h.memset(ap, val), (), writes)

    def ld(out, in_, writes, reads=(), q="sp"):
        return B.dma(q, lambda h: h.dma_start(out=out, in_=in_, allow_slow_non_contiguous=True), reads, writes)

    def st(out, in_, reads, q=None):
        q = q or os.environ.get("STQ", "sp")
        return B.dma(q, lambda h: h.dma_start(out=out, in_=in_, allow_slow_non_contiguous=True), reads, (), final=True)

    class Rot:
        def __init__(self, name, shape, dt, n):
            self.bufs = [(B.sb(f"{name}{i}", shape, dt), B.R(f"{name}{i}")) for i in range(n)]
            self.i = 0

        def next(self):
            b = self.bufs[self.i % len(self.bufs)]
            self.i += 1
            return b

    KT = B.sb("KT", [128, 4, 4096], BF16)
    rKT = [B.R(f"KT{i}") for i in range(32)]
    VA = B.sb("VA", [128, 32, 4, 130], BF16)
    rVA = [B.R(f"VA{i}") for i in range(32)]
    wslot = Rot("wslot", [128, 4096], BF16, NSLOT)
    gtp = B.sb("gtp", [128, 2, 1024], F32); rgtp = [B.R(), B.R()]
    ABc = B.sb("ABc", [128, 4, 8, 65], F32); rAB = B.R()
    xblk = B.sb("xblk", [128, 4, 1024], F32); rx = [B.R(f"x{i}") for i in range(4)]
    xnbf = B.sb("xnbf", [128, 1024], BF16); rxn = B.R()
    uT = B.sb("uT", [128, 8, 512], BF16); ruT = B.R()
    chunks = [(B.sb(f"ch{i}", [128, 512], BF16), B.R(f"ch{i}")) for i in range(22)]
    QT = B.sb("QT", [128, 4, 2, 512], BF16); rQT = B.R()
    QsTt = B.sb("QsTt", [128, 4, 64], BF16)
    qkbf = B.sb("qkbf", [128, 512], BF16); rqkbf = B.R()
    t32 = Rot("t32", [128, 512], F32, NT32)
    Et = Rot("Et", [128, 512], BF16, 3)
    attnbf = B.sb("attnbf", [128, 512], BF16); rattnbf = B.R()
    ropet = B.sb("ropet", [128, 33, 16], F32); rrope = B.R()
    small = B.sb("small", [128, 64], F32); rsmall = B.R()
    stat = Rot("stat", [128, 16], F32, 6)
    ident = B.sb("ident", [128, 128], BF16); rident = B.R()
    id32_t, rid32 = t32.next()
    id32 = id32_t[:, 0:128]
    tri = B.sb("tri", [128, 128], BF16); rtri = B.R()
    cols = B.sb("cols", [128, 40], F32); rcols = B.R()
    gq64 = B.sb("gq64", [128, 2, 64], F32); gsub = B.sb("gsub", [128, 128], F32); rgqk = B.R()
    lruc = B.sb("lruc", [128, 4, 12], F32); rlruc = B.R()
    Wg = B.sb("Wg", [128, 8, 128], BF16); rWg = B.R()
    hist = B.sb("hist", [128, 4, 16, 3], F32); rhist = B.R()
    carry = B.sb("carry", [128, 4, 16], F32); rcarry = B.R()
    fcw = B.sb("fcw", [128, 44, 4], F32); rfcw = B.R()
    fhist = B.sb("fhist", [128, 44, 16, 2], F32); rfhist = B.R()
    scx = B.sb("scx", [128, 8, 65], BF16); rscx = B.R()
    btile = B.sb("btile", [128, 44, 2], F32); rbt = B.R()
    def screp(kc):
        return chunks[kc // 4][0][:, (kc % 4) * 128:(kc % 4 + 1) * 128], chunks[kc // 4][1]
    ropetile_zero = None

    psB = B.ps("psB", [128, 5, 512], F32); rB = [B.R(f"pB{i}", excl=True) for i in range(5)]
    psA = [(B.ps(f"psA{i}", [128, 512], F32), B.R(f"pA{i}", excl=True)) for i in range(2)]
    psT = B.ps("psT", [128, 1024], BF16); rT = B.R("pT", excl=True)
    pa_i = [0]

    def PA():
        b = psA[pa_i[0] % 2]
        pa_i[0] += 1
        return b

    pb_i = [0]

    def PB():
        k = pb_i[0] % 5
        pb_i[0] += 1
        return psB[:, k, :], rB[k]

    C_EPS, C_PBIAS, C_PMUL, C_ZERO, C_NLAM, C_ONE = 0, 1, 2, 3, 4, 5

    memset(id32[:], 1.0, [rid32], eng="pool")
    B.op("pool", lambda h: h.affine_select(out=id32[:], in_=id32[:], pattern=[[-1, 128]], compare_op=ALU.is_equal,
                                           fill=0.0, base=0, channel_multiplier=1), [rid32], [rid32])
    cp(ident[:], id32[:], [rid32], [rident])
    memset(id32[:], 1.0, [rid32], eng="pool")
    B.op("pool", lambda h: h.affine_select(out=id32[:], in_=id32[:], pattern=[[1, 128]], compare_op=ALU.is_ge,
                                           fill=0.0, base=0, channel_multiplier=-1), [rid32], [rid32])
    cp(tri[:], id32[:], [rid32], [rtri])
    memset(cols[:], 0.0, [rcols])
    memset(cols[:, C_EPS:C_EPS + 1], EPS, [rcols])
    memset(cols[:, C_ONE:C_ONE + 1], 1.0, [rcols])
    ld(cols[:, C_PBIAS:C_PBIAS + 2], flags, [rcols])
    ld(gq64[:, 0, :], gq_rep[:, 0:64], [rgqk]); ld(gq64[:, 1, :], gk_rep[:, 0:64], [rgqk]); ld(gsub[:, :], gsub_rep[:, 0:128], [rgqk])
    ts(gsub[:, :], gsub[:, :], 1.0 - LAM_INIT, None, ALU.mult, None, [rgqk], [rgqk])
    ld(ropet[:, 0:16, :], ropep, [rrope]); ld(ropet[:, 16:32, :], ropeo, [rrope]); ld(ropet[0:64, 32, :], ropes, [rrope])
    memset(VA[:, :, :, 128:130], 1.0, rVA)
    memset(QT[:, :, :, :], 0.0, [rQT])
    epsc = cols[:, C_EPS:C_EPS + 1]
    zeroc = cols[:, C_ZERO:C_ZERO + 1]
    pbiasc = cols[:, C_PBIAS:C_PBIAS + 1]
    pmulc = cols[:, C_PMUL:C_PMUL + 1]

    lt, rlt = t32.next()
    ld(lt[:, 0:256], lamv.partition_broadcast(128), [rlt])
    tt(lt[:, 256:320], lt[:, 0:64], lt[:, 64:128], ALU.mult, [rlt], [rlt])
    tt(lt[:, 320:384], lt[:, 128:192], lt[:, 192:256], ALU.mult, [rlt], [rlt])
    B.op("dve", lambda h: h.tensor_reduce(out=small[:, 0:2], in_=lt[:, 256:384].rearrange("p (a d) -> p a d", d=64),
                                          axis=AX.X, op=ALU.add), [rlt], [rsmall])
    act(small[:, 2:4], small[:, 0:2], AF.Exp, [rsmall], [rsmall])
    tt(small[:, 4:5], small[:, 3:4], small[:, 2:3], ALU.subtract, [rsmall], [rsmall])
    ts(cols[:, C_NLAM:C_NLAM + 1], small[:, 4:5], -LAM_INIT, None, ALU.add, None, [rsmall], [rcols])

    ld(lruc[:, :, 0:4], clw, [rlruc]); ld(lruc[:, :, 4], clb, [rlruc]); ld(lruc[:, :, 5], b_rg, [rlruc])
    ld(lruc[:, :, 6], b_ig, [rlruc]); ld(lruc[:, :, 8], lru_lam, [rlruc])
    act(lruc[:, :, 9], lruc[:, :, 8], AF.Exp, [rlruc], [rlruc], scale=-1.0)
    act(lruc[:, :, 10], lruc[:, :, 9], AF.Ln, [rlruc], [rlruc], bias=cols[:, C_ONE:C_ONE + 1])
    ts(lruc[:, :, 7], lruc[:, :, 10], -8.0, None, ALU.mult, None, [rlruc], [rlruc])
    ld(fcw[:, :, 0:3], cfw, [rfcw]); ld(fcw[:, :, 3], cfb, [rfcw])
    memset(Wg[:], 0.0, [rWg])
    for n in range(8):
        ch, o = n // 2, (n % 2) * 64
        ld(Wg[o:o + 64, ch, o:o + 64], w_rg[n], [rWg], q="pool")
        ld(Wg[o:o + 64, 4 + ch, o:o + 64], w_ig[n], [rWg], q="pool")
    memset(hist[:], 0.0, [rhist]); memset(carry[:], 0.0, [rcarry]); memset(fhist[:], 0.0, [rfhist])

    def load_slot(src2d, nk, ncol):
        s, r = wslot.next()
        v = s[:, 0:nk * ncol].rearrange("p (a b) -> p a b", a=nk)
        B.dma("pool", lambda h: h.dma_start(out=v, in_=src2d.rearrange("(a p) n -> p a n", p=128)), (), [r])
        return v, r

    ct, rct = t32.next()
    ctv = ct[:, 0:136].rearrange("p (a b) -> p a b", a=8)
    ld(ctv, cT, [rct])
    ct2, rct2 = t32.next()
    ct2v = ct2[:, 0:136].rearrange("p (a b) -> p a b", a=8)
    act(ct2v, ctv, AF.Silu, [rct], [rct2])
    cp(scx[:, :, 0:1], ct2v[:, :, 0:1], [rct2], [rscx])
    for kc in range(8):
        cp(scx[:, kc, 1:65].rearrange("p (s t) -> p s t", t=4), ct2v[:, kc, 1:17].unsqueeze(2).to_broadcast([128, 16, 4]),
           [rct2], [rscx])
        cp(screp(kc)[0], ct2v[:, kc, 0:1].to_broadcast([128, 128]), [rct2], [screp(kc)[1]])
    adac = B.sb("adac", [128, 64], F32)
    badT, rbad = adac[:, 0:48], B.R()
    ld(badT[:, 0:48], b_adaT, [rbad])
    gT, rgT = adac[:, 48:64], B.R()
    ld(gT[:, 0:8], g1T, [rgT]); ld(gT[:, 8:16], g2T, [rgT])

    def ada_feat(slot_v, rslot, g, dst_idx, is_scale):
        for j in range(4):
            fch = g * 4 + j
            p, rp = PA()
            for kc in range(8):
                mm(p[:, 0:65], slot_v[:, kc, j * 128:(j + 1) * 128], scx[:, kc, :], kc == 0, kc == 7, [rslot, rscx], [rp])
            kk = fch % 8
            if is_scale:
                ts(ABc[:, dst_idx, kk, :], p[:, 0:65], badT[:, fch:fch + 1], 1.0, ALU.add, ALU.add, [rp, rbad, rAB], [rAB])
                gi = kk if dst_idx == 0 else 8 + kk
                ts(ABc[:, dst_idx, kk, :], ABc[:, dst_idx, kk, :], gT[:, gi:gi + 1], None, ALU.mult, None, [rAB, rgT], [rAB])
            else:
                ts(ABc[:, dst_idx, kk, :], p[:, 0:65], badT[:, fch:fch + 1], None, ALU.add, None, [rp, rbad, rAB], [rAB])

    def ada_tok(slot_v, rslot, g, which, half, sample):
        c0 = g * 512
        P = 64 if sample else 128
        dst = gtp[0:P, which, half * 512:(half + 1) * 512]
        ld(dst, b_ada[0:1, c0:c0 + 512].partition_broadcast(P), [rgtp[which]])
        p, rp = PA()
        for kc in range(8):
            if sample:
                mm(p[0:64, :], scx[:, kc, 1:65], slot_v[:, kc, :], kc == 0, kc == 7, [rslot, rscx], [rp])
            else:
                mm(p[:, :], screp(kc)[0], slot_v[:, kc, :], kc == 0, kc == 7, [rslot, screp(kc)[1]], [rp])
        tt(dst, dst, p[0:P, :], ALU.add, [rp, rgtp[which]], [rgtp[which]])

    def ada_group(g):
        sv, rs = load_slot(w_ada[:, g * 512:(g + 1) * 512], 8, 512)
        sec = g // 2
        if sec == 0:
            ada_feat(sv, rs, g, 1, False)
        elif sec == 1:
            ada_feat(sv, rs, g, 0, True)
        elif sec == 2:
            ada_tok(sv, rs, g, 0, g % 2, False)
        elif sec == 3:
            ada_feat(sv, rs, g, 3, False)
        elif sec == 4:
            ada_feat(sv, rs, g, 2, True)
        else:
            ada_tok(sv, rs, g, 1, g % 2, False)

    for g in range(4):
        ada_group(g)
    ADA_LATER = {0: (4, 5), 1: (6, 7, 8, 9), 2: (10, 11)}

    marks = []

    def mark(label):
        marks.append((label, len(B.ops["pe"]), len(B.ops["act"]), len(B.ops["dve"])))

    def finish():
        mark("end")
        if os.environ.get("KMARKS"):
            import json
            json.dump(marks, open(os.environ["KMARKS"], "w"))
        B.finalize()
        B.close()
        import sys
        print("ops:", {e: len(B.ops[e]) for e in B.ENGS}, "needed:", {e: sum(o.needed for o in B.ops[e]) for e in B.ENGS},
              "dma sems:", len(B.dma_sems), file=sys.stderr)
        return nc

    if stage == 0:
        for g in range(4, 12):
            ada_group(g)
        st(y_s[:, :], gtp[0:64, 0, :], [rgtp[0]])
        st(y_o[0:128, :], gtp[:, 1, :], [rgtp[1]])
        st(k_s[:, 0:260].rearrange("p (a b) -> p a b", a=4), ABc[0:64, :, 0, :], [rAB])
        return finish()

    def rstd_from_ss(out_ap, ss_ap, n, reads, writes):
        act(out_ap, ss_ap, AF.Ln, reads, writes, scale=1.0 / n, bias=epsc[0:ss_ap.shape[0], :])
        act(out_ap, out_ap, AF.Exp, writes, writes, scale=-0.5)

    def norm_transpose(P, xt, rxt, t, abi, col0, ncol):
        sb_, rs_ = stat.next()
        act(xnbf[0:P, :], xt, AF.Square, [rxt], [rxn, rs_], accum_out=sb_[0:P, 0:1])
        rstd_from_ss(sb_[0:P, 1:2], sb_[0:P, 0:1], 1024, [rs_, rcols], [rs_])
        ts(xnbf[0:P, :], xt, sb_[0:P, 1:2], None, ALU.mult, None, [rxt, rs_], [rxn])
        for kc in range(8):
            tr(psT[:, kc * 128:kc * 128 + P], xnbf[0:P, kc * 128:(kc + 1) * 128], ident[0:P, 0:P], [rxn, rident], [rT])
        pv = psT[:, :].rearrange("p (a b) -> p a b", a=8)[:, :, 0:P]
        if ncol == 1:
            A = ABc[:, abi, :, col0:col0 + 1].to_broadcast([128, 8, P])
            Bb = ABc[:, abi + 1, :, col0:col0 + 1].to_broadcast([128, 8, P])
        else:
            A = ABc[:, abi, :, col0:col0 + P]
            Bb = ABc[:, abi + 1, :, col0:col0 + P]
        for hf in range(2):
            tmp, rtmp = t32.next()
            tv = tmp[:, 0:4 * P].rearrange("p (a b) -> p a b", a=4)
            tt(tv, pv[:, hf * 4:(hf + 1) * 4, :], A[:, hf * 4:(hf + 1) * 4, :], ALU.mult, [rT, rAB], [rtmp])
            tt(uT[:, hf * 4:(hf + 1) * 4, t * 128:t * 128 + P], tv, Bb[:, hf * 4:(hf + 1) * 4, :], ALU.add, [rtmp, rAB], [ruT])

    def qk_post_gen(P, ps_ap, rps, gi, rope_t, k_dst, q_dst, ktile, t, dst_reg=None):
        sq, rsq = t32.next()
        qn, rqn = t32.next()
        sb_, rs_ = stat.next()
        act(sq[0:P, :], ps_ap, AF.Square, [rps], [rsq])
        yield
        B.op("dve", lambda h: h.tensor_reduce(out=sb_[0:P, 0:8], in_=sq[0:P, :].rearrange("p (g d) -> p g d", d=64),
                                              axis=AX.X, op=ALU.add), [rsq], [rs_])
        yield
        act(sb_[0:P, 8:16], sb_[0:P, 0:8], AF.Ln, [rs_, rcols], [rs_], scale=1.0 / 64, bias=epsc[0:P, :])
        yield
        act(sb_[0:P, 8:16], sb_[0:P, 8:16], AF.Exp, [rs_], [rs_], scale=-0.5)
        yield
        qv = qn[0:P, :].rearrange("p (g d) -> p g d", d=64)
        tt(qv, ps_ap.rearrange("p (g d) -> p g d", d=64), sb_[0:P, 8:16].unsqueeze(2).to_broadcast([P, 8, 64]), ALU.mult,
           [rps, rs_], [rqn])
        yield
        tt(qv, qv, gq64[0:P, gi, :].unsqueeze(1).to_broadcast([P, 8, 64]), ALU.mult, [rqn, rgqk], [rqn])
        yield
        cosb = ropet[0:P, rope_t, 0:8].unsqueeze(1).to_broadcast([P, 8, 8])
        sinb = ropet[0:P, rope_t, 8:16].unsqueeze(1).to_broadcast([P, 8, 8])
        tv = sq[0:P, 0:256].rearrange("p (a g d) -> p a g d", a=4, d=8)
        r1, r2 = qv[:, :, 0:8], qv[:, :, 8:16]
        tt(tv[:, 0], r1, cosb, ALU.mult, [rqn, rrope], [rsq])
        yield
        tt(tv[:, 1], r2, sinb, ALU.mult, [rqn, rrope], [rsq])
        yield
        tt(tv[:, 2], r1, sinb, ALU.mult, [rqn, rrope], [rsq])
        yield
        tt(tv[:, 3], r2, cosb, ALU.mult, [rqn, rrope], [rsq])
        yield
        tt(r1, tv[:, 0], tv[:, 1], ALU.subtract, [rsq], [rqn])
        yield
        tt(r2, tv[:, 2], tv[:, 3], ALU.add, [rsq], [rqn])
        yield
        if k_dst is not None:
            st(k_dst, qn[0:P, :], [rqn])
        cp(qkbf[0:P, :], qn[0:P, :], [rqn], [rqkbf])
        for hh in range(4):
            tr(psT[:, hh * 128:hh * 128 + P], qkbf[0:P, hh * 128:(hh + 1) * 128], ident[0:P, 0:P], [rqkbf, rident], [rT])
        pv = psT[:, 0:512].rearrange("p (a b) -> p a b", a=4)[:, :, 0:P]
        if q_dst == "blk":
            for c in range(2):
                cp(QT[c * 64:(c + 1) * 64, :, c, t * 128:t * 128 + P], pv[c * 64:(c + 1) * 64, :, :], [rT], [rQT])
        elif q_dst is not None:
            cp(q_dst, pv, [rT], [dst_reg if dst_reg is not None else rQT])
        else:
            cp(KT[:, :, ktile * 128:ktile * 128 + P], pv, [rT], [rKT[ktile]])

    def lockstep(gens):
        gens = list(gens)
        while gens:
            nxt = []
            for g in gens:
                try:
                    next(g)
                    nxt.append(g)
                except StopIteration:
                    pass
            gens = nxt

    def qk_post(*a, **k):
        lockstep([qk_post_gen(*a, **k)])

    def lru_chunk_gen(ch, T, slots, rslots, do_out, sample, moT_base):
        NS, TT = (16, 4) if sample else (1, T)
        lxv, rlx = slots[0], rslots[0]
        p, rp = PB()
        for kc in range(8):
            mm(p[:, 0:T], lxv[:, kc, ch * 128:(ch + 1) * 128], uT[:, kc, 0:T], kc == 0, kc == 7, [rlx, ruT], [rp])
        p3 = p[:, 0:T].rearrange("p (s t) -> p s t", s=NS)
        xc, rxc = t32.next()
        rg, rrg = t32.next()
        ig, rig = t32.next()
        x3 = xc[:, 0:T].rearrange("p (s t) -> p s t", s=NS)
        yield
        act(xc[:, 0:T], p[:, 0:T], AF.Identity, [rp, rlruc], [rxc], scale=lruc[:, ch, 3:4], bias=lruc[:, ch, 4:5])
        yield
        for k, sh in ((2, 1), (1, 2), (0, 3)):
            stt(x3[:, :, sh:TT], p3[:, :, 0:TT - sh], lruc[:, ch, k:k + 1], x3[:, :, sh:TT], ALU.mult, ALU.add,
                [rp, rlruc, rxc], [rxc])
            yield
        hv = hist[:, ch, 0:NS, :]
        stt(x3[:, :, 0:3], hv[:, :, 0:3], lruc[:, ch, 0:1], x3[:, :, 0:3], ALU.mult, ALU.add, [rhist, rlruc, rxc], [rxc])
        stt(x3[:, :, 0:2], hv[:, :, 1:3], lruc[:, ch, 1:2], x3[:, :, 0:2], ALU.mult, ALU.add, [rhist, rlruc, rxc], [rxc])
        stt(x3[:, :, 0:1], hv[:, :, 2:3], lruc[:, ch, 2:3], x3[:, :, 0:1], ALU.mult, ALU.add, [rhist, rlruc, rxc], [rxc])
        if sample:
            act(hv, p3[:, :, 1:4], AF.Identity, [rp], [rhist])
        else:
            act(hv, p3[:, :, TT - 3:TT], AF.Identity, [rp], [rhist])
        yield
        xb, rxb = Et.next()
        cp(xb[:, 0:T], xc[:, 0:T], [rxc], [rxb])
        yield
        pr, rpr = PB()
        mm(pr[:, 0:T], Wg[:, ch, :], xb[:, 0:T], True, True, [rWg, rxb], [rpr])
        act(rg[:, 0:T], pr[:, 0:T], AF.Sigmoid, [rpr, rlruc], [rrg], bias=lruc[:, ch, 5:6])
        yield
        pi, rpi = PB()
        mm(pi[:, 0:T], Wg[:, 4 + ch, :], xb[:, 0:T], True, True, [rWg, rxb], [rpi])
        act(ig[:, 0:T], pi[:, 0:T], AF.Sigmoid, [rpi, rlruc], [rig], bias=lruc[:, ch, 6:7])
        yield
        tt(ig[:, 0:T], ig[:, 0:T], xc[:, 0:T], ALU.mult, [rig, rxc], [rig])
        yield
        act(rg[:, 0:T], rg[:, 0:T], AF.Exp, [rrg, rlruc], [rrg], scale=lruc[:, ch, 7:8])
        yield
        tt(xc[:, 0:T], rg[:, 0:T], rg[:, 0:T], ALU.mult, [rrg], [rxc])
        yield
        act(xc[:, 0:T], xc[:, 0:T], AF.Ln, [rxc, rcols], [rxc], scale=-1.0, bias=cols[:, C_ONE:C_ONE + 1])
        yield
        act(xc[:, 0:T], xc[:, 0:T], AF.Exp, [rxc], [rxc], scale=0.5)
        yield
        tt(ig[:, 0:T], ig[:, 0:T], xc[:, 0:T], ALU.mult, [rig, rxc], [rig])
        yield
        a3 = rg[:, 0:T].rearrange("p (s t) -> p s t", s=NS)
        b3 = ig[:, 0:T].rearrange("p (s t) -> p s t", s=NS)
        if sample:
            h0 = carry[:, ch, :].unsqueeze(2)
            tmpc, rtmpc = stat.next()
            tt(tmpc[:, 0:16].unsqueeze(2), a3[:, :, 0:1], h0, ALU.mult, [rrg, rcarry], [rtmpc])
            tt(b3[:, :, 0:1], b3[:, :, 0:1], tmpc[:, 0:16].unsqueeze(2), ALU.add, [rig, rtmpc], [rig])
            memset(a3[:, :, 0:1], 0.0, [rrg])
            B.op("dve", lambda h, o=xc, a=rg, b=ig: h.tensor_tensor_scan(out=o[:, 0:T], data0=a[:, 0:T], data1=b[:, 0:T],
                                                                          initial=0.0, op0=ALU.mult, op1=ALU.add),
                 [rrg, rig], [rxc])
            cp(carry[:, ch, :].unsqueeze(2), x3[:, :, 3:4], [rxc], [rcarry])
        else:
            B.op("dve", lambda h, o=xc, a=rg, b=ig, c=ch: h.tensor_tensor_scan(out=o[:, 0:T], data0=a[:, 0:T], data1=b[:, 0:T],
                                                                                initial=carry[:, c, 0:1], op0=ALU.mult, op1=ALU.add),
                 [rrg, rig, rcarry], [rxc])
            cp(carry[:, ch, 0:1], xc[:, T - 1:T], [rxc], [rcarry])
        yield
        if do_out:
            lgv, rlg = slots[1], rslots[1]
            p2, rp2 = PB()
            for kc in range(8):
                mm(p2[:, 0:T], lgv[:, kc, ch * 128:(ch + 1) * 128], uT[:, kc, 0:T], kc == 0, kc == 7, [rlg, ruT], [rp2])
            gl, rgl = rg, rrg
            act(gl[:, 0:T], p2[:, 0:T], AF.Gelu_apprx_tanh, [rp2], [rgl])
            yield
            mo, rmo = chunks[moT_base + ch]
            tt(mo[:, 0:T], xc[:, 0:T], gl[:, 0:T], ALU.mult, [rxc, rgl], [rmo])

    def lru_chunks(T, nseq, slots, rslots, do_out, sample, moT_base):
        for pair in ((0, 1), (2, 3)):
            lockstep([lru_chunk_gen(ch, T, slots, rslots, do_out, sample, moT_base) for ch in pair])

    def sub_ln_to_moT(P, at, rat, nq, head, defer=False):
        sq, rsq = t32.next()
        act(sq[0:P, 0:nq * 128], at[0:P, 0:nq * 128], AF.Square, [rat], [rsq])
        sb_, rs_ = stat.next()
        B.op("dve", lambda h: h.tensor_reduce(out=sb_[0:P, 0:nq], in_=sq[0:P, 0:nq * 128].rearrange("p (g d) -> p g d", d=128),
                                              axis=AX.X, op=ALU.add), [rsq], [rs_])
        rstd_from_ss(sb_[0:P, 8:8 + nq], sb_[0:P, 0:nq], 128, [rs_, rcols], [rs_])
        a3 = at[0:P, 0:nq * 128].rearrange("p (g d) -> p g d", d=128)
        tt(a3, a3, sb_[0:P, 8:8 + nq].unsqueeze(2).to_broadcast([P, nq, 128]), ALU.mult, [rat, rs_], [rat])
        tt(attnbf[0:P, 0:nq * 128].rearrange("p (g d) -> p g d", d=128), a3,
           gsub[0:P, :].unsqueeze(1).to_broadcast([P, nq, 128]), ALU.mult, [rat, rgqk], [rattnbf])
        if defer:
            return
        sub_ln_tail(P, nq, head)

    def sub_ln_tail(P, nq, head):
        for qs in range(nq):
            tr(psT[:, qs * 128:qs * 128 + P], attnbf[0:P, qs * 128:(qs + 1) * 128], ident[0:P, 0:P], [rattnbf, rident], [rT])
        mo, rmo = chunks[head]
        if nq == 4:
            cp(mo[:, 0:512], psT[:, 0:512], [rT], [rmo])
        else:
            cp(mo[:, 0:P], psT[:, 0:P], [rT], [rmo])

    def attn_prompt(full_tiles, diag_tiles):
        Ov = [psB[:, 2 * c:2 * c + 2, :].rearrange("p a (s w) -> p (a s) w", w=256) for c in range(2)]
        rO = [[rB[0], rB[1]], [rB[2], rB[3]]]
        sbanks = [psA[0], psA[1], (psB[:, 4, :], rB[4])]
        LOOK = 2
        for hh in range(4):
            seq = [(kt, bc, None) for kt, bc in full_tiles] + [(kt, zeroc, j) for j, kt in enumerate(diag_tiles)]
            steps = [(n, kt, bc, dj, c) for n, (kt, bc, dj) in enumerate(seq) for c in range(2)]
            inflight = {}

            def s_issue(i):
                n, kt, bc, dj, c = steps[i]
                q0 = 0 if dj is None else dj * 128
                p, rp = sbanks[i % 3]
                mm(p[:, q0:512], KT[:, hh, kt * 128:(kt + 1) * 128], QT[:, hh, c, q0:512],
                   True, True, [rKT[kt], rQT], [rp])
                e, re = Et.next()
                act(e[:, q0:512], p[:, q0:512], AF.Exp, [rp, rcols], [re], scale=0.125, bias=bc)
                if dj is not None:
                    tt(e[:, q0:q0 + 128], e[:, q0:q0 + 128], tri[:, :], ALU.mult, [re, rtri], [re])
                inflight[i] = (e, re)

            def pv_issue(i):
                n, kt, bc, dj, c = steps[i]
                q0 = 0 if dj is None else dj * 128
                e, re = inflight.pop(i)
                for qs in range(q0 // 128, 4):
                    last = (dj is not None and qs == dj and qs % 2 == 1)
                    mm(Ov[c][:, qs, 0:129], e[:, qs * 128:(qs + 1) * 128], VA[:, kt, hh, 0:129], (n == 0 and qs % 2 == 0), last,
                       [re, rVA[kt]], [rO[c][qs // 2]])

            for i in range(len(steps) + LOOK):
                if i < len(steps):
                    s_issue(i)
                if i - LOOK >= 0:
                    pv_issue(i - LOOK)
            if hh > 0:
                sub_ln_tail(128, 4, hh - 1)
            sb_, rs_ = stat.next()
            B.op("dve", lambda h, s=sb_: h.reciprocal(out=s[:, 0:4].unsqueeze(2), in_=Ov[0][:, :, 128:129]), rO[0], [rs_])
            B.op("dve", lambda h, s=sb_: h.reciprocal(out=s[:, 4:8].unsqueeze(2), in_=Ov[1][:, :, 128:129]), rO[1], [rs_])
            ts(sb_[:, 4:8], sb_[:, 4:8], cols[:, C_NLAM:C_NLAM + 1], None, ALU.mult, None, [rs_, rcols], [rs_])
            at, rat = t32.next()
            a3 = at[:, :].rearrange("p (g d) -> p g d", d=128)
            tt(a3, Ov[0][:, :, 0:128], sb_[:, 0:4].unsqueeze(2).to_broadcast([128, 4, 128]), ALU.mult, rO[0] + [rs_], [rat])
            t2, rt2 = t32.next()
            b3 = t2[:, :].rearrange("p (g d) -> p g d", d=128)
            tt(b3, Ov[1][:, :, 0:128], sb_[:, 4:8].unsqueeze(2).to_broadcast([128, 4, 128]), ALU.mult, rO[1] + [rs_], [rt2])
            tt(at[:, :], at[:, :], t2[:, :], ALU.add, [rat, rt2], [rat])
            sub_ln_to_moT(128, at, rat, 4, hh, defer=True)
        sub_ln_tail(128, 4, 3)

    def out_proj_residual(P, ntile, gt_ap_fn, rgt):
        for nh in range(2):
            sv, rs = load_slot(w_out[:, nh * 512:(nh + 1) * 512], 8, 512)
            for t in range(ntile):
                p, rp = PB()
                for fc in range(8):
                    mo, rmo = chunks[fc]
                    mm(p[0:P, :], mo[:, t * 128:t * 128 + P], sv[:, fc, :], fc == 0, fc == 7, [rmo, rs], [rp])
                tmp, rtmp = t32.next()
                tt(tmp[0:P, :], p[0:P, :], gt_ap_fn(0, nh), ALU.mult, [rp, rgt[0]], [rtmp])
                xs_ = xblk[0:P, t, nh * 512:(nh + 1) * 512]
                tt(xs_, xs_, tmp[0:P, :], ALU.add, [rx[t], rtmp], [rx[t]])

    def ffn_block(P, ntile, T, abcol0, abncol, gt_ap_fn, rgt, sample, full, y_dst_fn, only_last_tile=False):
        NS, TT = (16, 4) if sample else (1, T)
        if only_last_tile:
            norm_transpose(128, xblk[:, ntile - 1, :], rx[ntile - 1], 0, 2, abcol0, abncol)
            T = 128; TT = 128
        else:
            for t in range(ntile):
                norm_transpose(P, xblk[0:P, t, :], rx[t], t, 2, abcol0, abncol)
        segs = [(0, 4), (4, 4), (8, 4), (12, 4), (16, 4), (20, 2)]
        if not sample:
            tt(btile[:, :, 0], fcw[:, :, 0], fhist[:, :, 0, 0], ALU.mult, [rfcw, rfhist], [rbt])
            tt(small[:, 0:44], fcw[:, :, 1], fhist[:, :, 0, 1], ALU.mult, [rfcw, rfhist], [rsmall])
            tt(btile[:, :, 0], btile[:, :, 0], small[:, 0:44], ALU.add, [rbt, rsmall], [rbt])
            tt(btile[:, :, 1], fcw[:, :, 0], fhist[:, :, 0, 1], ALU.mult, [rfcw, rfhist], [rbt])
        for s0, n in segs:
            sg, rsg = load_slot(w_up[:, s0 * 128:(s0 + n) * 128], 8, n * 128)
            sv_, rsv = load_slot(w_up[:, 2816 + s0 * 128:2816 + (s0 + n) * 128], 8, n * 128)
            for j in range(n):
                c = s0 + j
                hcs = []
                for which, (sl, rsl, fidx) in enumerate(((sg, rsg, c), (sv_, rsv, 22 + c))):
                    p, rp = PB()
                    if only_last_tile:
                        for kc in range(8):
                            mm(p[:, 0:2], sl[:, kc, j * 128:(j + 1) * 128], uT[:, kc, 126:128], kc == 0, kc == 7, [rsl, ruT], [rp])
                        act(fhist[:, fidx, 0:1, :], p[:, 0:2].unsqueeze(1), AF.Identity, [rp], [rfhist])
                        continue
                    for kc in range(8):
                        mm(p[:, 0:T], sl[:, kc, j * 128:(j + 1) * 128], uT[:, kc, 0:T], kc == 0, kc == 7, [rsl, ruT], [rp])
                    p3 = p[:, 0:T].rearrange("p (s t) -> p s t", s=NS)
                    hc, rhc = t32.next()
                    h3 = hc[:, 0:T].rearrange("p (s t) -> p s t", s=NS)
                    act(hc[:, 0:T], p[:, 0:T], AF.Identity, [rp, rfcw], [rhc], scale=fcw[:, fidx, 2:3], bias=fcw[:, fidx, 3:4])
                    stt(h3[:, :, 1:TT], p3[:, :, 0:TT - 1], fcw[:, fidx, 1:2], h3[:, :, 1:TT], ALU.mult, ALU.add, [rp, rfcw, rhc], [rhc])
                    stt(h3[:, :, 2:TT], p3[:, :, 0:TT - 2], fcw[:, fidx, 0:1], h3[:, :, 2:TT], ALU.mult, ALU.add, [rp, rfcw, rhc], [rhc])
                    fh = fhist[:, fidx, 0:NS, :]
                    if sample:
                        stt(h3[:, :, 0:2], fh[:, :, 0:2], fcw[:, fidx, 0:1], h3[:, :, 0:2], ALU.mult, ALU.add, [rfhist, rfcw, rhc], [rhc])
                        stt(h3[:, :, 0:1], fh[:, :, 1:2], fcw[:, fidx, 1:2], h3[:, :, 0:1], ALU.mult, ALU.add, [rfhist, rfcw, rhc], [rhc])
                    else:
                        tt(hc[:, 0:2], hc[:, 0:2], btile[:, fidx, :], ALU.add, [rhc, rbt], [rhc])
                    act(fh, p3[:, :, TT - 2:TT], AF.Identity, [rp], [rfhist])
                    hcs.append((hc, rhc))
                if full:
                    (hg, rhg), (hv, rhv) = hcs
                    act(hg[:, 0:T], hg[:, 0:T], AF.Silu, [rhg], [rhg])
                    a_, ra_ = chunks[c]
                    tt(a_[:, 0:T], hg[:, 0:T], hv[:, 0:T], ALU.mult, [rhg, rhv], [ra_])
        if not full:
            return
        for nh in range(2):
            accs = [PB() for _ in range(ntile)]
            for s0, n in ((0, 8), (8, 8), (16, 6)):
                sd, rsd = load_slot(w_down[s0 * 128:(s0 + n) * 128, nh * 512:(nh + 1) * 512], n, 512)
                for t in range(ntile):
                    p, rp = accs[t]
                    for j in range(n):
                        c = s0 + j
                        a_, ra_ = chunks[c]
                        mm(p[0:P, :], a_[:, t * 128:t * 128 + P], sd[:, j, :], c == 0, c == 21, [ra_, rsd], [rp])
            for t in range(ntile):
                p, rp = accs[t]
                tmp, rtmp = t32.next()
                tt(tmp[0:P, :], p[0:P, :], gt_ap_fn(1, nh), ALU.mult, [rp, rgt[1]], [rtmp])
                xs_ = xblk[0:P, t, nh * 512:(nh + 1) * 512]
                tt(xs_, xs_, tmp[0:P, :], ALU.add, [rx[t], rtmp], [rx[t]])
        for t in range(ntile):
            st(y_dst_fn(t), xblk[0:P, t, :], [rx[t]])

    def gtp_fn(which, nh):
        return gtp[:, which, nh * 512:(nh + 1) * 512]

    def gts_fn(which, nh):
        return gtp[0:64, which, nh * 512:(nh + 1) * 512]

    def prompt_block(kind, bi):
        mark(f"{kind}{bi}:start")
        src = xo if kind == "own" else xp
        r0 = bi * 512
        do_q = kind in ("own", "halo")
        tile0 = (16 + bi * 4) if kind == "own" else bi * 4
        for t in range(4):
            ld(xblk[:, t, :], src[r0 + t * 128:r0 + (t + 1) * 128, :], [rx[t]])
        for t in range(4):
            norm_transpose(128, xblk[:, t, :], rx[t], t, 0, 0, 1)
        groups = ([0] if do_q else []) + [1, 2]
        for cg in groups:
            sv, rs = load_slot(w_in[:, cg * 512:(cg + 1) * 512], 8, 512)
            if cg in (0, 1):
                for pair in ((0, 1, 2), (3,)):
                    gens = []
                    for t in pair:
                        p, rp = PB()
                        for kc in range(8):
                            mm(p[:, :], uT[:, kc, t * 128:(t + 1) * 128], sv[:, kc, :], kc == 0, kc == 7, [ruT, rs], [rp])
                        kt = tile0 + t
                        if cg == 0:
                            gens.append(qk_post_gen(128, p[:, :], rp, 0, kt, None, "blk", None, t))
                        else:
                            kd = k_o[r0 + t * 128:r0 + (t + 1) * 128, :] if kind == "own" else None
                            gens.append(qk_post_gen(128, p[:, :], rp, 1, kt, kd, None, kt, t))
                    lockstep(gens)
                continue
            for t in range(4):
                p, rp = PB()
                for kc in range(8):
                    mm(p[:, :], uT[:, kc, t * 128:(t + 1) * 128], sv[:, kc, :], kc == 0, kc == 7, [ruT, rs], [rp])
                kt = tile0 + t
                if kind == "own":
                    v32, rv32 = t32.next()
                    cp(v32[:, :], p[:, :], [rp], [rv32])
                    st(v_o[r0 + t * 128:r0 + (t + 1) * 128, :], v32[:, :], [rv32])
                cp(VA[:, kt, :, 0:128], p[:, :].rearrange("p (a b) -> p a b", a=4), [rp], [rVA[kt]])
        mark(f"{kind}{bi}:lru")
        lxs, rlxs = load_slot(w_in[:, 1536:2048], 8, 512)
        if do_q:
            lgs, rlgs = load_slot(w_in[:, 2048:2560], 8, 512)
            lru_chunks(512, 1, (lxs, lgs), (rlxs, rlgs), True, False, 4)
        else:
            lru_chunks(512, 1, (lxs, None), (rlxs, None), False, False, 4)
        if not do_q:
            return
        if kind == "halo":
            full = [(kt, pbiasc) for kt in range(0, 12)]
            diag = [12, 13, 14, 15]
        else:
            full = [(kt, pbiasc) for kt in range(0, 16)] + [(kt, zeroc) for kt in range(16, 16 + bi * 4)]
            diag = [16 + bi * 4 + j for j in range(4)]
        mark(f"{kind}{bi}:attn")
        attn_prompt(full, diag)
        mark(f"{kind}{bi}:outproj")
        out_proj_residual(128, 4, gtp_fn, rgtp)
        mark(f"{kind}{bi}:ffn")
        if kind == "halo":
            ffn_block(128, 4, 512, 0, 1, gtp_fn, rgtp, False, False, None, only_last_tile=True)
            ts(carry[:, :, 0:1], carry[:, :, 0:1], pmulc, None, ALU.mult, None, [rcarry, rcols], [rcarry])
            ts(hist[:, :, 0, :], hist[:, :, 0, :], pmulc, None, ALU.mult, None, [rhist, rcols], [rhist])
            ts(fhist[:, :, 0, :], fhist[:, :, 0, :], pmulc, None, ALU.mult, None, [rfhist, rcols], [rfhist])
        elif stage == 31:
            pass
        elif stage == 32:
            ffn_block(128, 4, 512, 0, 1, gtp_fn, rgtp, False, False, None)
        else:
            ffn_block(128, 4, 512, 0, 1, gtp_fn, rgtp, False, True, lambda t: y_o[r0 + t * 128:r0 + (t + 1) * 128, :])

    for bi in range(3):
        prompt_block("pre", bi)
        for g in ADA_LATER[bi]:
            ada_group(g)
        if stage == 1:
            return finish()
    prompt_block("halo", 3)
    if stage == 2:
        return finish()
    for bi in range(4):
        prompt_block("own", bi)
        if stage in (3, 31, 32):
            return finish()
    if stage == 4:
        return finish()
    st(lc_p, hist[:, :, 0, :], [rhist]); st(lh_p, carry[:, :, 0], [rcarry]); st(fc_p, fhist[:, :, 0, :], [rfhist])

    mark("sample:start")
    for g in (4, 5, 10, 11):
        sv, rs = load_slot(w_ada[:, g * 512:(g + 1) * 512], 8, 512)
        ada_tok(sv, rs, g, 0 if g < 6 else 1, g % 2, True)
    ld(hist[:], slc, [rhist]); ld(carry[:], slh, [rcarry]); ld(fhist[:], sfc, [rfhist])
    ld(xblk[0:64, 0, :], xs, [rx[0]])
    norm_transpose(64, xblk[0:64, 0, :], rx[0], 0, 0, 1, 64)
    allK = rKT + rVA
    rKpTs = [B.R("KpT0"), B.R("KpT1")]; rvss = [[B.R(f"vs{i}_{g}") for g in range(4)] for i in range(2)]
    rkg = [B.R(f"kg{i}") for i in range(2)]; rQsb = B.R("Qsb"); rKsT = B.R("KsT"); rVs = B.R("VAs")
    for rr in [rKpTs[0], rKpTs[1]] + rvss[0] + rvss[1]:
        for o_ in allK:
            if o_.last_w is not None:
                rr.readers.append(o_.last_w)
            rr.readers.extend(o_.readers)
    KpTs = [KT[:, :, 0:2048], KT[:, :, 2048:4096]]
    VAf = VA[:, :, :, :].rearrange("p a b c -> p (a b c)")
    vss = [VAf[:, i * 8192:(i + 1) * 8192].rearrange("p (g f) -> p g f", f=512) for i in range(2)]
    smpbuf = B.sb("smpbuf", [128, 5376], BF16)
    kg = [smpbuf[:, i * 2048:(i + 1) * 2048].rearrange("p (g f) -> p g f", f=512) for i in range(2)]
    Qsb = smpbuf[:, 4096:4096 + 512].rearrange("p (a s c) -> p a s c", a=4, s=16)
    KsT = smpbuf[:, 4608:4608 + 256].rearrange("p (a t) -> p a t", a=4)
    VAs = smpbuf[:, 4864:4864 + 512].rearrange("p (a d) -> p a d", a=4)
    QsT = QsTt[:, :, :]
    for cg in range(3):
        sv, rs = load_slot(w_in[:, cg * 512:(cg + 1) * 512], 8, 512)
        p, rp = PB()
        for kc in range(8):
            mm(p[0:64, :], uT[:, kc, 0:64], sv[:, kc, :], kc == 0, kc == 7, [ruT, rs], [rp])
        if cg == 0:
            qk_post(64, p[0:64, :], rp, 0, 32, None, QsTt[:, :, :], None, 0)
        elif cg == 1:
            sq_k = k_s
            qk_post(64, p[0:64, :], rp, 1, 32, sq_k, KsT, None, 0, dst_reg=rKsT)
        else:
            v32, rv32 = t32.next()
            cp(v32[0:64, :], p[0:64, :], [rp], [rv32])
            st(v_s, v32[0:64, :], [rv32])
            cp(VAs[0:64, :, :], p[0:64, :].rearrange("p (a b) -> p a b", a=4), [rp], [rVs])
    lxs, rlxs = load_slot(w_in[:, 1536:2048], 8, 512)
    lgs, rlgs = load_slot(w_in[:, 2048:2560], 8, 512)
    lru_chunks(64, 16, (lxs, lgs), (rlxs, rlgs), True, True, 4)
    st(lc_s, hist[:], [rhist]); st(lh_s, carry[:], [rcarry])
    memset(Qsb, 0.0, [rQsb])
    for c in range(2):
        cp(Qsb[c * 64:(c + 1) * 64, :, :, c * 4:(c + 1) * 4], QsT[c * 64:(c + 1) * 64, :, :].rearrange("p a (s t) -> p a s t", t=4),
           [rQT], [rQsb])
    ptt_, rpti = t32.next()
    pti = ptt_[:, 0:256].bitcast(I32)
    ptf_, rptf = t32.next()
    ptf = ptf_[:, 0:256]
    idx = B.sb("idx", [128, 64], I32); ridx = B.R()
    gselt = B.sb("gselt", [128, 5], F32); rgsel = B.R()
    ld(pti[:], ptab.partition_broadcast(128), [rpti])
    ld(gselt[:], gsel, [rgsel])
    cp(ptf[:], pti[:], [rpti], [rptf])
    ptf3 = ptf.rearrange("p (a j) -> p a j", j=4)
    i2f = ptf_[:, 256:320]
    ts(i2f, ptf3[:, :, 0], gselt[:, 0:1], None, ALU.mult, None, [rptf, rgsel], [rptf])
    for j in range(1, 4):
        stt(i2f, ptf3[:, :, j], gselt[:, j:j + 1], i2f, ALU.mult, ALU.add, [rptf, rgsel], [rptf])
    ts(i2f, i2f, 32.0, gselt[:, 4:5], ALU.mult, ALU.add, [rptf, rgsel], [rptf])
    cp(idx[:], i2f, [rptf], [ridx])
    cache_k4 = cache_k.rearrange("(r f) d -> r (f d)", f=4)
    cache_v4 = cache_v.rearrange("(r f) d -> r (f d)", f=4)
    msk = B.sb("msk", [64, 16, 8], F32); rmsk = B.R()
    cmbt = B.sb("cmbt", [8, 124], F32); rcmb = B.R()
    ld(msk[:], smask, [rmsk]); ld(cmbt[:], cmb, [rcmb])
    ones_bf = B.sb("ones_bf", [128, 1], BF16); rones = B.R()
    memset(ones_bf[:], 1.0, [rones])
    lamc8 = B.sb("lamc8", [8, 1], F32); rlamc8 = B.R()
    memset(lamc8[:], 1.0, [rlamc8])
    cp(lamc8[0:8, :], cols[0:8, C_NLAM:C_NLAM + 1], [rcols], [rlamc8])
    memset(lamc8[0:4, :], 1.0, [rlamc8])
    att_all, ratt = psB[:, 4, :], rB[4]
    pb4 = [0]

    def PB4():
        k = pb4[0] % 3
        pb4[0] += 1
        return psB[:, k, :], rB[k]
    psT2 = psB[:, 3, :].bitcast(BF16)
    tbufs = [(psT, rT), (psT2, rB[3])]
    mark("sample:seqs")
    for s in range(16):
        KpT, rKpT = KpTs[s % 2], rKpTs[s % 2]
        vsb, rvs = vss[s % 2], rvss[s % 2]
        for g in range(4):
            kb, rkb = kg[g % 2], rkg[g % 2]
            B.dma("pool", lambda h, kb=kb, col=s * 4 + g: h.indirect_dma_start(
                out=kb.rearrange("p a f -> p (a f)"), out_offset=None, in_=cache_k4[:, :],
                in_offset=bass.IndirectOffsetOnAxis(ap=idx[:, col:col + 1], axis=0)), [ridx], [rkb])
            for j in range(4):
                pg = g * 4 + j
                tb, rtb = tbufs[pg % 2]
                for hh in range(4):
                    tr(tb[:, hh * 128:(hh + 1) * 128], kb[:, j, hh * 128:(hh + 1) * 128], ident[:, :], [rkb, rident], [rtb])
                cp(KpT[:, :, pg * 128:(pg + 1) * 128], tb[:, 0:512].rearrange("p (a b) -> p a b", a=4), [rtb], [rKpT])
        for g in range(4):
            B.dma("pool", lambda h, g=g, col=s * 4 + g, vsb=vsb: h.indirect_dma_start(
                out=vsb[:, g * 4:(g + 1) * 4, :].rearrange("p a f -> p (a f)"), out_offset=None, in_=cache_v4[:, :],
                in_offset=bass.IndirectOffsetOnAxis(ap=idx[:, col:col + 1], axis=0)), [ridx], [rvs[g]])
        pS, rpS = PA()
        pS4 = pS[:, :].rearrange("p (k a c) -> p k a c", k=16, a=4)
        for pg in range(16):
            for hh in range(4):
                mm(pS4[:, pg, hh, :], KpT[:, hh, pg * 128:(pg + 1) * 128], Qsb[:, hh, s, :], True, True, [rKpT, rQsb], [rpS])
        eS, reS = Et.next()
        eS4 = eS[:, :].rearrange("p (k a c) -> p k a c", k=16, a=4)
        act(eS[:, :], pS[:, :], AF.Exp, [rpS], [reS], scale=0.125)
        pN, rpN = PA()
        for hh in range(4):
            mm(pN[0:64, hh * 8:(hh + 1) * 8], KsT[:, hh, :], Qsb[:, hh, s, :], True, True, [rKsT, rQsb], [rpN])
        eN, reN = Et.next()
        eN32, reN32 = t32.next()
        act(eN32[0:64, 0:32], pN[0:64, 0:32], AF.Exp, [rpN], [reN32], scale=0.125)
        tt(eN[0:64, 0:32].rearrange("p (a c) -> p a c", a=4), eN32[0:64, 0:32].rearrange("p (a c) -> p a c", a=4),
           msk[:, s, :].unsqueeze(1).to_broadcast([64, 4, 8]), ALU.mult, [reN32, rmsk], [reN])
        pO, rpO = PB4()
        pSm, rpSm = PA()
        for hh in range(4):
            for pg in range(16):
                mm(pO[0:8, hh * 128:(hh + 1) * 128], eS4[:, pg, hh, :], vsb[:, pg, hh * 128:(hh + 1) * 128], pg == 0, False,
                   [reS, rvs[pg // 4]], [rpO])
                mm(pSm[0:8, hh:hh + 1], eS4[:, pg, hh, :], ones_bf[:, :], pg == 0, False, [reS, rones], [rpSm])
            mm(pO[0:8, hh * 128:(hh + 1) * 128], eN[0:64, hh * 8:(hh + 1) * 8], VAs[0:64, hh, :], False, True, [reN, rVs], [rpO])
            mm(pSm[0:8, hh:hh + 1], eN[0:64, hh * 8:(hh + 1) * 8], ones_bf[0:64, :], False, True, [reN, rones], [rpSm])
        sb_, rs_ = stat.next()
        B.op("dve", lambda h, s_=sb_, p_=pSm: h.reciprocal(out=s_[0:8, 0:4], in_=p_[0:8, 0:4]), [rpSm], [rs_])
        ts(sb_[0:8, 0:4], sb_[0:8, 0:4], lamc8[:, 0:1], None, ALU.mult, None, [rs_, rlamc8], [rs_])
        osc, rosc = t32.next()
        tt(osc[0:8, :].rearrange("p (a d) -> p a d", a=4), pO[0:8, :].rearrange("p (a d) -> p a d", a=4),
           sb_[0:8, 0:4].unsqueeze(2).to_broadcast([8, 4, 128]), ALU.mult, [rpO, rs_], [rosc])
        mm(att_all[0:64, :], cmbt[:, 60 - 4 * s:124 - 4 * s], osc[0:8, :], s == 0, s == 15, [rcmb, rosc], [ratt])
    mark("sample:tail")
    at, rat = t32.next()
    cp(at[0:64, :], att_all[0:64, :], [ratt], [rat])
    for hh in range(4):
        ah, rah = t32.next()
        cp(ah[0:64, 0:128], at[0:64, hh * 128:(hh + 1) * 128], [rat], [rah])
        sub_ln_to_moT(64, ah, rah, 1, hh)
    out_proj_residual(64, 1, gts_fn, rgtp)
    ffn_block(64, 1, 64, 1, 64, gts_fn, rgtp, True, True, lambda t: y_s[:, :])
    st(fc_s, fhist[:], [rfhist])

    return finish()


_NC_CACHE = {}


def _rope_table(pos):
    half = 8
    freqs = (np.float32(500000.0) ** (-np.arange(half, dtype=np.float32) * np.float32(2.0) / np.float32(16))).astype(np.float32)
    ang = pos.astype(np.float32)[:, None] * freqs[None, :]
    return np.concatenate([np.cos(ang), np.sin(ang)], axis=1).astype(np.float32)


def kernel(x_prompt, x_sample, cache_k, cache_v, page_table, state_lru_conv, state_lru_h, state_ffn_conv,
           c_prompt, c_sample, g_norm1, g_norm2, w_ada, b_ada, w_in, g_q, g_k, lam_q1, lam_k1, lam_q2, lam_k2,
           g_subln, w_out, conv_lru_w, conv_lru_b, w_rgate, b_rgate, w_igate, b_igate, lru_lambda,
           w_up, conv_ffn_w, conv_ffn_b, w_down):
    f32 = np.float32
    A = lambda a: np.ascontiguousarray(np.asarray(a))
    x_prompt = A(x_prompt); x_sample = A(x_sample)
    n_rows = int(np.prod(np.shape(cache_k)[:3]))
    if "nc" not in _NC_CACHE:
        _NC_CACHE["nc"] = build_program(n_rows, _NC_CACHE.get("stage", 99))
    nc = _NC_CACHE["nc"]

    ck = A(cache_k).reshape(-1, 512)
    cv = A(cache_v).reshape(-1, 512)
    featT = lambda v, n: A(np.asarray(v, f32).reshape(n, 128).T)
    shared = {
        "w_ada": A(w_ada[0]), "b_adaT": featT(b_ada[0], 48), "b_ada": A(b_ada[0].reshape(1, 6144)),
        "g1T": featT(g_norm1[0], 8), "g2T": featT(g_norm2[0], 8),
        "w_in": A(w_in[0]), "w_out": A(w_out[0]), "w_up": A(w_up[0]), "w_down": A(w_down[0]),
        "gq_rep": A(np.broadcast_to(np.tile(np.asarray(g_q[0]), 8)[None, :], (128, 512))),
        "gk_rep": A(np.broadcast_to(np.tile(np.asarray(g_k[0]), 8)[None, :], (128, 512))),
        "gsub_rep": A(np.broadcast_to(np.tile(np.asarray(g_subln[0]), 4)[None, :], (128, 512))),
        "lamv": A(np.concatenate([lam_q1[0], lam_k1[0], lam_q2[0], lam_k2[0]]).reshape(1, 256)),
        "clw": A(np.asarray(conv_lru_w[0]).reshape(4, 4, 128).transpose(2, 1, 0)),
        "clb": featT(conv_lru_b[0], 4),
        "w_rg": A(w_rgate[0]), "w_ig": A(w_igate[0]),
        "b_rg": featT(np.asarray(b_rgate[0]).reshape(-1), 4), "b_ig": featT(np.asarray(b_igate[0]).reshape(-1), 4),
        "lru_lam": featT(lru_lambda[0], 4),
        "cfw": A(np.asarray(conv_ffn_w[0]).reshape(3, 44, 128).transpose(2, 1, 0)),
        "cfb": featT(conv_ffn_b[0], 44),
        "cache_k": ck, "cache_v": cv,
    }
    smask = np.zeros((64, 16, 8), f32)
    cmb = np.zeros((8, 124), f32)
    for c in range(2):
        for q in range(4):
            cmb[c * 4 + q, 60 + q] = 1.0
    for s in range(16):
        for j in range(4):
            for c in range(2):
                for q in range(4):
                    if j <= q:
                        smask[s * 4 + j, s, c * 4 + q] = 1.0
    gsel = np.zeros((128, 5), f32)
    for p in range(128):
        gsel[p, p // 32] = 1.0
        gsel[p, 4] = float(p % 32)
    shared["gsel"] = gsel
    shared["smask"] = smask
    shared["cmb"] = cmb

    past_len = page_table.shape[1] * 128
    in_maps = []
    for core in range(8):
        b, h = core // 2, core % 2
        m = dict(shared)
        m["xp"] = A(x_prompt[b, 0:2048])
        m["xo"] = A(x_prompt[b, h * 2048:(h + 1) * 2048])
        m["xs"] = A(x_sample[core * 16:(core + 1) * 16].reshape(64, 1024))
        cvec = np.concatenate([np.asarray(c_prompt[b])[None, :], np.asarray(c_sample[core * 16:(core + 1) * 16])], axis=0)
        m["cT"] = A(cvec.reshape(17, 8, 128).transpose(2, 1, 0))
        rp = _rope_table(np.arange(0, 2048))
        ro = _rope_table(np.arange(h * 2048, (h + 1) * 2048))
        m["ropep"] = A(rp.reshape(16, 128, 16).transpose(1, 0, 2))
        m["ropeo"] = A(ro.reshape(16, 128, 16).transpose(1, 0, 2))
        m["ropes"] = A(np.tile(_rope_table(past_len + np.arange(4)), (16, 1)))
        fl = np.zeros((128, 2), f32)
        fl[:, 0] = 0.0 if h == 1 else -10000.0
        fl[:, 1] = 1.0 if h == 1 else 0.0
        m["flags"] = fl
        sl = slice(core * 16, (core + 1) * 16)
        m["slc"] = A(np.asarray(state_lru_conv[0][sl]).reshape(16, 3, 4, 128).transpose(3, 2, 0, 1))
        m["slh"] = A(np.asarray(state_lru_h[0][sl]).reshape(16, 4, 128).transpose(2, 1, 0))
        m["sfc"] = A(np.asarray(state_ffn_conv[0][sl]).reshape(16, 2, 44, 128).transpose(3, 2, 0, 1))
        m["ptab"] = A(np.asarray(page_table[sl], np.int32).reshape(1, 256))
        in_maps.append(m)

    res = run_bass_kernel_spmd(nc, in_maps, core_ids=list(range(8)))
    R = res.results

    y_p = np.zeros((4, 4096, 1024), f32); k_p = np.zeros((1, 4, 4096, 512), f32); v_p = np.zeros((1, 4, 4096, 512), f32)
    y_s = np.zeros((128, 4, 1024), f32); k_s = np.zeros((1, 128, 4, 512), f32); v_s = np.zeros((1, 128, 4, 512), f32)
    lc_p = np.zeros((1, 4, 3, 512), f32); lh_p = np.zeros((1, 4, 512), f32); fc_p = np.zeros((1, 4, 2, 5632), f32)
    lc_s = np.zeros((1, 128, 3, 512), f32); lh_s = np.zeros((1, 128, 512), f32); fc_s = np.zeros((1, 128, 2, 5632), f32)
    for core in range(8):
        b, h = core // 2, core % 2
        r = R[core]
        y_p[b, h * 2048:(h + 1) * 2048] = r["y_o"]
        k_p[0, b, h * 2048:(h + 1) * 2048] = r["k_o"]
        v_p[0, b, h * 2048:(h + 1) * 2048] = r["v_o"]
        sl = slice(core * 16, (core + 1) * 16)
        y_s[sl] = r["y_s"].reshape(16, 4, 1024)
        k_s[0, sl] = r["k_s"].reshape(16, 4, 512)
        v_s[0, sl] = r["v_s"].reshape(16, 4, 512)
        if h == 1:
            lc_p[0, b] = r["lc_p"].transpose(2, 1, 0).reshape(3, 512)
            lh_p[0, b] = r["lh_p"].T.reshape(512)
            fc_p[0, b] = r["fc_p"].transpose(2, 1, 0).reshape(2, 5632)
        lc_s[0, sl] = r["lc_s"].transpose(2, 3, 1, 0).reshape(16, 3, 512)
        lh_s[0, sl] = r["lh_s"].transpose(2, 1, 0).reshape(16, 512)
        fc_s[0, sl] = r["fc_s"].transpose(2, 3, 1, 0).reshape(16, 2, 5632)
    return (y_p, y_s.reshape(128, 4, 1024), k_p.reshape(1, 4, 4096, 4, 2, 64), v_p.reshape(1, 4, 4096, 4, 128),
            lc_p, lh_p, fc_p, k_s.reshape(1, 128, 4, 4, 2, 64), v_s.reshape(1, 128, 4, 4, 128), lc_s, lh_s, fc_s)
```

```python
import math, os
import numpy as np
from contextlib import ExitStack
import concourse.bass as bass
import concourse.mybir as mybir
from concourse.bass_utils import run_bass_kernel_spmd

F32 = mybir.dt.float32
BF16 = mybir.dt.bfloat16
I32 = mybir.dt.int32
ALU = mybir.AluOpType
AF = mybir.ActivationFunctionType
AX = mybir.AxisListType


class Region:
    __slots__ = ("name", "last_w", "readers", "wsem", "rsem", "wcnt", "rcnt", "excl")

    def __init__(self, name, excl=False):
        self.name = name
        self.excl = excl
        self.last_w = None
        self.readers = []
        self.wsem = None
        self.rsem = None
        self.wcnt = 0
        self.rcnt = 0


class Op:
    __slots__ = ("eng", "fn", "deps", "idx", "needed", "dma_tok", "name")


class Builder:
    ENGS = ("pe", "act", "dve", "pool", "sp")

    def __init__(self, nc, self_sync=True):
        self.nc = nc
        self.ops = {e: [] for e in self.ENGS}
        self.self_sync = self_sync
        self.dma_sems = {}
        self.final_toks = []
        self.es = ExitStack()
        self.nreg = 0

    def sb(self, name, shape, dt):
        return self.es.enter_context(self.nc.sbuf_tensor(name, list(shape), dt))

    def ps(self, name, shape, dt):
        return self.es.enter_context(self.nc.psum_tensor(name, list(shape), dt))

    def R(self, name=None, excl=False):
        self.nreg += 1
        return Region(name or f"r{self.nreg}", excl)

    def _deps(self, eng, reads, writes):
        deps = []
        for r in reads:
            if r.last_w is not None:
                deps.append(r.last_w)
            if r.excl:
                deps.extend(t for t in r.readers if t[0] == "c" and t[1] != eng)
        for w in writes:
            if w.last_w is not None:
                deps.append(w.last_w)
            deps.extend(w.readers)
        return deps

    def op(self, eng, fn, reads=(), writes=(), name=None):
        o = Op()
        o.eng = eng
        o.fn = fn
        o.deps = self._deps(eng, reads, writes)
        o.idx = len(self.ops[eng])
        o.needed = False
        o.dma_tok = None
        o.name = name
        self.ops[eng].append(o)
        tok = ("c", eng, o.idx)
        for w in writes:
            w.last_w = tok
            w.readers = []
        for r in reads:
            if all(r is not w for w in writes):
                r.readers.append(tok)
        return tok

    def dma(self, q, fn, reads=(), writes=(), final=False, name=None):
        o = Op()
        o.eng = q
        o.fn = fn
        o.deps = self._deps(q, reads, writes)
        o.idx = len(self.ops[q])
        o.needed = False
        o.name = name
        if writes:
            reg = writes[0]
            key = ("w", id(reg))
            reg.wcnt += 16
            cnt = reg.wcnt
        else:
            reg = reads[0]
            key = ("r", id(reg))
            reg.rcnt += 16
            cnt = reg.rcnt
        if key not in self.dma_sems:
            self.dma_sems[key] = len(self.dma_sems)
        tok = ("d", key, cnt)
        o.dma_tok = tok
        self.ops[q].append(o)
        for w in writes:
            w.last_w = tok
            w.readers = []
        for r in reads:
            r.readers.append(tok)
        if final:
            self.final_toks.append(tok)
        return tok

    def finalize(self):
        nc = self.nc
        for e in self.ENGS:
            for o in self.ops[e]:
                for d in o.deps:
                    if d[0] == "c":
                        if d[1] == e and o.dma_tok is None and not (self.self_sync and e != "pe"):
                            continue
                        self.ops[d[1]][d[2]].needed = True
        semval = {}
        for e in self.ENGS:
            c = 0
            for o in self.ops[e]:
                if o.needed:
                    c += 1
                    semval[(e, o.idx)] = c
        es = self.es
        esems = {e: es.enter_context(nc.semaphore(f"s_{e}")) for e in self.ENGS}
        dsems = {}
        for key, i in self.dma_sems.items():
            dsems[key] = es.enter_context(nc.semaphore(f"d_{i}"))
        handles = {}
        block = es.enter_context(nc.Block())
        ops = self.ops
        self_sync = self.self_sync
        final_toks = self.final_toks

        def emit(e, h):
            waited = {}
            for o in ops[e]:
                need = {}
                for d in o.deps:
                    if d[0] == "c":
                        if d[1] == e and o.dma_tok is None and not (self_sync and e != "pe"):
                            continue
                        s = ("c", d[1])
                        v = semval[(d[1], d[2])]
                    else:
                        s = ("d", d[1])
                        v = d[2]
                    if waited.get(s, 0) >= v:
                        continue
                    if need.get(s, 0) < v:
                        need[s] = v
                for s, v in need.items():
                    sem = esems[s[1]] if s[0] == "c" else dsems[s[1]]
                    h.wait_ge(sem, v)
                    waited[s] = v
                inst = o.fn(h)
                if o.dma_tok is not None:
                    inst.then_inc(dsems[o.dma_tok[1]], 16)
                elif o.needed:
                    inst.then_inc(esems[e], 1)
            if e == "sp":
                fin = {}
                for t in final_toks:
                    if fin.get(t[1], 0) < t[2]:
                        fin[t[1]] = t[2]
                for k, v in fin.items():
                    if waited.get(("d", k), 0) < v:
                        h.wait_ge(dsems[k], v)

        @block.tensor
        def _(h):
            emit("pe", h)

        @block.scalar
        def _(h):
            emit("act", h)

        @block.vector
        def _(h):
            emit("dve", h)

        @block.gpsimd
        def _(h):
            emit("pool", h)

        @block.sync
        def _(h):
            emit("sp", h)

    def close(self):
        self.es.close()


LAM_INIT = 0.8 - 0.6 * math.exp(-0.3 * 0)
EPS = 1e-6
NSLOT = 3
NT32 = 7


def build_program(n_rows=327680, stage=99):
    nc = bass.Bass("TRN2", target_bir_lowering=False)
    B = Builder(nc)

    def din(name, shape, dt=F32):
        return nc.dram_tensor(name, list(shape), dt, kind="ExternalInput").ap()

    def dout(name, shape, dt=F32):
        return nc.dram_tensor(name, list(shape), dt, kind="ExternalOutput").ap()

    xp = din("xp", [2048, 1024]); xo = din("xo", [2048, 1024]); xs = din("xs", [64, 1024])
    cT = din("cT", [128, 8, 17])
    ropep = din("ropep", [128, 16, 16]); ropeo = din("ropeo", [128, 16, 16]); ropes = din("ropes", [64, 16])
    flags = din("flags", [128, 2])
    w_ada = din("w_ada", [1024, 6144]); b_adaT = din("b_adaT", [128, 48]); b_ada = din("b_ada", [1, 6144])
    g1T = din("g1T", [128, 8]); g2T = din("g2T", [128, 8])
    w_in = din("w_in", [1024, 2560]); w_out = din("w_out", [1024, 1024])
    w_up = din("w_up", [1024, 5632]); w_down = din("w_down", [2816, 1024])
    gq_rep = din("gq_rep", [128, 512]); gk_rep = din("gk_rep", [128, 512]); gsub_rep = din("gsub_rep", [128, 512])
    lamv = din("lamv", [1, 256])
    clw = din("clw", [128, 4, 4]); clb = din("clb", [128, 4])
    w_rg = din("w_rg", [8, 64, 64]); w_ig = din("w_ig", [8, 64, 64])
    b_rg = din("b_rg", [128, 4]); b_ig = din("b_ig", [128, 4]); lru_lam = din("lru_lam", [128, 4])
    cfw = din("cfw", [128, 44, 3]); cfb = din("cfb", [128, 44])
    slc = din("slc", [128, 4, 16, 3]); slh = din("slh", [128, 4, 16]); sfc = din("sfc", [128, 44, 16, 2])
    ptab = din("ptab", [1, 256], I32)
    smask = din("smask", [64, 16, 8]); cmb = din("cmb", [8, 124]); gsel = din("gsel", [128, 5])
    cache_k = din("cache_k", [n_rows, 512]); cache_v = din("cache_v", [n_rows, 512])

    y_o = dout("y_o", [2048, 1024]); y_s = dout("y_s", [64, 1024])
    k_o = dout("k_o", [2048, 512]); v_o = dout("v_o", [2048, 512])
    k_s = dout("k_s", [64, 512]); v_s = dout("v_s", [64, 512])
    lc_p = dout("lc_p", [128, 4, 3]); lh_p = dout("lh_p", [128, 4]); fc_p = dout("fc_p", [128, 44, 2])
    lc_s = dout("lc_s", [128, 4, 16, 3]); lh_s = dout("lh_s", [128, 4, 16]); fc_s = dout("fc_s", [128, 44, 16, 2])

    def mm(out, lhsT, rhs, start, stop, reads, writes):
        return B.op("pe", lambda h: h.matmul(out, lhsT=lhsT, rhs=rhs, start=start, stop=stop), reads, writes)

    def tr(out, in_, ident, reads, writes):
        return B.op("pe", lambda h: h.transpose(out=out, in_=in_, identity=ident), reads, writes)

    def act(out, in_, func, reads, writes, **kw):
        return B.op("act", lambda h: h.activation(out=out, in_=in_, func=func, **kw), reads, writes)

    def tt(out, in0, in1, op, reads, writes, eng="dve"):
        return B.op(eng, lambda h: h.tensor_tensor(out=out, in0=in0, in1=in1, op=op), reads, writes)

    def ts(out, in0, s1, s2, op0, op1, reads, writes, eng="dve"):
        if s2 is None:
            return B.op(eng, lambda h: h.tensor_scalar(out=out, in0=in0, scalar1=s1, scalar2=None, op0=op0), reads, writes)
        return B.op(eng, lambda h: h.tensor_scalar(out=out, in0=in0, scalar1=s1, scalar2=s2, op0=op0, op1=op1), reads, writes)

    def stt(out, in0, scalar, in1, op0, op1, reads, writes, eng="dve"):
        return B.op(eng, lambda h: h.scalar_tensor_tensor(out=out, in0=in0, scalar=scalar, in1=in1, op0=op0, op1=op1), reads, writes)

    def cp(out, in_, reads, writes, eng="dve"):
        return B.op(eng, lambda h: h.tensor_copy(out, in_), reads, writes)

    def memset(ap, val, writes, eng="dve"):
        return B.op(eng, lambda h: h.memset(ap, val), (), writes)

    def ld(out, in_, writes, reads=(), q="sp"):
        return B.dma(q, lambda h: h.dma_start(out=out, in_=in_, allow_slow_non_contiguous=True), reads, writes)

    def st(out, in_, reads, q=None):
        q = q or os.environ.get("STQ", "sp")
        return B.dma(q, lambda h: h.dma_start(out=out, in_=in_, allow_slow_non_contiguous=True), reads, (), final=True)

    class Rot:
        def __init__(self, name, shape, dt, n):
            self.bufs = [(B.sb(f"{name}{i}", shape, dt), B.R(f"{name}{i}")) for i in range(n)]
            self.i = 0

        def next(self):
            b = self.bufs[self.i % len(self.bufs)]
            self.i += 1
            return b

    KT = B.sb("KT", [128, 4, 4096], BF16)
    rKT = [B.R(f"KT{i}") for i in range(32)]
    VA = B.sb("VA", [128, 32, 4, 130], BF16)
    rVA = [B.R(f"VA{i}") for i in range(32)]
    wslot = Rot("wslot", [128, 4096], BF16, NSLOT)
    gtp = B.sb("gtp", [128, 2, 1024], F32); rgtp = [B.R(), B.R()]
    ABc = B.sb("ABc", [128, 4, 8, 65], F32); rAB = B.R()
    xblk = B.sb("xblk", [128, 4, 1024], F32); rx = [B.R(f"x{i}") for i in range(4)]
    xnbf = B.sb("xnbf", [128, 1024], BF16); rxn = B.R()
    uT = B.sb("uT", [128, 8, 512], BF16); ruT = B.R()
    chunks = [(B.sb(f"ch{i}", [128, 512], BF16), B.R(f"ch{i}")) for i in range(22)]
    QT = B.sb("QT", [128, 4, 2, 512], BF16); rQT = B.R()
    QsTt = B.sb("QsTt", [128, 4, 64], BF16)
    qkbf = B.sb("qkbf", [128, 512], BF16); rqkbf = B.R()
    t32 = Rot("t32", [128, 512], F32, NT32)
    Et = Rot("Et", [128, 512], BF16, 3)
    attnbf = B.sb("attnbf", [128, 512], BF16); rattnbf = B.R()
    ropet = B.sb("ropet", [128, 33, 16], F32); rrope = B.R()
    small = B.sb("small", [128, 64], F32); rsmall = B.R()
    stat = Rot("stat", [128, 16], F32, 6)
    ident = B.sb("ident", [128, 128], BF16); rident = B.R()
    id32_t, rid32 = t32.next()
    id32 = id32_t[:, 0:128]
    tri = B.sb("tri", [128, 128], BF16); rtri = B.R()
    cols = B.sb("cols", [128, 40], F32); rcols = B.R()
    gq64 = B.sb("gq64", [128, 2, 64], F32); gsub = B.sb("gsub", [128, 128], F32); rgqk = B.R()
    lruc = B.sb("lruc", [128, 4, 12], F32); rlruc = B.R()
    Wg = B.sb("Wg", [128, 8, 128], BF16); rWg = B.R()
    hist = B.sb("hist", [128, 4, 16, 3], F32); rhist = B.R()
    carry = B.sb("carry", [128, 4, 16], F32); rcarry = B.R()
    fcw = B.sb("fcw", [128, 44, 4], F32); rfcw = B.R()
    fhist = B.sb("fhist", [128, 44, 16, 2], F32); rfhist = B.R()
    scx = B.sb("scx", [128, 8, 65], BF16); rscx = B.R()
    btile = B.sb("btile", [128, 44, 2], F32); rbt = B.R()
    def screp(kc):
        return chunks[kc // 4][0][:, (kc % 4) * 128:(kc % 4 + 1) * 128], chunks[kc // 4][1]
    ropetile_zero = None

    psB = B.ps("psB", [128, 5, 512], F32); rB = [B.R(f"pB{i}", excl=True) for i in range(5)]
    psA = [(B.ps(f"psA{i}", [128, 512], F32), B.R(f"pA{i}", excl=True)) for i in range(2)]
    psT = B.ps("psT", [128, 1024], BF16); rT = B.R("pT", excl=True)
    pa_i = [0]

    def PA():
        b = psA[pa_i[0] % 2]
        pa_i[0] += 1
        return b

    pb_i = [0]

    def PB():
        k = pb_i[0] % 5
        pb_i[0] += 1
        return psB[:, k, :], rB[k]

    C_EPS, C_PBIAS, C_PMUL, C_ZERO, C_NLAM, C_ONE = 0, 1, 2, 3, 4, 5

    memset(id32[:], 1.0, [rid32], eng="pool")
    B.op("pool", lambda h: h.affine_select(out=id32[:], in_=id32[:], pattern=[[-1, 128]], compare_op=ALU.is_equal,
                                           fill=0.0, base=0, channel_multiplier=1), [rid32], [rid32])
    cp(ident[:], id32[:], [rid32], [rident])
    memset(id32[:], 1.0, [rid32], eng="pool")
    B.op("pool", lambda h: h.affine_select(out=id32[:], in_=id32[:], pattern=[[1, 128]], compare_op=ALU.is_ge,
                                           fill=0.0, base=0, channel_multiplier=-1), [rid32], [rid32])
    cp(tri[:], id32[:], [rid32], [rtri])
    memset(cols[:], 0.0, [rcols])
    memset(cols[:, C_EPS:C_EPS + 1], EPS, [rcols])
    memset(cols[:, C_ONE:C_ONE + 1], 1.0, [rcols])
    ld(cols[:, C_PBIAS:C_PBIAS + 2], flags, [rcols])
    ld(gq64[:, 0, :], gq_rep[:, 0:64], [rgqk]); ld(gq64[:, 1, :], gk_rep[:, 0:64], [rgqk]); ld(gsub[:, :], gsub_rep[:, 0:128], [rgqk])
    ts(gsub[:, :], gsub[:, :], 1.0 - LAM_INIT, None, ALU.mult, None, [rgqk], [rgqk])
    ld(ropet[:, 0:16, :], ropep, [rrope]); ld(ropet[:, 16:32, :], ropeo, [rrope]); ld(ropet[0:64, 32, :], ropes, [rrope])
    memset(VA[:, :, :, 128:130], 1.0, rVA)
    memset(QT[:, :, :, :], 0.0, [rQT])
    epsc = cols[:, C_EPS:C_EPS + 1]
    zeroc = cols[:, C_ZERO:C_ZERO + 1]
    pbiasc = cols[:, C_PBIAS:C_PBIAS + 1]
    pmulc = cols[:, C_PMUL:C_PMUL + 1]

    lt, rlt = t32.next()
    ld(lt[:, 0:256], lamv.partition_broadcast(128), [rlt])
    tt(lt[:, 256:320], lt[:, 0:64], lt[:, 64:128], ALU.mult, [rlt], [rlt])
    tt(lt[:, 320:384], lt[:, 128:192], lt[:, 192:256], ALU.mult, [rlt], [rlt])
    B.op("dve", lambda h: h.tensor_reduce(out=small[:, 0:2], in_=lt[:, 256:384].rearrange("p (a d) -> p a d", d=64),
                                          axis=AX.X, op=ALU.add), [rlt], [rsmall])
    act(small[:, 2:4], small[:, 0:2], AF.Exp, [rsmall], [rsmall])
    tt(small[:, 4:5], small[:, 3:4], small[:, 2:3], ALU.subtract, [rsmall], [rsmall])
    ts(cols[:, C_NLAM:C_NLAM + 1], small[:, 4:5], -LAM_INIT, None, ALU.add, None, [rsmall], [rcols])

    ld(lruc[:, :, 0:4], clw, [rlruc]); ld(lruc[:, :, 4], clb, [rlruc]); ld(lruc[:, :, 5], b_rg, [rlruc])
    ld(lruc[:, :, 6], b_ig, [rlruc]); ld(lruc[:, :, 8], lru_lam, [rlruc])
    act(lruc[:, :, 9], lruc[:, :, 8], AF.Exp, [rlruc], [rlruc], scale=-1.0)
    act(lruc[:, :, 10], lruc[:, :, 9], AF.Ln, [rlruc], [rlruc], bias=cols[:, C_ONE:C_ONE + 1])
    ts(lruc[:, :, 7], lruc[:, :, 10], -8.0, None, ALU.mult, None, [rlruc], [rlruc])
    ld(fcw[:, :, 0:3], cfw, [rfcw]); ld(fcw[:, :, 3], cfb, [rfcw])
    memset(Wg[:], 0.0, [rWg])
    for n in range(8):
        ch, o = n // 2, (n % 2) * 64
        ld(Wg[o:o + 64, ch, o:o + 64], w_rg[n], [rWg], q="pool")
        ld(Wg[o:o + 64, 4 + ch, o:o + 64], w_ig[n], [rWg], q="pool")
    memset(hist[:], 0.0, [rhist]); memset(carry[:], 0.0, [rcarry]); memset(fhist[:], 0.0, [rfhist])

    def load_slot(src2d, nk, ncol):
        s, r = wslot.next()
        v = s[:, 0:nk * ncol].rearrange("p (a b) -> p a b", a=nk)
        B.dma("pool", lambda h: h.dma_start(out=v, in_=src2d.rearrange("(a p) n -> p a n", p=128)), (), [r])
        return v, r

    ct, rct = t32.next()
    ctv = ct[:, 0:136].rearrange("p (a b) -> p a b", a=8)
    ld(ctv, cT, [rct])
    ct2, rct2 = t32.next()
    ct2v = ct2[:, 0:136].rearrange("p (a b) -> p a b", a=8)
    act(ct2v, ctv, AF.Silu, [rct], [rct2])
    cp(scx[:, :, 0:1], ct2v[:, :, 0:1], [rct2], [rscx])
    for kc in range(8):
        cp(scx[:, kc, 1:65].rearrange("p (s t) -> p s t", t=4), ct2v[:, kc, 1:17].unsqueeze(2).to_broadcast([128, 16, 4]),
           [rct2], [rscx])
        cp(screp(kc)[0], ct2v[:, kc, 0:1].to_broadcast([128, 128]), [rct2], [screp(kc)[1]])
    adac = B.sb("adac", [128, 64], F32)
    badT, rbad = adac[:, 0:48], B.R()
    ld(badT[:, 0:48], b_adaT, [rbad])
    gT, rgT = adac[:, 48:64], B.R()
    ld(gT[:, 0:8], g1T, [rgT]); ld(gT[:, 8:16], g2T, [rgT])

    def ada_feat(slot_v, rslot, g, dst_idx, is_scale):
        for j in range(4):
            fch = g * 4 + j
            p, rp = PA()
            for kc in range(8):
                mm(p[:, 0:65], slot_v[:, kc, j * 128:(j + 1) * 128], scx[:, kc, :], kc == 0, kc == 7, [rslot, rscx], [rp])
            kk = fch % 8
            if is_scale:
                ts(ABc[:, dst_idx, kk, :], p[:, 0:65], badT[:, fch:fch + 1], 1.0, ALU.add, ALU.add, [rp, rbad, rAB], [rAB])
                gi = kk if dst_idx == 0 else 8 + kk
                ts(ABc[:, dst_idx, kk, :], ABc[:, dst_idx, kk, :], gT[:, gi:gi + 1], None, ALU.mult, None, [rAB, rgT], [rAB])
            else:
                ts(ABc[:, dst_idx, kk, :], p[:, 0:65], badT[:, fch:fch + 1], None, ALU.add, None, [rp, rbad, rAB], [rAB])

    def ada_tok(slot_v, rslot, g, which, half, sample):
        c0 = g * 512
        P = 64 if sample else 128
        dst = gtp[0:P, which, half * 512:(half + 1) * 512]
        ld(dst, b_ada[0:1, c0:c0 + 512].partition_broadcast(P), [rgtp[which]])
        p, rp = PA()
        for kc in range(8):
            if sample:
                mm(p[0:64, :], scx[:, kc, 1:65], slot_v[:, kc, :], kc == 0, kc == 7, [rslot, rscx], [rp])
            else:
                mm(p[:, :], screp(kc)[0], slot_v[:, kc, :], kc == 0, kc == 7, [rslot, screp(kc)[1]], [rp])
        tt(dst, dst, p[0:P, :], ALU.add, [rp, rgtp[which]], [rgtp[which]])

    def ada_group(g):
        sv, rs = load_slot(w_ada[:, g * 512:(g + 1) * 512], 8, 512)
        sec = g // 2
        if sec == 0:
            ada_feat(sv, rs, g, 1, False)
        elif sec == 1:
            ada_feat(sv, rs, g, 0, True)
        elif sec == 2:
            ada_tok(sv, rs, g, 0, g % 2, False)
        elif sec == 3:
            ada_feat(sv, rs, g, 3, False)
        elif sec == 4:
            ada_feat(sv, rs, g, 2, True)
        else:
            ada_tok(sv, rs, g, 1, g % 2, False)

    for g in range(4):
        ada_group(g)
    ADA_LATER = {0: (4, 5), 1: (6, 7, 8, 9), 2: (10, 11)}

    marks = []

    def mark(label):
        marks.append((label, len(B.ops["pe"]), len(B.ops["act"]), len(B.ops["dve"])))

    def finish():
        mark("end")
        if os.environ.get("KMARKS"):
            import json
            json.dump(marks, open(os.environ["KMARKS"], "w"))
        B.finalize()
        B.close()
        import sys
        print("ops:", {e: len(B.ops[e]) for e in B.ENGS}, "needed:", {e: sum(o.needed for o in B.ops[e]) for e in B.ENGS},
              "dma sems:", len(B.dma_sems), file=sys.stderr)
        return nc

    if stage == 0:
        for g in range(4, 12):
            ada_group(g)
        st(y_s[:, :], gtp[0:64, 0, :], [rgtp[0]])
        st(y_o[0:128, :], gtp[:, 1, :], [rgtp[1]])
        st(k_s[:, 0:260].rearrange("p (a b) -> p a b", a=4), ABc[0:64, :, 0, :], [rAB])
        return finish()

    def rstd_from_ss(out_ap, ss_ap, n, reads, writes):
        act(out_ap, ss_ap, AF.Ln, reads, writes, scale=1.0 / n, bias=epsc[0:ss_ap.shape[0], :])
        act(out_ap, out_ap, AF.Exp, writes, writes, scale=-0.5)

    def norm_transpose(P, xt, rxt, t, abi, col0, ncol):
        sb_, rs_ = stat.next()
        act(xnbf[0:P, :], xt, AF.Square, [rxt], [rxn, rs_], accum_out=sb_[0:P, 0:1])
        rstd_from_ss(sb_[0:P, 1:2], sb_[0:P, 0:1], 1024, [rs_, rcols], [rs_])
        ts(xnbf[0:P, :], xt, sb_[0:P, 1:2], None, ALU.mult, None, [rxt, rs_], [rxn])
        for kc in range(8):
            tr(psT[:, kc * 128:kc * 128 + P], xnbf[0:P, kc * 128:(kc + 1) * 128], ident[0:P, 0:P], [rxn, rident], [rT])
        pv = psT[:, :].rearrange("p (a b) -> p a b", a=8)[:, :, 0:P]
        if ncol == 1:
            A = ABc[:, abi, :, col0:col0 + 1].to_broadcast([128, 8, P])
            Bb = ABc[:, abi + 1, :, col0:col0 + 1].to_broadcast([128, 8, P])
        else:
            A = ABc[:, abi, :, col0:col0 + P]
            Bb = ABc[:, abi + 1, :, col0:col0 + P]
        for hf in range(2):
            tmp, rtmp = t32.next()
            tv = tmp[:, 0:4 * P].rearrange("p (a b) -> p a b", a=4)
            tt(tv, pv[:, hf * 4:(hf + 1) * 4, :], A[:, hf * 4:(hf + 1) * 4, :], ALU.mult, [rT, rAB], [rtmp])
            tt(uT[:, hf * 4:(hf + 1) * 4, t * 128:t * 128 + P], tv, Bb[:, hf * 4:(hf + 1) * 4, :], ALU.add, [rtmp, rAB], [ruT])

    def qk_post_gen(P, ps_ap, rps, gi, rope_t, k_dst, q_dst, ktile, t, dst_reg=None):
        sq, rsq = t32.next()
        qn, rqn = t32.next()
        sb_, rs_ = stat.next()
        act(sq[0:P, :], ps_ap, AF.Square, [rps], [rsq])
        yield
        B.op("dve", lambda h: h.tensor_reduce(out=sb_[0:P, 0:8], in_=sq[0:P, :].rearrange("p (g d) -> p g d", d=64),
                                              axis=AX.X, op=ALU.add), [rsq], [rs_])
        yield
        act(sb_[0:P, 8:16], sb_[0:P, 0:8], AF.Ln, [rs_, rcols], [rs_], scale=1.0 / 64, bias=epsc[0:P, :])
        yield
        act(sb_[0:P, 8:16], sb_[0:P, 8:16], AF.Exp, [rs_], [rs_], scale=-0.5)
        yield
        qv = qn[0:P, :].rearrange("p (g d) -> p g d", d=64)
        tt(qv, ps_ap.rearrange("p (g d) -> p g d", d=64), sb_[0:P, 8:16].unsqueeze(2).to_broadcast([P, 8, 64]), ALU.mult,
           [rps, rs_], [rqn])
        yield
        tt(qv, qv, gq64[0:P, gi, :].unsqueeze(1).to_broadcast([P, 8, 64]), ALU.mult, [rqn, rgqk], [rqn])
        yield
        cosb = ropet[0:P, rope_t, 0:8].unsqueeze(1).to_broadcast([P, 8, 8])
        sinb = ropet[0:P, rope_t, 8:16].unsqueeze(1).to_broadcast([P, 8, 8])
        tv = sq[0:P, 0:256].rearrange("p (a g d) -> p a g d", a=4, d=8)
        r1, r2 = qv[:, :, 0:8], qv[:, :, 8:16]
        tt(tv[:, 0], r1, cosb, ALU.mult, [rqn, rrope], [rsq])
        yield
        tt(tv[:, 1], r2, sinb, ALU.mult, [rqn, rrope], [rsq])
        yield
        tt(tv[:, 2], r1, sinb, ALU.mult, [rqn, rrope], [rsq])
        yield
        tt(tv[:, 3], r2, cosb, ALU.mult, [rqn, rrope], [rsq])
        yield
        tt(r1, tv[:, 0], tv[:, 1], ALU.subtract, [rsq], [rqn])
        yield
        tt(r2, tv[:, 2], tv[:, 3], ALU.add, [rsq], [rqn])
        yield
        if k_dst is not None:
            st(k_dst, qn[0:P, :], [rqn])
        cp(qkbf[0:P, :], qn[0:P, :], [rqn], [rqkbf])
        for hh in range(4):
            tr(psT[:, hh * 128:hh * 128 + P], qkbf[0:P, hh * 128:(hh + 1) * 128], ident[0:P, 0:P], [rqkbf, rident], [rT])
        pv = psT[:, 0:512].rearrange("p (a b) -> p a b", a=4)[:, :, 0:P]
        if q_dst == "blk":
            for c in range(2):
                cp(QT[c * 64:(c + 1) * 64, :, c, t * 128:t * 128 + P], pv[c * 64:(c + 1) * 64, :, :], [rT], [rQT])
        elif q_dst is not None:
            cp(q_dst, pv, [rT], [dst_reg if dst_reg is not None else rQT])
        else:
            cp(KT[:, :, ktile * 128:ktile * 128 + P], pv, [rT], [rKT[ktile]])

    def lockstep(gens):
        gens = list(gens)
        while gens:
            nxt = []
            for g in gens:
                try:
                    next(g)
                    nxt.append(g)
                except StopIteration:
                    pass
            gens = nxt

    def qk_post(*a, **k):
        lockstep([qk_post_gen(*a, **k)])

    def lru_chunk_gen(ch, T, slots, rslots, do_out, sample, moT_base):
        NS, TT = (16, 4) if sample else (1, T)
        lxv, rlx = slots[0], rslots[0]
        p, rp = PB()
        for kc in range(8):
            mm(p[:, 0:T], lxv[:, kc, ch * 128:(ch + 1) * 128], uT[:, kc, 0:T], kc == 0, kc == 7, [rlx, ruT], [rp])
        p3 = p[:, 0:T].rearrange("p (s t) -> p s t", s=NS)
        xc, rxc = t32.next()
        rg, rrg = t32.next()
        ig, rig = t32.next()
        x3 = xc[:, 0:T].rearrange("p (s t) -> p s t", s=NS)
        yield
        act(xc[:, 0:T], p[:, 0:T], AF.Identity, [rp, rlruc], [rxc], scale=lruc[:, ch, 3:4], bias=lruc[:, ch, 4:5])
        yield
        for k, sh in ((2, 1), (1, 2), (0, 3)):
            stt(x3[:, :, sh:TT], p3[:, :, 0:TT - sh], lruc[:, ch, k:k + 1], x3[:, :, sh:TT], ALU.mult, ALU.add,
                [rp, rlruc, rxc], [rxc])
            yield
        hv = hist[:, ch, 0:NS, :]
        stt(x3[:, :, 0:3], hv[:, :, 0:3], lruc[:, ch, 0:1], x3[:, :, 0:3], ALU.mult, ALU.add, [rhist, rlruc, rxc], [rxc])
        stt(x3[:, :, 0:2], hv[:, :, 1:3], lruc[:, ch, 1:2], x3[:, :, 0:2], ALU.mult, ALU.add, [rhist, rlruc, rxc], [rxc])
        stt(x3[:, :, 0:1], hv[:, :, 2:3], lruc[:, ch, 2:3], x3[:, :, 0:1], ALU.mult, ALU.add, [rhist, rlruc, rxc], [rxc])
        if sample:
            act(hv, p3[:, :, 1:4], AF.Identity, [rp], [rhist])
        else:
            act(hv, p3[:, :, TT - 3:TT], AF.Identity, [rp], [rhist])
        yield
        xb, rxb = Et.next()
        cp(xb[:, 0:T], xc[:, 0:T], [rxc], [rxb])
        yield
        pr, rpr = PB()
        mm(pr[:, 0:T], Wg[:, ch, :], xb[:, 0:T], True, True, [rWg, rxb], [rpr])
        act(rg[:, 0:T], pr[:, 0:T], AF.Sigmoid, [rpr, rlruc], [rrg], bias=lruc[:, ch, 5:6])
        yield
        pi, rpi = PB()
        mm(pi[:, 0:T], Wg[:, 4 + ch, :], xb[:, 0:T], True, True, [rWg, rxb], [rpi])
        act(ig[:, 0:T], pi[:, 0:T], AF.Sigmoid, [rpi, rlruc], [rig], bias=lruc[:, ch, 6:7])
        yield
        tt(ig[:, 0:T], ig[:, 0:T], xc[:, 0:T], ALU.mult, [rig, rxc], [rig])
        yield
        act(rg[:, 0:T], rg[:, 0:T], AF.Exp, [rrg, rlruc], [rrg], scale=lruc[:, ch, 7:8])
        yield
        tt(xc[:, 0:T], rg[:, 0:T], rg[:, 0:T], ALU.mult, [rrg], [rxc])
        yield
        act(xc[:, 0:T], xc[:, 0:T], AF.Ln, [rxc, rcols], [rxc], scale=-1.0, bias=cols[:, C_ONE:C_ONE + 1])
        yield
        act(xc[:, 0:T], xc[:, 0:T], AF.Exp, [rxc], [rxc], scale=0.5)
        yield
        tt(ig[:, 0:T], ig[:, 0:T], xc[:, 0:T], ALU.mult, [rig, rxc], [rig])
        yield
        a3 = rg[:, 0:T].rearrange("p (s t) -> p s t", s=NS)
        b3 = ig[:, 0:T].rearrange("p (s t) -> p s t", s=NS)
        if sample:
            h0 = carry[:, ch, :].unsqueeze(2)
            tmpc, rtmpc = stat.next()
            tt(tmpc[:, 0:16].unsqueeze(2), a3[:, :, 0:1], h0, ALU.mult, [rrg, rcarry], [rtmpc])
            tt(b3[:, :, 0:1], b3[:, :, 0:1], tmpc[:, 0:16].unsqueeze(2), ALU.add, [rig, rtmpc], [rig])
            memset(a3[:, :, 0:1], 0.0, [rrg])
            B.op("dve", lambda h, o=xc, a=rg, b=ig: h.tensor_tensor_scan(out=o[:, 0:T], data0=a[:, 0:T], data1=b[:, 0:T],
                                                                          initial=0.0, op0=ALU.mult, op1=ALU.add),
                 [rrg, rig], [rxc])
            cp(carry[:, ch, :].unsqueeze(2), x3[:, :, 3:4], [rxc], [rcarry])
        else:
            B.op("dve", lambda h, o=xc, a=rg, b=ig, c=ch: h.tensor_tensor_scan(out=o[:, 0:T], data0=a[:, 0:T], data1=b[:, 0:T],
                                                                                initial=carry[:, c, 0:1], op0=ALU.mult, op1=ALU.add),
                 [rrg, rig, rcarry], [rxc])
            cp(carry[:, ch, 0:1], xc[:, T - 1:T], [rxc], [rcarry])
        yield
        if do_out:
            lgv, rlg = slots[1], rslots[1]
            p2, rp2 = PB()
            for kc in range(8):
                mm(p2[:, 0:T], lgv[:, kc, ch * 128:(ch + 1) * 128], uT[:, kc, 0:T], kc == 0, kc == 7, [rlg, ruT], [rp2])
            gl, rgl = rg, rrg
            act(gl[:, 0:T], p2[:, 0:T], AF.Gelu_apprx_tanh, [rp2], [rgl])
            yield
            mo, rmo = chunks[moT_base + ch]
            tt(mo[:, 0:T], xc[:, 0:T], gl[:, 0:T], ALU.mult, [rxc, rgl], [rmo])

    def lru_chunks(T, nseq, slots, rslots, do_out, sample, moT_base):
        for pair in ((0, 1), (2, 3)):
            lockstep([lru_chunk_gen(ch, T, slots, rslots, do_out, sample, moT_base) for ch in pair])

    def sub_ln_to_moT(P, at, rat, nq, head, defer=False):
        sq, rsq = t32.next()
        act(sq[0:P, 0:nq * 128], at[0:P, 0:nq * 128], AF.Square, [rat], [rsq])
        sb_, rs_ = stat.next()
        B.op("dve", lambda h: h.tensor_reduce(out=sb_[0:P, 0:nq], in_=sq[0:P, 0:nq * 128].rearrange("p (g d) -> p g d", d=128),
                                              axis=AX.X, op=ALU.add), [rsq], [rs_])
        rstd_from_ss(sb_[0:P, 8:8 + nq], sb_[0:P, 0:nq], 128, [rs_, rcols], [rs_])
        a3 = at[0:P, 0:nq * 128].rearrange("p (g d) -> p g d", d=128)
        tt(a3, a3, sb_[0:P, 8:8 + nq].unsqueeze(2).to_broadcast([P, nq, 128]), ALU.mult, [rat, rs_], [rat])
        tt(attnbf[0:P, 0:nq * 128].rearrange("p (g d) -> p g d", d=128), a3,
           gsub[0:P, :].unsqueeze(1).to_broadcast([P, nq, 128]), ALU.mult, [rat, rgqk], [rattnbf])
        if defer:
            return
        sub_ln_tail(P, nq, head)

    def sub_ln_tail(P, nq, head):
        for qs in range(nq):
            tr(psT[:, qs * 128:qs * 128 + P], attnbf[0:P, qs * 128:(qs + 1) * 128], ident[0:P, 0:P], [rattnbf, rident], [rT])
        mo, rmo = chunks[head]
        if nq == 4:
            cp(mo[:, 0:512], psT[:, 0:512], [rT], [rmo])
        else:
            cp(mo[:, 0:P], psT[:, 0:P], [rT], [rmo])

    def attn_prompt(full_tiles, diag_tiles):
        Ov = [psB[:, 2 * c:2 * c + 2, :].rearrange("p a (s w) -> p (a s) w", w=256) for c in range(2)]
        rO = [[rB[0], rB[1]], [rB[2], rB[3]]]
        sbanks = [psA[0], psA[1], (psB[:, 4, :], rB[4])]
        LOOK = 2
        for hh in range(4):
            seq = [(kt, bc, None) for kt, bc in full_tiles] + [(kt, zeroc, j) for j, kt in enumerate(diag_tiles)]
            steps = [(n, kt, bc, dj, c) for n, (kt, bc, dj) in enumerate(seq) for c in range(2)]
            inflight = {}

            def s_issue(i):
                n, kt, bc, dj, c = steps[i]
                q0 = 0 if dj is None else dj * 128
                p, rp = sbanks[i % 3]
                mm(p[:, q0:512], KT[:, hh, kt * 128:(kt + 1) * 128], QT[:, hh, c, q0:512],
                   True, True, [rKT[kt], rQT], [rp])
                e, re = Et.next()
                act(e[:, q0:512], p[:, q0:512], AF.Exp, [rp, rcols], [re], scale=0.125, bias=bc)
                if dj is not None:
                    tt(e[:, q0:q0 + 128], e[:, q0:q0 + 128], tri[:, :], ALU.mult, [re, rtri], [re])
                inflight[i] = (e, re)

            def pv_issue(i):
                n, kt, bc, dj, c = steps[i]
                q0 = 0 if dj is None else dj * 128
                e, re = inflight.pop(i)
                for qs in range(q0 // 128, 4):
                    last = (dj is not None and qs == dj and qs % 2 == 1)
                    mm(Ov[c][:, qs, 0:129], e[:, qs * 128:(qs + 1) * 128], VA[:, kt, hh, 0:129], (n == 0 and qs % 2 == 0), last,
                       [re, rVA[kt]], [rO[c][qs // 2]])

            for i in range(len(steps) + LOOK):
                if i < len(steps):
                    s_issue(i)
                if i - LOOK >= 0:
                    pv_issue(i - LOOK)
            if hh > 0:
                sub_ln_tail(128, 4, hh - 1)
            sb_, rs_ = stat.next()
            B.op("dve", lambda h, s=sb_: h.reciprocal(out=s[:, 0:4].unsqueeze(2), in_=Ov[0][:, :, 128:129]), rO[0], [rs_])
            B.op("dve", lambda h, s=sb_: h.reciprocal(out=s[:, 4:8].unsqueeze(2), in_=Ov[1][:, :, 128:129]), rO[1], [rs_])
            ts(sb_[:, 4:8], sb_[:, 4:8], cols[:, C_NLAM:C_NLAM + 1], None, ALU.mult, None, [rs_, rcols], [rs_])
            at, rat = t32.next()
            a3 = at[:, :].rearrange("p (g d) -> p g d", d=128)
            tt(a3, Ov[0][:, :, 0:128], sb_[:, 0:4].unsqueeze(2).to_broadcast([128, 4, 128]), ALU.mult, rO[0] + [rs_], [rat])
            t2, rt2 = t32.next()
            b3 = t2[:, :].rearrange("p (g d) -> p g d", d=128)
            tt(b3, Ov[1][:, :, 0:128], sb_[:, 4:8].unsqueeze(2).to_broadcast([128, 4, 128]), ALU.mult, rO[1] + [rs_], [rt2])
            tt(at[:, :], at[:, :], t2[:, :], ALU.add, [rat, rt2], [rat])
            sub_ln_to_moT(128, at, rat, 4, hh, defer=True)
        sub_ln_tail(128, 4, 3)

    def out_proj_residual(P, ntile, gt_ap_fn, rgt):
        for nh in range(2):
            sv, rs = load_slot(w_out[:, nh * 512:(nh + 1) * 512], 8, 512)
            for t in range(ntile):
                p, rp = PB()
                for fc in range(8):
                    mo, rmo = chunks[fc]
                    mm(p[0:P, :], mo[:, t * 128:t * 128 + P], sv[:, fc, :], fc == 0, fc == 7, [rmo, rs], [rp])
                tmp, rtmp = t32.next()
                tt(tmp[0:P, :], p[0:P, :], gt_ap_fn(0, nh), ALU.mult, [rp, rgt[0]], [rtmp])
                xs_ = xblk[0:P, t, nh * 512:(nh + 1) * 512]
                tt(xs_, xs_, tmp[0:P, :], ALU.add, [rx[t], rtmp], [rx[t]])

    def ffn_block(P, ntile, T, abcol0, abncol, gt_ap_fn, rgt, sample, full, y_dst_fn, only_last_tile=False):
        NS, TT = (16, 4) if sample else (1, T)
        if only_last_tile:
            norm_transpose(128, xblk[:, ntile - 1, :], rx[ntile - 1], 0, 2, abcol0, abncol)
            T = 128; TT = 128
        else:
            for t in range(ntile):
                norm_transpose(P, xblk[0:P, t, :], rx[t], t, 2, abcol0, abncol)
        segs = [(0, 4), (4, 4), (8, 4), (12, 4), (16, 4), (20, 2)]
        if not sample:
            tt(btile[:, :, 0], fcw[:, :, 0], fhist[:, :, 0, 0], ALU.mult, [rfcw, rfhist], [rbt])
            tt(small[:, 0:44], fcw[:, :, 1], fhist[:, :, 0, 1], ALU.mult, [rfcw, rfhist], [rsmall])
            tt(btile[:, :, 0], btile[:, :, 0], small[:, 0:44], ALU.add, [rbt, rsmall], [rbt])
            tt(btile[:, :, 1], fcw[:, :, 0], fhist[:, :, 0, 1], ALU.mult, [rfcw, rfhist], [rbt])
        for s0, n in segs:
            sg, rsg = load_slot(w_up[:, s0 * 128:(s0 + n) * 128], 8, n * 128)
            sv_, rsv = load_slot(w_up[:, 2816 + s0 * 128:2816 + (s0 + n) * 128], 8, n * 128)
            for j in range(n):
                c = s0 + j
                hcs = []
                for which, (sl, rsl, fidx) in enumerate(((sg, rsg, c), (sv_, rsv, 22 + c))):
                    p, rp = PB()
                    if only_last_tile:
                        for kc in range(8):
                            mm(p[:, 0:2], sl[:, kc, j * 128:(j + 1) * 128], uT[:, kc, 126:128], kc == 0, kc == 7, [rsl, ruT], [rp])
                        act(fhist[:, fidx, 0:1, :], p[:, 0:2].unsqueeze(1), AF.Identity, [rp], [rfhist])
                        continue
                    for kc in range(8):
                        mm(p[:, 0:T], sl[:, kc, j * 128:(j + 1) * 128], uT[:, kc, 0:T], kc == 0, kc == 7, [rsl, ruT], [rp])
                    p3 = p[:, 0:T].rearrange("p (s t) -> p s t", s=NS)
                    hc, rhc = t32.next()
                    h3 = hc[:, 0:T].rearrange("p (s t) -> p s t", s=NS)
                    act(hc[:, 0:T], p[:, 0:T], AF.Identity, [rp, rfcw], [rhc], scale=fcw[:, fidx, 2:3], bias=fcw[:, fidx, 3:4])
                    stt(h3[:, :, 1:TT], p3[:, :, 0:TT - 1], fcw[:, fidx, 1:2], h3[:, :, 1:TT], ALU.mult, ALU.add, [rp, rfcw, rhc], [rhc])
                    stt(h3[:, :, 2:TT], p3[:, :, 0:TT - 2], fcw[:, fidx, 0:1], h3[:, :, 2:TT], ALU.mult, ALU.add, [rp, rfcw, rhc], [rhc])
                    fh = fhist[:, fidx, 0:NS, :]
                    if sample:
                        stt(h3[:, :, 0:2], fh[:, :, 0:2], fcw[:, fidx, 0:1], h3[:, :, 0:2], ALU.mult, ALU.add, [rfhist, rfcw, rhc], [rhc])
                        stt(h3[:, :, 0:1], fh[:, :, 1:2], fcw[:, fidx, 1:2], h3[:, :, 0:1], ALU.mult, ALU.add, [rfhist, rfcw, rhc], [rhc])
                    else:
                        tt(hc[:, 0:2], hc[:, 0:2], btile[:, fidx, :], ALU.add, [rhc, rbt], [rhc])
                    act(fh, p3[:, :, TT - 2:TT], AF.Identity, [rp], [rfhist])
                    hcs.append((hc, rhc))
                if full:
                    (hg, rhg), (hv, rhv) = hcs
                    act(hg[:, 0:T], hg[:, 0:T], AF.Silu, [rhg], [rhg])
                    a_, ra_ = chunks[c]
                    tt(a_[:, 0:T], hg[:, 0:T], hv[:, 0:T], ALU.mult, [rhg, rhv], [ra_])
        if not full:
            return
        for nh in range(2):
            accs = [PB() for _ in range(ntile)]
            for s0, n in ((0, 8), (8, 8), (16, 6)):
                sd, rsd = load_slot(w_down[s0 * 128:(s0 + n) * 128, nh * 512:(nh + 1) * 512], n, 512)
                for t in range(ntile):
                    p, rp = accs[t]
                    for j in range(n):
                        c = s0 + j
                        a_, ra_ = chunks[c]
                        mm(p[0:P, :], a_[:, t * 128:t * 128 + P], sd[:, j, :], c == 0, c == 21, [ra_, rsd], [rp])
            for t in range(ntile):
                p, rp = accs[t]
                tmp, rtmp = t32.next()
                tt(tmp[0:P, :], p[0:P, :], gt_ap_fn(1, nh), ALU.mult, [rp, rgt[1]], [rtmp])
                xs_ = xblk[0:P, t, nh * 512:(nh + 1) * 512]
                tt(xs_, xs_, tmp[0:P, :], ALU.add, [rx[t], rtmp], [rx[t]])
                if nh == 1:
                    st(y_dst_fn(t), xblk[0:P, t, :], [rx[t]])

    def gtp_fn(which, nh):
        return gtp[:, which, nh * 512:(nh + 1) * 512]

    def gts_fn(which, nh):
        return gtp[0:64, which, nh * 512:(nh + 1) * 512]

    def prompt_block(kind, bi):
        mark(f"{kind}{bi}:start")
        src = xo if kind == "own" else xp
        r0 = bi * 512
        do_q = kind in ("own", "halo")
        tile0 = (16 + bi * 4) if kind == "own" else bi * 4
        for t in range(4):
            ld(xblk[:, t, :], src[r0 + t * 128:r0 + (t + 1) * 128, :], [rx[t]])
        for t in range(4):
            norm_transpose(128, xblk[:, t, :], rx[t], t, 0, 0, 1)
        groups = ([0] if do_q else []) + [1, 2]
        for cg in groups:
            sv, rs = load_slot(w_in[:, cg * 512:(cg + 1) * 512], 8, 512)
            if cg in (0, 1):
                for pair in ((0, 1, 2), (3,)):
                    gens = []
                    for t in pair:
                        p, rp = PB()
                        for kc in range(8):
                            mm(p[:, :], uT[:, kc, t * 128:(t + 1) * 128], sv[:, kc, :], kc == 0, kc == 7, [ruT, rs], [rp])
                        kt = tile0 + t
                        if cg == 0:
                            gens.append(qk_post_gen(128, p[:, :], rp, 0, kt, None, "blk", None, t))
                        else:
                            kd = k_o[r0 + t * 128:r0 + (t + 1) * 128, :] if kind == "own" else None
                            gens.append(qk_post_gen(128, p[:, :], rp, 1, kt, kd, None, kt, t))
                    lockstep(gens)
                continue
            for t in range(4):
                p, rp = PB()
                for kc in range(8):
                    mm(p[:, :], uT[:, kc, t * 128:(t + 1) * 128], sv[:, kc, :], kc == 0, kc == 7, [ruT, rs], [rp])
                kt = tile0 + t
                if kind == "own":
                    v32, rv32 = t32.next()
                    cp(v32[:, :], p[:, :], [rp], [rv32])
                    st(v_o[r0 + t * 128:r0 + (t + 1) * 128, :], v32[:, :], [rv32])
                cp(VA[:, kt, :, 0:128], p[:, :].rearrange("p (a b) -> p a b", a=4), [rp], [rVA[kt]])
        mark(f"{kind}{bi}:lru")
        lxs, rlxs = load_slot(w_in[:, 1536:2048], 8, 512)
        if do_q:
            lgs, rlgs = load_slot(w_in[:, 2048:2560], 8, 512)
            lru_chunks(512, 1, (lxs, lgs), (rlxs, rlgs), True, False, 4)
        else:
            lru_chunks(512, 1, (lxs, None), (rlxs, None), False, False, 4)
        if not do_q:
            return
        if kind == "halo":
            full = [(kt, pbiasc) for kt in range(0, 12)]
            diag = [12, 13, 14, 15]
        else:
            full = [(kt, pbiasc) for kt in range(0, 16)] + [(kt, zeroc) for kt in range(16, 16 + bi * 4)]
            diag = [16 + bi * 4 + j for j in range(4)]
        mark(f"{kind}{bi}:attn")
        attn_prompt(full, diag)
        mark(f"{kind}{bi}:outproj")
        out_proj_residual(128, 4, gtp_fn, rgtp)
        mark(f"{kind}{bi}:ffn")
        if kind == "halo":
            ffn_block(128, 4, 512, 0, 1, gtp_fn, rgtp, False, False, None, only_last_tile=True)
            ts(carry[:, :, 0:1], carry[:, :, 0:1], pmulc, None, ALU.mult, None, [rcarry, rcols], [rcarry])
            ts(hist[:, :, 0, :], hist[:, :, 0, :], pmulc, None, ALU.mult, None, [rhist, rcols], [rhist])
            ts(fhist[:, :, 0, :], fhist[:, :, 0, :], pmulc, None, ALU.mult, None, [rfhist, rcols], [rfhist])
        elif stage == 31:
            pass
        elif stage == 32:
            ffn_block(128, 4, 512, 0, 1, gtp_fn, rgtp, False, False, None)
        else:
            ffn_block(128, 4, 512, 0, 1, gtp_fn, rgtp, False, True, lambda t: y_o[r0 + t * 128:r0 + (t + 1) * 128, :])

    for bi in range(3):
        prompt_block("pre", bi)
        for g in ADA_LATER[bi]:
            ada_group(g)
        if stage == 1:
            return finish()
    prompt_block("halo", 3)
    if stage == 2:
        return finish()
    for bi in range(4):
        prompt_block("own", bi)
        if stage in (3, 31, 32):
            return finish()
    if stage == 4:
        return finish()
    st(lc_p, hist[:, :, 0, :], [rhist]); st(lh_p, carry[:, :, 0], [rcarry]); st(fc_p, fhist[:, :, 0, :], [rfhist])

    mark("sample:start")
    for g in (4, 5, 10, 11):
        sv, rs = load_slot(w_ada[:, g * 512:(g + 1) * 512], 8, 512)
        ada_tok(sv, rs, g, 0 if g < 6 else 1, g % 2, True)
    ld(hist[:], slc, [rhist]); ld(carry[:], slh, [rcarry]); ld(fhist[:], sfc, [rfhist])
    ld(xblk[0:64, 0, :], xs, [rx[0]])
    norm_transpose(64, xblk[0:64, 0, :], rx[0], 0, 0, 1, 64)
    allK = rKT + rVA
    rKpTs = [B.R("KpT0"), B.R("KpT1")]; rvss = [[B.R(f"vs{i}_{g}") for g in range(4)] for i in range(2)]
    rkg = [B.R(f"kg{i}") for i in range(2)]; rQsb = B.R("Qsb"); rKsT = B.R("KsT"); rVs = B.R("VAs")
    for rr in [rKpTs[0], rKpTs[1]] + rvss[0] + rvss[1]:
        for o_ in allK:
            if o_.last_w is not None:
                rr.readers.append(o_.last_w)
            rr.readers.extend(o_.readers)
    KpTs = [KT[:, :, 0:2048], KT[:, :, 2048:4096]]
    VAf = VA[:, :, :, :].rearrange("p a b c -> p (a b c)")
    vss = [VAf[:, i * 8192:(i + 1) * 8192].rearrange("p (g f) -> p g f", f=512) for i in range(2)]
    smpbuf = B.sb("smpbuf", [128, 5376], BF16)
    kg = [smpbuf[:, i * 2048:(i + 1) * 2048].rearrange("p (g f) -> p g f", f=512) for i in range(2)]
    Qsb = smpbuf[:, 4096:4096 + 512].rearrange("p (a s c) -> p a s c", a=4, s=16)
    KsT = smpbuf[:, 4608:4608 + 256].rearrange("p (a t) -> p a t", a=4)
    VAs = smpbuf[:, 4864:4864 + 512].rearrange("p (a d) -> p a d", a=4)
    QsT = QsTt[:, :, :]
    for cg in range(3):
        sv, rs = load_slot(w_in[:, cg * 512:(cg + 1) * 512], 8, 512)
        p, rp = PB()
        for kc in range(8):
            mm(p[0:64, :], uT[:, kc, 0:64], sv[:, kc, :], kc == 0, kc == 7, [ruT, rs], [rp])
        if cg == 0:
            qk_post(64, p[0:64, :], rp, 0, 32, None, QsTt[:, :, :], None, 0)
        elif cg == 1:
            sq_k = k_s
            qk_post(64, p[0:64, :], rp, 1, 32, sq_k, KsT, None, 0, dst_reg=rKsT)
        else:
            v32, rv32 = t32.next()
            cp(v32[0:64, :], p[0:64, :], [rp], [rv32])
            st(v_s, v32[0:64, :], [rv32])
            cp(VAs[0:64, :, :], p[0:64, :].rearrange("p (a b) -> p a b", a=4), [rp], [rVs])
    lxs, rlxs = load_slot(w_in[:, 1536:2048], 8, 512)
    lgs, rlgs = load_slot(w_in[:, 2048:2560], 8, 512)
    lru_chunks(64, 16, (lxs, lgs), (rlxs, rlgs), True, True, 4)
    st(lc_s, hist[:], [rhist]); st(lh_s, carry[:], [rcarry])
    memset(Qsb, 0.0, [rQsb])
    for c in range(2):
        cp(Qsb[c * 64:(c + 1) * 64, :, :, c * 4:(c + 1) * 4], QsT[c * 64:(c + 1) * 64, :, :].rearrange("p a (s t) -> p a s t", t=4),
           [rQT], [rQsb])
    ptt_, rpti = t32.next()
    pti = ptt_[:, 0:256].bitcast(I32)
    ptf_, rptf = t32.next()
    ptf = ptf_[:, 0:256]
    idx = B.sb("idx", [128, 64], I32); ridx = B.R()
    gselt = B.sb("gselt", [128, 5], F32); rgsel = B.R()
    ld(pti[:], ptab.partition_broadcast(128), [rpti])
    ld(gselt[:], gsel, [rgsel])
    cp(ptf[:], pti[:], [rpti], [rptf])
    ptf3 = ptf.rearrange("p (a j) -> p a j", j=4)
    i2f = ptf_[:, 256:320]
    ts(i2f, ptf3[:, :, 0], gselt[:, 0:1], None, ALU.mult, None, [rptf, rgsel], [rptf])
    for j in range(1, 4):
        stt(i2f, ptf3[:, :, j], gselt[:, j:j + 1], i2f, ALU.mult, ALU.add, [rptf, rgsel], [rptf])
    ts(i2f, i2f, 32.0, gselt[:, 4:5], ALU.mult, ALU.add, [rptf, rgsel], [rptf])
    cp(idx[:], i2f, [rptf], [ridx])
    cache_k4 = cache_k.rearrange("(r f) d -> r (f d)", f=4)
    cache_v4 = cache_v.rearrange("(r f) d -> r (f d)", f=4)
    msk = B.sb("msk", [64, 16, 8], F32); rmsk = B.R()
    cmbt = B.sb("cmbt", [8, 124], F32); rcmb = B.R()
    ld(msk[:], smask, [rmsk]); ld(cmbt[:], cmb, [rcmb])
    ones_bf = B.sb("ones_bf", [128, 1], BF16); rones = B.R()
    memset(ones_bf[:], 1.0, [rones])
    lamc8 = B.sb("lamc8", [8, 1], F32); rlamc8 = B.R()
    memset(lamc8[:], 1.0, [rlamc8])
    cp(lamc8[0:8, :], cols[0:8, C_NLAM:C_NLAM + 1], [rcols], [rlamc8])
    memset(lamc8[0:4, :], 1.0, [rlamc8])
    att_all, ratt = psB[:, 4, :], rB[4]
    pb4 = [0]

    def PB4():
        k = pb4[0] % 3
        pb4[0] += 1
        return psB[:, k, :], rB[k]
    psT2 = psB[:, 3, :].bitcast(BF16)
    tbufs = [(psT, rT), (psT2, rB[3])]
    mark("sample:seqs")
    for s in range(16):
        KpT, rKpT = KpTs[s % 2], rKpTs[s % 2]
        vsb, rvs = vss[s % 2], rvss[s % 2]
        for g in range(4):
            kb, rkb = kg[g % 2], rkg[g % 2]
            B.dma("pool", lambda h, kb=kb, col=s * 4 + g: h.indirect_dma_start(
                out=kb.rearrange("p a f -> p (a f)"), out_offset=None, in_=cache_k4[:, :],
                in_offset=bass.IndirectOffsetOnAxis(ap=idx[:, col:col + 1], axis=0)), [ridx], [rkb])
            for j in range(4):
                pg = g * 4 + j
                tb, rtb = tbufs[pg % 2]
                for hh in range(4):
                    tr(tb[:, hh * 128:(hh + 1) * 128], kb[:, j, hh * 128:(hh + 1) * 128], ident[:, :], [rkb, rident], [rtb])
                cp(KpT[:, :, pg * 128:(pg + 1) * 128], tb[:, 0:512].rearrange("p (a b) -> p a b", a=4), [rtb], [rKpT])
        for g in range(4):
            B.dma("pool", lambda h, g=g, col=s * 4 + g, vsb=vsb: h.indirect_dma_start(
                out=vsb[:, g * 4:(g + 1) * 4, :].rearrange("p a f -> p (a f)"), out_offset=None, in_=cache_v4[:, :],
                in_offset=bass.IndirectOffsetOnAxis(ap=idx[:, col:col + 1], axis=0)), [ridx], [rvs[g]])
        pS, rpS = PA()
        pS4 = pS[:, :].rearrange("p (k a c) -> p k a c", k=16, a=4)
        for pg in range(16):
            for hh in range(4):
                mm(pS4[:, pg, hh, :], KpT[:, hh, pg * 128:(pg + 1) * 128], Qsb[:, hh, s, :], True, True, [rKpT, rQsb], [rpS])
        eS, reS = Et.next()
        eS4 = eS[:, :].rearrange("p (k a c) -> p k a c", k=16, a=4)
        act(eS[:, :], pS[:, :], AF.Exp, [rpS], [reS], scale=0.125)
        pN, rpN = PA()
        for hh in range(4):
            mm(pN[0:64, hh * 8:(hh + 1) * 8], KsT[:, hh, :], Qsb[:, hh, s, :], True, True, [rKsT, rQsb], [rpN])
        eN, reN = Et.next()
        eN32, reN32 = t32.next()
        act(eN32[0:64, 0:32], pN[0:64, 0:32], AF.Exp, [rpN], [reN32], scale=0.125)
        tt(eN[0:64, 0:32].rearrange("p (a c) -> p a c", a=4), eN32[0:64, 0:32].rearrange("p (a c) -> p a c", a=4),
           msk[:, s, :].unsqueeze(1).to_broadcast([64, 4, 8]), ALU.mult, [reN32, rmsk], [reN])
        pO, rpO = PB4()
        pSm, rpSm = PA()
        for hh in range(4):
            for pg in range(16):
                mm(pO[0:8, hh * 128:(hh + 1) * 128], eS4[:, pg, hh, :], vsb[:, pg, hh * 128:(hh + 1) * 128], pg == 0, False,
                   [reS, rvs[pg // 4]], [rpO])
                mm(pSm[0:8, hh:hh + 1], eS4[:, pg, hh, :], ones_bf[:, :], pg == 0, False, [reS, rones], [rpSm])
            mm(pO[0:8, hh * 128:(hh + 1) * 128], eN[0:64, hh * 8:(hh + 1) * 8], VAs[0:64, hh, :], False, True, [reN, rVs], [rpO])
            mm(pSm[0:8, hh:hh + 1], eN[0:64, hh * 8:(hh + 1) * 8], ones_bf[0:64, :], False, True, [reN, rones], [rpSm])
        sb_, rs_ = stat.next()
        B.op("dve", lambda h, s_=sb_, p_=pSm: h.reciprocal(out=s_[0:8, 0:4], in_=p_[0:8, 0:4]), [rpSm], [rs_])
        ts(sb_[0:8, 0:4], sb_[0:8, 0:4], lamc8[:, 0:1], None, ALU.mult, None, [rs_, rlamc8], [rs_])
        osc, rosc = t32.next()
        tt(osc[0:8, :].rearrange("p (a d) -> p a d", a=4), pO[0:8, :].rearrange("p (a d) -> p a d", a=4),
           sb_[0:8, 0:4].unsqueeze(2).to_broadcast([8, 4, 128]), ALU.mult, [rpO, rs_], [rosc])
        mm(att_all[0:64, :], cmbt[:, 60 - 4 * s:124 - 4 * s], osc[0:8, :], s == 0, s == 15, [rcmb, rosc], [ratt])
    mark("sample:tail")
    at, rat = t32.next()
    cp(at[0:64, :], att_all[0:64, :], [ratt], [rat])
    for hh in range(4):
        ah, rah = t32.next()
        cp(ah[0:64, 0:128], at[0:64, hh * 128:(hh + 1) * 128], [rat], [rah])
        sub_ln_to_moT(64, ah, rah, 1, hh)
    out_proj_residual(64, 1, gts_fn, rgtp)
    ffn_block(64, 1, 64, 1, 64, gts_fn, rgtp, True, True, lambda t: y_s[:, :])
    st(fc_s, fhist[:], [rfhist])

    return finish()


_NC_CACHE = {}


def _rope_table(pos):
    half = 8
    freqs = (np.float32(500000.0) ** (-np.arange(half, dtype=np.float32) * np.float32(2.0) / np.float32(16))).astype(np.float32)
    ang = pos.astype(np.float32)[:, None] * freqs[None, :]
    return np.concatenate([np.cos(ang), np.sin(ang)], axis=1).astype(np.float32)


def kernel(x_prompt, x_sample, cache_k, cache_v, page_table, state_lru_conv, state_lru_h, state_ffn_conv,
           c_prompt, c_sample, g_norm1, g_norm2, w_ada, b_ada, w_in, g_q, g_k, lam_q1, lam_k1, lam_q2, lam_k2,
           g_subln, w_out, conv_lru_w, conv_lru_b, w_rgate, b_rgate, w_igate, b_igate, lru_lambda,
           w_up, conv_ffn_w, conv_ffn_b, w_down):
    f32 = np.float32
    A = lambda a: np.ascontiguousarray(np.asarray(a))
    x_prompt = A(x_prompt); x_sample = A(x_sample)
    n_rows = int(np.prod(np.shape(cache_k)[:3]))
    if "nc" not in _NC_CACHE:
        _NC_CACHE["nc"] = build_program(n_rows, _NC_CACHE.get("stage", 99))
    nc = _NC_CACHE["nc"]

    ck = A(cache_k).reshape(-1, 512)
    cv = A(cache_v).reshape(-1, 512)
    featT = lambda v, n: A(np.asarray(v, f32).reshape(n, 128).T)
    shared = {
        "w_ada": A(w_ada[0]), "b_adaT": featT(b_ada[0], 48), "b_ada": A(b_ada[0].reshape(1, 6144)),
        "g1T": featT(g_norm1[0], 8), "g2T": featT(g_norm2[0], 8),
        "w_in": A(w_in[0]), "w_out": A(w_out[0]), "w_up": A(w_up[0]), "w_down": A(w_down[0]),
        "gq_rep": A(np.broadcast_to(np.tile(np.asarray(g_q[0]), 8)[None, :], (128, 512))),
        "gk_rep": A(np.broadcast_to(np.tile(np.asarray(g_k[0]), 8)[None, :], (128, 512))),
        "gsub_rep": A(np.broadcast_to(np.tile(np.asarray(g_subln[0]), 4)[None, :], (128, 512))),
        "lamv": A(np.concatenate([lam_q1[0], lam_k1[0], lam_q2[0], lam_k2[0]]).reshape(1, 256)),
        "clw": A(np.asarray(conv_lru_w[0]).reshape(4, 4, 128).transpose(2, 1, 0)),
        "clb": featT(conv_lru_b[0], 4),
        "w_rg": A(w_rgate[0]), "w_ig": A(w_igate[0]),
        "b_rg": featT(np.asarray(b_rgate[0]).reshape(-1), 4), "b_ig": featT(np.asarray(b_igate[0]).reshape(-1), 4),
        "lru_lam": featT(lru_lambda[0], 4),
        "cfw": A(np.asarray(conv_ffn_w[0]).reshape(3, 44, 128).transpose(2, 1, 0)),
        "cfb": featT(conv_ffn_b[0], 44),
        "cache_k": ck, "cache_v": cv,
    }
    smask = np.zeros((64, 16, 8), f32)
    cmb = np.zeros((8, 124), f32)
    for c in range(2):
        for q in range(4):
            cmb[c * 4 + q, 60 + q] = 1.0
    for s in range(16):
        for j in range(4):
            for c in range(2):
                for q in range(4):
                    if j <= q:
                        smask[s * 4 + j, s, c * 4 + q] = 1.0
    gsel = np.zeros((128, 5), f32)
    for p in range(128):
        gsel[p, p // 32] = 1.0
        gsel[p, 4] = float(p % 32)
    shared["gsel"] = gsel
    shared["smask"] = smask
    shared["cmb"] = cmb

    past_len = page_table.shape[1] * 128
    in_maps = []
    for core in range(8):
        b, h = core // 2, core % 2
        m = dict(shared)
        m["xp"] = A(x_prompt[b, 0:2048])
        m["xo"] = A(x_prompt[b, h * 2048:(h + 1) * 2048])
        m["xs"] = A(x_sample[core * 16:(core + 1) * 16].reshape(64, 1024))
        cvec = np.concatenate([np.asarray(c_prompt[b])[None, :], np.asarray(c_sample[core * 16:(core + 1) * 16])], axis=0)
        m["cT"] = A(cvec.reshape(17, 8, 128).transpose(2, 1, 0))
        rp = _rope_table(np.arange(0, 2048))
        ro = _rope_table(np.arange(h * 2048, (h + 1) * 2048))
        m["ropep"] = A(rp.reshape(16, 128, 16).transpose(1, 0, 2))
        m["ropeo"] = A(ro.reshape(16, 128, 16).transpose(1, 0, 2))
        m["ropes"] = A(np.tile(_rope_table(past_len + np.arange(4)), (16, 1)))
        fl = np.zeros((128, 2), f32)
        fl[:, 0] = 0.0 if h == 1 else -10000.0
        fl[:, 1] = 1.0 if h == 1 else 0.0
        m["flags"] = fl
        sl = slice(core * 16, (core + 1) * 16)
        m["slc"] = A(np.asarray(state_lru_conv[0][sl]).reshape(16, 3, 4, 128).transpose(3, 2, 0, 1))
        m["slh"] = A(np.asarray(state_lru_h[0][sl]).reshape(16, 4, 128).transpose(2, 1, 0))
        m["sfc"] = A(np.asarray(state_ffn_conv[0][sl]).reshape(16, 2, 44, 128).transpose(3, 2, 0, 1))
        m["ptab"] = A(np.asarray(page_table[sl], np.int32).reshape(1, 256))
        in_maps.append(m)

    res = run_bass_kernel_spmd(nc, in_maps, core_ids=list(range(8)))
    R = res.results

    y_p = np.zeros((4, 4096, 1024), f32); k_p = np.zeros((1, 4, 4096, 512), f32); v_p = np.zeros((1, 4, 4096, 512), f32)
    y_s = np.zeros((128, 4, 1024), f32); k_s = np.zeros((1, 128, 4, 512), f32); v_s = np.zeros((1, 128, 4, 512), f32)
    lc_p = np.zeros((1, 4, 3, 512), f32); lh_p = np.zeros((1, 4, 512), f32); fc_p = np.zeros((1, 4, 2, 5632), f32)
    lc_s = np.zeros((1, 128, 3, 512), f32); lh_s = np.zeros((1, 128, 512), f32); fc_s = np.zeros((1, 128, 2, 5632), f32)
    for core in range(8):
        b, h = core // 2, core % 2
        r = R[core]
        y_p[b, h * 2048:(h + 1) * 2048] = r["y_o"]
        k_p[0, b, h * 2048:(h + 1) * 2048] = r["k_o"]
        v_p[0, b, h * 2048:(h + 1) * 2048] = r["v_o"]
        sl = slice(core * 16, (core + 1) * 16)
        y_s[sl] = r["y_s"].reshape(16, 4, 1024)
        k_s[0, sl] = r["k_s"].reshape(16, 4, 512)
        v_s[0, sl] = r["v_s"].reshape(16, 4, 512)
        if h == 1:
            lc_p[0, b] = r["lc_p"].transpose(2, 1, 0).reshape(3, 512)
            lh_p[0, b] = r["lh_p"].T.reshape(512)
            fc_p[0, b] = r["fc_p"].transpose(2, 1, 0).reshape(2, 5632)
        lc_s[0, sl] = r["lc_s"].transpose(2, 3, 1, 0).reshape(16, 3, 512)
        lh_s[0, sl] = r["lh_s"].transpose(2, 1, 0).reshape(16, 512)
        fc_s[0, sl] = r["fc_s"].transpose(2, 3, 1, 0).reshape(16, 2, 5632)
    return (y_p, y_s.reshape(128, 4, 1024), k_p.reshape(1, 4, 4096, 4, 2, 64), v_p.reshape(1, 4, 4096, 4, 128),
            lc_p, lh_p, fc_p, k_s.reshape(1, 128, 4, 4, 2, 64), v_s.reshape(1, 128, 4, 4, 128), lc_s, lh_s, fc_s)
```
